# Optimizing a Trainium2 kernel written in Bass

```python
import jax, jax.numpy as jnp
from jax import lax
import numpy as np

D_MODEL = 2048
BATCH = 4
SEQ = 4096
DEPTH = 4

CHUNK = 64
QBLOCK = 128
N_A_LAYERS = DEPTH // 2
N_B_LAYERS = DEPTH - N_A_LAYERS
EXPAND = 2
W_A = EXPAND * D_MODEL
POOL_WINDOWS = (2, 4, 8, 16)
N_POOL_GROUPS = len(POOL_WINDOWS)
G_A = W_A // N_POOL_GROUPS
N_HEADS = 16
QK_NOPE = 128
QK_ROPE = 64
QK_HEAD = QK_NOPE + QK_ROPE
V_HEAD = 128
KV_LORA = 512
Q_LORA = 512
W_B = N_HEADS * V_HEAD
ROPE_THETA = 10000.0
EPS = 1e-6

kernel_name = "yoco_pool_mla_gated_hybrid"


def rmsnorm(x, g):
    x32 = x.astype(jnp.float32)
    r = x32 * lax.rsqrt(jnp.mean(x32 * x32, axis=-1, keepdims=True) + EPS)
    return (r * g.astype(jnp.float32)).astype(x.dtype)


def rope_tables(positions):
    inv = ROPE_THETA ** (-jnp.arange(0, QK_ROPE, 2, dtype=jnp.float32) / QK_ROPE)
    ang = positions.astype(jnp.float32)[..., None] * inv
    return jnp.cos(ang), jnp.sin(ang)


def apply_rope(t, cos, sin):
    c = cos[:, :, None, :].astype(t.dtype)
    s = sin[:, :, None, :].astype(t.dtype)
    t1, t2 = t[..., : QK_ROPE // 2], t[..., QK_ROPE // 2:]
    return jnp.concatenate([t1 * c - t2 * s, t2 * c + t1 * s], axis=-1)


def qk_norm_rope(t, g, cos, sin):
    t = rmsnorm(t, g)
    return jnp.concatenate([t[..., :QK_NOPE], apply_rope(t[..., QK_NOPE:], cos, sin)], axis=-1)


def causal_mean_pool(u, w):
    B, S, C = u.shape
    u32 = u.astype(jnp.float32)
    c0 = jnp.concatenate([jnp.zeros((B, 1, C), jnp.float32), jnp.cumsum(u32, axis=1)], axis=1)
    padded = jnp.pad(c0, ((0, 0), (w - 1, 0), (0, 0)))
    window_sum = c0[:, 1:] - padded[:, :S]
    count = jnp.minimum(jnp.arange(1, S + 1, dtype=jnp.float32), float(w))
    return (window_sum / count[None, :, None]).astype(u.dtype)


def pool_mixer_layer(x, norm_g, w_in, w_group, scale, w_out):
    B, S, _ = x.shape
    h = rmsnorm(x, norm_g)
    proj = h @ w_in
    u, gate = proj[..., :W_A], proj[..., W_A:]
    u = u.reshape(B, S, N_POOL_GROUPS, G_A)
    pooled = jnp.stack(
        [causal_mean_pool(u[:, :, gi], w) - u[:, :, gi] for gi, w in enumerate(POOL_WINDOWS)],
        axis=2)
    y = jnp.einsum('bsgc,gcd->bsgd', pooled, w_group).reshape(B, S, W_A) * scale
    y = y * jax.nn.silu(gate)
    return x + y @ w_out


def shared_kv(x, kv_norm_g, kv_w_a, kv_latent_g, kv_w_b, k_norm_g, cos, sin):
    B, S, _ = x.shape
    h = rmsnorm(x, kv_norm_g)
    ckv = h @ kv_w_a
    c, k_rope = ckv[..., :KV_LORA], ckv[..., KV_LORA:]
    c = rmsnorm(c, kv_latent_g)
    kv = (c @ kv_w_b).reshape(B, S, N_HEADS, QK_NOPE + V_HEAD)
    k_nope, v = kv[..., :QK_NOPE], kv[..., QK_NOPE:]
    k_rope = jnp.broadcast_to(k_rope[:, :, None, :], (B, S, N_HEADS, QK_ROPE))
    k = qk_norm_rope(jnp.concatenate([k_nope, k_rope], axis=-1), k_norm_g, cos, sin)
    return k, v


def chunk_causal_attention(q, k, v):
    S = q.shape[1]
    scale = QK_HEAD ** -0.5
    outs = []
    for i in range(S // QBLOCK):
        q0, k_end = i * QBLOCK, (i + 1) * QBLOCK
        s = jnp.einsum('bqhd,bkhd->bhqk', q[:, q0:k_end], k[:, :k_end],
                       preferred_element_type=jnp.float32) * scale
        q_chunk = (q0 + jnp.arange(QBLOCK)) // CHUNK
        k_chunk = jnp.arange(k_end) // CHUNK
        mask = k_chunk[None, :] <= q_chunk[:, None]
        s = jnp.where(mask[None, None], s, -1e30)
        p = jax.nn.softmax(s, axis=-1).astype(v.dtype)
        outs.append(jnp.einsum('bhqk,bkhd->bqhd', p, v[:, :k_end]))
    return jnp.concatenate(outs, axis=1)


def mla_layer(x, k, v, norm_g, w_in, q_latent_g, w_q_b, q_norm_g, w_out, cos, sin):
    B, S, _ = x.shape
    h = rmsnorm(x, norm_g)
    proj = h @ w_in
    q_lat, gate = proj[..., :Q_LORA], proj[..., Q_LORA:]
    q = (rmsnorm(q_lat, q_latent_g) @ w_q_b).reshape(B, S, N_HEADS, QK_HEAD)
    q = qk_norm_rope(q, q_norm_g, cos, sin)
    o = chunk_causal_attention(q, k, v).reshape(B, S, W_B)
    return x + (o * jax.nn.silu(gate)) @ w_out


def setup_inputs(seed: int = 0) -> dict:
    key = jax.random.key(seed)
    ks = jax.random.split(key, 24)
    f32 = jnp.float32

    def w(k, shape, fan_in):
        return jax.random.normal(k, shape, f32) * (fan_in ** -0.5)

    def gain(k, shape):
        return 1.0 + 0.02 * jax.random.normal(k, shape, f32)

    x = jax.random.normal(ks[0], (BATCH, SEQ, D_MODEL), f32)
    offsets = jax.random.randint(ks[1], (BATCH,), 0, 64, dtype=jnp.int32) * CHUNK
    positions = (offsets[:, None] + jnp.arange(SEQ, dtype=jnp.int32)[None, :]).astype(jnp.int32)
    return {
        "x": x,
        "positions": positions,
        "a_norm_g": gain(ks[2], (N_A_LAYERS, D_MODEL)),
        "a_w_in": w(ks[3], (N_A_LAYERS, D_MODEL, 2 * W_A), D_MODEL),
        "a_w_group": w(ks[4], (N_A_LAYERS, N_POOL_GROUPS, G_A, G_A), G_A),
        "a_scale": gain(ks[5], (N_A_LAYERS, W_A)),
        "a_w_out": w(ks[6], (N_A_LAYERS, W_A, D_MODEL), W_A),
        "kv_norm_g": gain(ks[7], (D_MODEL,)),
        "kv_w_a": w(ks[8], (D_MODEL, KV_LORA + QK_ROPE), D_MODEL),
        "kv_latent_g": gain(ks[9], (KV_LORA,)),
        "kv_w_b": w(ks[10], (KV_LORA, N_HEADS * (QK_NOPE + V_HEAD)), KV_LORA),
        "k_norm_g": gain(ks[11], (QK_HEAD,)),
        "b_norm_g": gain(ks[12], (N_B_LAYERS, D_MODEL)),
        "b_w_in": w(ks[13], (N_B_LAYERS, D_MODEL, Q_LORA + W_B), D_MODEL),
        "b_q_latent_g": gain(ks[14], (N_B_LAYERS, Q_LORA)),
        "b_w_q_b": w(ks[15], (N_B_LAYERS, Q_LORA, N_HEADS * QK_HEAD), Q_LORA),
        "b_q_norm_g": gain(ks[16], (N_B_LAYERS, QK_HEAD)),
        "b_w_out": w(ks[17], (N_B_LAYERS, W_B, D_MODEL), W_B),
    }


def reference(x, positions, a_norm_g, a_w_in, a_w_group, a_scale, a_w_out,
              kv_norm_g, kv_w_a, kv_latent_g, kv_w_b, k_norm_g,
              b_norm_g, b_w_in, b_q_latent_g, b_w_q_b, b_q_norm_g, b_w_out):
    cos, sin = rope_tables(positions)
    k = v = None
    for layer in range(DEPTH):
        if layer < N_A_LAYERS:
            x = pool_mixer_layer(x, a_norm_g[layer], a_w_in[layer], a_w_group[layer],
                                 a_scale[layer], a_w_out[layer])
        else:
            if layer == N_A_LAYERS:
                k, v = shared_kv(x, kv_norm_g, kv_w_a, kv_latent_g, kv_w_b, k_norm_g, cos, sin)
            j = layer - N_A_LAYERS
            x = mla_layer(x, k, v, b_norm_g[j], b_w_in[j], b_q_latent_g[j], b_w_q_b[j],
                          b_q_norm_g[j], b_w_out[j], cos, sin)
    return x
```

```python
import numpy as np
import ml_dtypes
from contextlib import ExitStack
import concourse.bass as bass
import concourse.mybir as mybir
from concourse.bass_utils import run_bass_kernel_spmd

F32 = mybir.dt.float32
BF16 = mybir.dt.bfloat16
I32 = mybir.dt.int32
AF = mybir.ActivationFunctionType
ALU = mybir.AluOpType

D = 2048
DC = 16
T = 512
HL = 32
TH = T + HL
PADW = 16 + TH
NSLOT = 4
SBS = ([0, 3, 4, 7], [1, 2, 5, 6])
NOFF = [4, 12, 20, 28]
NOFF_OFS = [0, 4, 16, 36]
EPS = 1e-6
NH = 16
WIN = (2, 4, 8, 16)
SM_SCALE = 192 ** -0.5
NEG = -30000.0
WSLOT = 8192
NWSLOT = 3
DBG_SLOTS = [0, 1, 2, 3]
DBG_LAYERS = 2
DBG = False

V_ANORM = 0
V_ASCALE = 32
V_KVNORM = 96
V_KVLAT = 112
V_KGN = 116
V_KGR = 117
V_KGRS = 118
V_BNORM = 119
V_BQLAT = 151
V_QGN = 159
V_QGR = 161
V_QGRS = 163
V_INVF = 165
V_SGN = 166
V_CINV = 168
V_ABIAS = 424
NV = 512


class Ev:
    __slots__ = ("sem", "key", "val")

    def __init__(self, sem, key, val):
        self.sem, self.key, self.val = sem, key, val


class Buf:
    __slots__ = ("name", "writers", "readers")

    def __init__(self, name):
        self.name = name
        self.writers = []
        self.readers = {}


class Eng:
    def __init__(self, sched, name, handle, sem):
        self.S = sched
        self.name = name
        self.h = handle
        self.sem = sem
        self.cnt = 0
        self.known = {}
        self.thunks = []

    def _wait(self, ev):
        if ev is None:
            return
        if self.name == "pe" and ev.key == "pe":
            return
        if self.known.get(ev.key, 0) >= ev.val:
            return
        self.known[ev.key] = ev.val
        h, sem, val = self.h, ev.sem, ev.val
        self.thunks.append(lambda: h.wait_ge(sem, val))

    def deps(self, reads, writes):
        for b in reads:
            for e in b.writers:
                self._wait(e)
        for b in writes:
            for e in b.writers:
                self._wait(e)
            for e in b.readers.values():
                self._wait(e)

    def mark(self, ev, reads, writes):
        for b in reads:
            old = b.readers.get(ev.key)
            if old is None or old.val < ev.val:
                b.readers[ev.key] = ev
        for b in writes:
            b.writers = [ev]
            b.readers = {}

    def op(self, fn, reads=(), writes=()):
        self.deps(reads, writes)
        self.cnt += 1
        sem, val = self.sem, self.cnt
        self.thunks.append(lambda: fn(self.h).then_inc(sem, 1))
        ev = Ev(sem, self.name, val)
        self.mark(ev, reads, writes)
        return ev

    def group(self, fns, reads=(), writes=()):
        self.deps(reads, writes)
        self.cnt += 1
        sem, val = self.sem, self.cnt
        h = self.h
        for f in fns[:-1]:
            self.thunks.append(lambda f=f: f(h))
        last = fns[-1]
        self.thunks.append(lambda: last(h).then_inc(sem, 1))
        ev = Ev(sem, self.name, val)
        self.mark(ev, reads, writes)
        return ev

    def dma(self, semname, out, in_, reads=(), writes=()):
        self.deps(reads, writes)
        sem, val = self.S.dma_sem(semname)
        h = self.h
        self.thunks.append(lambda: h.dma_start(out=out, in_=in_).then_inc(sem, 16))
        ev = Ev(sem, "dma:" + semname, val)
        self.mark(ev, reads, writes)
        return ev


class Sched:
    def __init__(self, nc, es):
        self.nc = nc
        self.es = es
        self.dsems = {}
        mk = lambda n, h: Eng(self, n, h, es.enter_context(nc.semaphore("s_" + n)))
        self.pe = mk("pe", nc.tensor)
        self.act = mk("act", nc.scalar)
        self.dve = mk("dve", nc.vector)
        self.pool = mk("pool", nc.gpsimd)
        self.sp = mk("sp", nc.sync)
        self.all_ev = []

    def dma_sem(self, name):
        if name not in self.dsems:
            self.dsems[name] = [self.es.enter_context(self.nc.semaphore("d_" + name)), 0]
        ent = self.dsems[name]
        ent[1] += 16
        return ent[0], ent[1]

    def barrier(self):
        engs = (self.pe, self.act, self.dve, self.pool, self.sp)
        for x in engs:
            for y in engs:
                if y is not x and y.cnt:
                    x._wait(Ev(y.sem, y.name, y.cnt))
            for name, (sem, val) in self.dsems.items():
                if val:
                    x._wait(Ev(sem, "dma:" + name, val))

    def emit(self):
        nc = self.nc
        lists = {}
        for e in (self.pe, self.act, self.dve, self.pool, self.sp):
            lists[e.name] = e.thunks
            e.thunks = []
        with nc.Block() as block:
            @block.tensor
            def _(e):
                for t in lists["pe"]:
                    t()

            @block.scalar
            def _(e):
                for t in lists["act"]:
                    t()

            @block.vector
            def _(e):
                for t in lists["dve"]:
                    t()

            @block.gpsimd
            def _(e):
                for t in lists["pool"]:
                    t()

            @block.sync
            def _(e):
                for t in lists["sp"]:
                    t()


class Prog:
    def __init__(self, mode):
        self.mode = mode
        self.nc = bass.Bass("TRN2", target_bir_lowering=False)
        self.es = ExitStack()
        self.bufs = {}

    def dram_in(self, name, shape, dt):
        return self.nc.dram_tensor(name, list(shape), dt, kind="ExternalInput").ap()

    def dram_out(self, name, shape, dt):
        return self.nc.dram_tensor(name, list(shape), dt, kind="ExternalOutput").ap()

    def dram_int(self, name, shape, dt):
        return self.nc.dram_tensor(name, list(shape), dt, kind="Internal").ap()

    def sb(self, name, shape, dt):
        self._sbn = getattr(self, "_sbn", 0) + 1
        return self.es.enter_context(self.nc.sbuf_tensor(f"sb{self._sbn}_{name}", list(shape), dt))

    def B(self, name):
        if name not in self.bufs:
            self.bufs[name] = Buf(name)
        return self.bufs[name]

    def w_init(self):
        self.wplan = []
        self.wissued = 0
        self.wused = 0
        self.wbase = 0
        self.wend = 0
        self.nw = 0
        self.wheld = set()

    def w_alloc(self, nw, end):
        assert self.wissued == self.wused
        self.nw = nw
        self.wbase = self.wused
        self.wend = end
        self.wslots = [self.sb(f"wslot{i}", [128, WSLOT], BF16) for i in range(nw)]

    def w_issue_upto(self, n):
        S = self.S
        while self.wissued < min(n, self.wend):
            i = self.wissued
            if (i - self.nw) in self.wheld:
                break
            k = (i - self.wbase) % self.nw
            src, K, ncols = self.wplan[i]
            dst = self.wslots[k][:, 0:K * ncols].rearrange("p (k n) -> p k n", n=ncols)
            S.pool.dma(f"w{k}", dst, src, reads=(), writes=(self.B(f"wslot{k}"),))
            self.wissued += 1

    def w_get(self, src, K, ncols, hold=False):
        i = self.wused
        if hold:
            self.wheld.add(i)
        assert i < self.wend, "weight plan exhausted"
        psrc, pK, pn = self.wplan[i]
        assert (pK, pn) == (K, ncols) and str(psrc) == str(src), f"weight plan mismatch at {i}"
        self.w_issue_upto(i + self.nw)
        self.wused += 1
        k = (i - self.wbase) % self.nw
        return self.wslots[k][:, 0:K * ncols].rearrange("p (k n) -> p k n", n=ncols), self.B(f"wslot{k}")

    @staticmethod
    def wblk(W2d, r0, K, c0, ncols):
        return W2d[r0:r0 + K * 128, c0:c0 + ncols].rearrange("(k p) n -> p k n", p=128)

    def mm_group(self, out, pairs, reads, writes):
        n = len(pairs)
        fns = []
        for i, (l, r) in enumerate(pairs):
            fns.append(lambda h, l=l, r=r, i=i: h.matmul(out, l, r, start=(i == 0), stop=(i == n - 1)))
        return self.S.pe.group(fns, reads=reads, writes=writes)

    def rms_stats(self, src_fn, nchunk, ncols_total, col_ranges, g_cols, dst_fn, src_bufs, dst_bufs, inv_n,
                  tag):
        S = self.S
        sq = self.sqb
        sq_views = []
        for c in range(nchunk):
            sqt = sq[c % len(sq)]
            sqB = self.B(f"sqb{c % len(sq)}")
            src = src_fn(c)
            dst = sqt[:, 0:ncols_total]
            S.act.op(lambda h, src=src, dst=dst: h.activation(out=dst, in_=src, func=AF.Square),
                     reads=(src_bufs[c],), writes=(sqB,))
            for (c0, c1, ps, psB) in col_ranges:
                l = self.ones[:, :]
                r = sqt[:, c0:c1]
                S.pe.group([lambda h, ps=ps, l=l, r=r, c=c: h.matmul(ps, l, r, start=(c == 0),
                                                                  stop=(c == nchunk - 1))],
                           reads=(sqB,), writes=(psB,))
        rstdB = self.B("rstd")
        for (c0, c1, ps, psB) in col_ranges:
            sd = self.sd[:, c0:c1]
            S.act.op(lambda h, ps=ps, sd=sd: h.activation(out=sd, in_=ps, func=AF.Ln, scale=inv_n,
                                                        bias=self.epsc[:, 0:1]),
                     reads=(psB,), writes=(self.B("sd"),))
            rs = self.rstd[:, c0:c1]
            S.act.op(lambda h, sd=sd, rs=rs: h.activation(out=rs, in_=sd, func=AF.Exp, scale=-0.5),
                     reads=(self.B("sd"),), writes=(rstdB,))
        for c in range(nchunk):
            src = src_fn(c)
            dst = dst_fn(c)
            g = self.vecs[:, g_cols + c:g_cols + c + 1]
            rs = self.rstd[:, 0:ncols_total]
            S.dve.op(lambda h, src=src, dst=dst, g=g, rs=rs: h.scalar_tensor_tensor(
                out=dst, in0=src, scalar=g, in1=rs, op0=ALU.mult, op1=ALU.mult),
                reads=(src_bufs[c], rstdB), writes=(dst_bufs[c],))

    def build(self):
        nc, es = self.nc, self.es
        mode = self.mode
        doA = mode in ("A", "F")
        doB = mode in ("B", "F")
        with es:
            self.S = S = Sched(nc, es)
            self.es_outer = es
            vecs_d = self.dram_in("vecs", [128, NV], F32)
            pos_d = self.dram_in("pos", [64, NSLOT * T], I32)
            ident_d = self.dram_in("ident", [128, 128], F32)
            if doA:
                xin = self.dram_in("xin", [NSLOT, TH, D], F32)
                a_w_in = self.dram_in("a_w_in", [2 * 2048, 8192], F32)
                a_w_g = self.dram_in("a_w_g", [8 * 1024, 1024], F32)
                a_w_out = self.dram_in("a_w_out", [2 * 4096, 2048], F32)
                kv_w_a = self.dram_in("kv_w_a", [2048, 640], F32)
                kv_w_b = self.dram_in("kv_w_b", [512, 4096], F32)
            if doB:
                b_w_in = self.dram_in("b_w_in", [2 * 2048, 2560], F32)
                b_w_qb = self.dram_in("b_w_qb", [2 * 512, 4096], F32)
                b_w_out = self.dram_in("b_w_out", [2 * 2048, 2048], F32)
                out_d = self.dram_out("out", [NSLOT * T, D], F32)
                if DBG:
                    self.dbg_bf = self.dram_out("dbg_bf", [128, 6, T], BF16)
                    self.dbg_f = self.dram_out("dbg_f", [128, 4, T], F32)
            KS, VS = [NH // 2 * 192, T], [NH // 2 * 128, T]
            KS2, VS2 = [NH * 192, T], [NH * 128, T]
            sh = [(s_, hh) for s_ in range(NSLOT) for hh in range(2)]
            if mode == "A":
                x2T = self.dram_out("x2T", [128, DC, NSLOT * T], F32)
                kT_loc = {k: self.dram_out(f"kT_loc_{k[0]}_{k[1]}", KS, BF16) for k in sh}
                v_loc = {k: self.dram_out(f"v_loc_{k[0]}_{k[1]}", VS, BF16) for k in sh}
            elif mode == "B":
                x2T = self.dram_in("x2T", [128, DC, NSLOT * T], F32)
                kT_loc = {k: self.dram_in(f"kT_loc_{k[0]}_{k[1]}", KS, BF16) for k in sh}
                v_loc = {k: self.dram_in(f"v_loc_{k[0]}_{k[1]}", VS, BF16) for k in sh}
                kT_all = {k: self.dram_in(f"kT_all_{k[0]}_{k[1]}", KS2, BF16) for k in sh}
                v_all = {k: self.dram_in(f"v_all_{k[0]}_{k[1]}", VS2, BF16) for k in sh}
            else:
                x2T = self.dram_int("x2T", [128, DC, NSLOT * T], F32)
                kT_loc = {k: self.dram_int(f"kT_loc_{k[0]}_{k[1]}", KS, BF16) for k in sh}
                v_loc = {k: self.dram_int(f"v_loc_{k[0]}_{k[1]}", VS, BF16) for k in sh}
                kT_all = {k: self.dram_int(f"kT_all_{k[0]}_{k[1]}", KS2, BF16) for k in sh}
                v_all = {k: self.dram_int(f"v_all_{k[0]}_{k[1]}", VS2, BF16) for k in sh}
            self.kT_loc, self.v_loc = kT_loc, v_loc
            if doB:
                self.kT_all, self.v_all = kT_all, v_all

            self.vecs = self.sb("vecs", [128, NV], F32)
            self.ident = self.sb("ident", [128, 128], F32)
            self.ones = self.sb("ones", [128, 128], BF16)
            self.epsc = self.sb("epsc", [128, 1], F32)
            self.C64 = self.sb("C64", [64, NSLOT * T], F32)
            self.S64 = self.sb("S64", [64, NSLOT * T], F32)
            self.sqb = [self.sb(f"sqb{i}", [128, TH], BF16) for i in range(4)]
            self.sd = self.sb("sd", [128, TH], F32)
            self.rstd = self.sb("rstd", [128, TH], F32)
            self.xT = self.sb("xT", [128, DC, TH], F32)
            self.h = self.sb("h", [128, DC, TH], BF16)
            self.ps = [es.enter_context(nc.psum_tensor(f"ps{i}", [128, 512], F32)) for i in range(8)]
            self.psB = [self.B(f"ps{i}") for i in range(8)]
            self.w_init()

            S.sp.dma("c0", self.vecs[:, :], vecs_d, writes=(self.B("vecs"),))
            S.sp.dma("c0", self.ident[:, :], ident_d, writes=(self.B("ident"),))
            S.dve.op(lambda h: h.memset(self.ones[:, :], 1.0), writes=(self.B("ones"),))
            S.dve.op(lambda h: h.memset(self.epsc[:, :], EPS), writes=(self.B("epsc"),))
            self.rope_tables(pos_d)

            if doA:
                for s in range(NSLOT):
                    for l in range(2):
                        self.plan_A_layer(a_w_in, a_w_g, a_w_out, l)
                    self.plan_KV(kv_w_a, kv_w_b)
            self.wplanA_end = len(self.wplan)
            if doB:
                for s in DBG_SLOTS:
                    for j in range(DBG_LAYERS):
                        self.plan_B_layer(b_w_in, b_w_qb, b_w_out, j)

            if doA:
                with ExitStack() as esA:
                    es_saved, self.es = self.es, esA
                    self.alloc_A()
                    for s in range(NSLOT):
                        self.load_x_tile(xin, s)
                        for l in range(2):
                            self.A_layer(a_w_in, a_w_g, a_w_out, l, s)
                        self.KV_tile(kv_w_a, kv_w_b, s)
                        self.store_x2(x2T, s)
                        if mode == "F":
                            self.exchange(s)
                    if mode == "F":
                        S.barrier()
                        S.emit()
                    self.es = es_saved
                    if mode == "A":
                        self.finish()
            if doB:
                with ExitStack() as esB:
                    es_saved, self.es = self.es, esB
                    self.alloc_B()
                    for s in DBG_SLOTS:
                        self.load_x2(x2T, s)
                        for j in range(DBG_LAYERS):
                            self.B_layer(b_w_in, b_w_qb, b_w_out, j, s)
                        self.store_out(out_d, s)
                    self.es = es_saved
                    self.finish()
        return nc

    def finish(self):
        S = self.S
        assert self.wused == len(self.wplan), (self.wused, len(self.wplan))
        S.barrier()
        S.emit()

    def rope_tables(self, pos_d):
        S = self.S
        NT = NSLOT * T
        with ExitStack() as es2:
            posi = es2.enter_context(self.nc.sbuf_tensor("posi", [64, NT], I32))
            ang = es2.enter_context(self.nc.sbuf_tensor("ang", [64, NT], F32))
            t1 = es2.enter_context(self.nc.sbuf_tensor("rt1", [64, NT], F32))
            ti = es2.enter_context(self.nc.sbuf_tensor("rti", [64, NT], I32))
            Bp, Ba, Bt, Bi = self.B("posi"), self.B("ang"), self.B("rt1"), self.B("rti")
            Bv = self.B("vecs")
            S.sp.dma("c1", posi[:, :], pos_d, writes=(Bp,))
            S.dve.op(lambda h: h.tensor_copy(out=ang[:, :], in_=posi[:, :]), reads=(Bp,), writes=(Ba,))
            invf = self.vecs[0:64, V_INVF:V_INVF + 1]
            S.dve.op(lambda h: h.tensor_scalar(out=ang[:, :], in0=ang[:, :], scalar1=invf, scalar2=None,
                                               op0=ALU.mult), reads=(Ba, Bv), writes=(Ba,))
            C1 = 6.28125
            C2 = 2.0 * np.pi - C1
            for which, dst, Bd in (("sin", self.S64, self.B("S64")), ("cos", self.C64, self.B("C64"))):
                src = ang
                if which == "cos":
                    S.dve.op(lambda h: h.tensor_scalar(out=t1[:, :], in0=ang[:, :], scalar1=float(np.pi / 2),
                                                       scalar2=None, op0=ALU.add), reads=(Ba,), writes=(Bt,))
                    src = t1
                    Bs = Bt
                else:
                    Bs = Ba
                S.dve.op(lambda h, src=src, dst=dst: h.tensor_scalar(out=dst[:, :], in0=src[:, :],
                                                            scalar1=float(1.0 / (2 * np.pi)), scalar2=None,
                                                            op0=ALU.mult), reads=(Bs,), writes=(Bd,))
                S.dve.op(lambda h, dst=dst: h.tensor_copy(out=ti[:, :], in_=dst[:, :]), reads=(Bd,), writes=(Bi,))
                S.dve.op(lambda h, dst=dst: h.tensor_copy(out=dst[:, :], in_=ti[:, :]), reads=(Bi,), writes=(Bd,))
                S.dve.op(lambda h, src=src, dst=dst: h.scalar_tensor_tensor(out=t1[:, :], in0=dst[:, :], scalar=-C1,
                                                                   in1=src[:, :], op0=ALU.mult, op1=ALU.add),
                         reads=(Bd, Bs), writes=(Bt,))
                S.dve.op(lambda h, dst=dst: h.scalar_tensor_tensor(out=t1[:, :], in0=dst[:, :], scalar=-C2,
                                                          in1=t1[:, :], op0=ALU.mult, op1=ALU.add),
                         reads=(Bd, Bt), writes=(Bt,))
                S.dve.op(lambda h: h.tensor_scalar(out=t1[:, :], in0=t1[:, :], scalar1=3.14159, scalar2=-3.14159,
                                                   op0=ALU.min, op1=ALU.max), reads=(Bt,), writes=(Bt,))
                if which == "sin":
                    sgn = self.vecs[0:64, V_SGN:V_SGN + 1]
                    S.act.op(lambda h, dst=dst: h.activation(out=dst[:, :], in_=t1[:, :], func=AF.Sin, scale=sgn),
                             reads=(Bt, Bv), writes=(Bd,))
                else:
                    S.act.op(lambda h, dst=dst: h.activation(out=dst[:, :], in_=t1[:, :], func=AF.Sin),
                             reads=(Bt,), writes=(Bd,))
            S.barrier()
            S.emit()

    def alloc_A(self):
        self.w_alloc(3, self.wplanA_end)
        self.xtok = [self.sb(f"xtok{i}", [128, D], F32) for i in range(2)]
        self.U = self.sb("U", [128, 8, PADW], F32)
        self.pa = self.sb("pa", [128, 2, PADW], F32)
        self.pb = self.sb("pb", [128, 2, PADW], F32)
        self.sg = self.sb("sg", [128, 8, TH], BF16)
        self.PL = self.sb("PL", [128, 8, TH], BF16)
        self.y = self.sb("y", [128, 8, TH], BF16)
        self.cf = self.U[:, 0:4, 0:T]
        self.krw = self.U[0:64, 4:8, 0:T]
        self.cn = self.PL[:, 0:4, 0:T]
        self.sqr = self.PL[0:64, 4, 0:T]
        self.kst = [self.sg[:, i, 0:T] for i in range(2)]
        self.krst = [self.sg[0:64, 2 + i, 0:T] for i in range(2)]
        self.vst = [self.y[:, 4 * i:4 * i + 4, 0:T] for i in range(2)]
        self.fsc = self.sb("fsc", [128, 4], F32)
        S = self.S
        S.dve.op(lambda h: h.memset(self.U[:, :, 0:16], 0.0), reads=(), writes=(self.B("U"),))

    def plan_A_layer(self, w_in, w_g, w_out, l):
        for g in range(4):
            for half in range(2):
                self.wplan.append((self.wblk(w_in, l * 2048, 16, g * 1024 + half * 512, 512), 16, 512))
            for half in range(2):
                self.wplan.append((self.wblk(w_in, l * 2048, 16, 4096 + g * 1024 + half * 512, 512), 16, 512))
            for half in range(2):
                self.wplan.append((self.wblk(w_g, (l * 4 + g) * 1024, 8, half * 512, 512), 8, 512))
            for q in range(4):
                self.wplan.append((self.wblk(w_out, l * 4096 + g * 1024, 8, q * 512, 512), 8, 512))

    def load_x_tile(self, xin, s):
        S = self.S
        Bid = self.B("ident")
        subs = [(0, HL, 0)] + [(HL + i * 128, 128, HL + i * 128) for i in range(4)]
        pre = getattr(self, "x_prefetched", -1)
        for si, (r0, nr, c0) in enumerate(subs):
            xt = self.xtok[si % 2]
            Bx = self.B(f"xtok{si % 2}")
            if not (pre == s and si < 2):
                S.sp.dma(f"xtok{si % 2}", xt[0:nr, :], xin[s, r0:r0 + nr, :], writes=(Bx,))
            for cg in range(4):
                pi = 6 + (cg % 2)
                ps, psB = self.ps[pi], self.psB[pi]
                fns = []
                for j in range(4):
                    c = cg * 4 + j
                    fns.append(lambda h, ps=ps, xt=xt, c=c, j=j, nr=nr: h.transpose(
                        out=ps[:, j * 128:j * 128 + nr], in_=xt[0:nr, c * 128:(c + 1) * 128],
                        identity=self.ident[0:nr, 0:nr]))
                S.pe.group(fns, reads=(Bx, Bid), writes=(psB,))
                src = ps[:, :].rearrange("p (j n) -> p j n", n=128)[:, :, 0:nr]
                dst = self.xT[:, cg * 4:cg * 4 + 4, c0:c0 + nr]
                wb = tuple(self.B(f"xT{c}") for c in range(cg * 4, cg * 4 + 4))
                eng = S.act if cg % 2 == 0 else S.dve
                if eng is S.act:
                    eng.op(lambda h, src=src, dst=dst: h.activation(out=dst, in_=src, func=AF.Copy),
                           reads=(psB,), writes=wb)
                else:
                    eng.op(lambda h, src=src, dst=dst: h.tensor_copy(out=dst, in_=src), reads=(psB,), writes=wb)
        if s + 1 < NSLOT:
            for si, (r0, nr, c0) in enumerate(subs[:2]):
                S.sp.dma(f"xtok{si % 2}", self.xtok[si % 2][0:nr, :], xin[s + 1, r0:r0 + nr, :],
                         writes=(self.B(f"xtok{si % 2}"),))
            self.x_prefetched = s + 1

    def A_layer(self, w_in, w_g, w_out, l, s):
        S = self.S
        halo_full = (l == 0)
        xB = [self.B(f"xT{c}") for c in range(DC)]
        hB = [self.B(f"h{c}") for c in range(DC)]
        Bv = self.B("vecs")
        self.rms_stats(lambda c: self.xT[:, c, :], DC, TH,
                       [(HL, TH, self.ps[5][:, :], self.psB[5]), (0, HL, self.ps[4][:, 0:HL], self.psB[4])],
                       V_ANORM + l * 16, lambda c: self.h[:, c, :], xB, hB, 1.0 / D, "a")
        mi = [0]

        def next_main():
            i = mi[0] % 4
            mi[0] += 1
            return self.ps[i], self.psB[i]

        hi = [0]

        def next_halo():
            i = hi[0] % 16
            hi[0] += 1
            return self.ps[4][:, i * HL:(i + 1) * HL], self.psB[4]

        for g in range(4):
            w = WIN[g]
            UB = self.B("U")
            sgB = self.B("sg")
            PLB = self.B("PL")
            yB = self.B("y")
            for half in range(2):
                blk, bB = self.w_get(self.wblk(w_in, l * 2048, 16, g * 1024 + half * 512, 512), 16, 512)
                for m in range(4):
                    j = half * 4 + m
                    ps, psB = next_main()
                    self.mm_group(ps[:, :], [(blk[:, k, m * 128:(m + 1) * 128], self.h[:, k, HL:TH])
                                             for k in range(DC)], reads=tuple(hB) + (bB,), writes=(psB,))
                    S.act.op(lambda h, ps=ps, j=j: h.activation(out=self.U[:, j, 16 + HL:PADW], in_=ps[:, :],
                                                               func=AF.Copy), reads=(psB,), writes=(UB,))
                    ph, phB = next_halo()
                    self.mm_group(ph, [(blk[:, k, m * 128:(m + 1) * 128], self.h[:, k, 0:HL])
                                       for k in range(DC)], reads=tuple(hB) + (bB,), writes=(phB,))
                    S.act.op(lambda h, ph=ph, j=j: h.activation(out=self.U[:, j, 16:16 + HL], in_=ph,
                                                               func=AF.Copy), reads=(phB,), writes=(UB,))
            paB, pbB = self.B("pa"), self.B("pb")
            for jj in range(4):
                Uv = self.U[:, 2 * jj:2 * jj + 2, :]
                A_, B_ = self.pa, self.pb
                S.dve.op(lambda h, Uv=Uv: h.tensor_tensor(out=A_[:, :, 1:PADW], in0=Uv[:, :, 1:PADW],
                                                          in1=Uv[:, :, 0:PADW - 1], op=ALU.add),
                         reads=(UB,), writes=(paB,))
                cur, curB, oth, othB = A_, paB, B_, pbB
                sh = 2
                lo = 1
                while sh < w:
                    lo2 = lo + sh
                    S.dve.op(lambda h, cur=cur, oth=oth, lo2=lo2, sh=sh: h.tensor_tensor(
                        out=oth[:, :, lo2:PADW], in0=cur[:, :, lo2:PADW], in1=cur[:, :, lo2 - sh:PADW - sh],
                        op=ALU.add), reads=(curB,), writes=(othB,))
                    cur, curB, oth, othB = oth, othB, cur, curB
                    lo = lo2
                    sh *= 2
                S.dve.op(lambda h, cur=cur, Uv=Uv, jj=jj, w=w: h.scalar_tensor_tensor(
                    out=self.PL[:, 2 * jj:2 * jj + 2, :], in0=cur[:, :, 16:PADW], scalar=1.0 / w,
                    in1=Uv[:, :, 16:PADW], op0=ALU.mult, op1=ALU.subtract), reads=(curB, UB), writes=(PLB,))
                for q in range(2):
                    ci = V_CINV + (s * 4 + g) * 16
                    cinv = self.vecs[:, ci:ci + 16]
                    S.dve.op(lambda h, cur=cur, q=q, cinv=cinv, oth=oth: h.tensor_tensor(
                        out=oth[:, q, 0:16], in0=cur[:, q, 16 + HL:16 + HL + 16], in1=cinv, op=ALU.mult),
                        reads=(curB, Bv), writes=(othB,))
                    S.dve.op(lambda h, q=q, jj=jj, Uv=Uv, oth=oth: h.tensor_tensor(
                        out=self.PL[:, 2 * jj + q, HL:HL + 16], in0=oth[:, q, 0:16],
                        in1=Uv[:, q, 16 + HL:16 + HL + 16], op=ALU.subtract), reads=(othB, UB), writes=(PLB,))
            for half in range(2):
                blk, bB = self.w_get(self.wblk(w_in, l * 2048, 16, 4096 + g * 1024 + half * 512, 512), 16, 512)
                for m in range(4):
                    j = half * 4 + m
                    ps, psB = next_main()
                    self.mm_group(ps[:, :], [(blk[:, k, m * 128:(m + 1) * 128], self.h[:, k, HL:TH])
                                             for k in range(DC)], reads=tuple(hB) + (bB,), writes=(psB,))
                    S.act.op(lambda h, ps=ps, j=j: h.activation(out=self.sg[:, j, HL:TH], in_=ps[:, :],
                                                               func=AF.Silu), reads=(psB,), writes=(sgB,))
                    if halo_full:
                        ph, phB = next_halo()
                        self.mm_group(ph, [(blk[:, k, m * 128:(m + 1) * 128], self.h[:, k, 0:HL])
                                           for k in range(DC)], reads=tuple(hB) + (bB,), writes=(phB,))
                        S.act.op(lambda h, ph=ph, j=j: h.activation(out=self.sg[:, j, 0:HL], in_=ph,
                                                                   func=AF.Silu), reads=(phB,), writes=(sgB,))
            for half in range(2):
                blk, bB = self.w_get(self.wblk(w_g, (l * 4 + g) * 1024, 8, half * 512, 512), 8, 512)
                for m in range(4):
                    j = half * 4 + m
                    sc = self.vecs[:, V_ASCALE + l * 32 + g * 8 + j:V_ASCALE + l * 32 + g * 8 + j + 1]
                    ps, psB = next_main()
                    self.mm_group(ps[:, :], [(blk[:, k, m * 128:(m + 1) * 128], self.PL[:, k, HL:TH])
                                             for k in range(8)], reads=(PLB, bB), writes=(psB,))
                    S.dve.op(lambda h, ps=ps, j=j, sc=sc: h.scalar_tensor_tensor(
                        out=self.y[:, j, HL:TH], in0=ps[:, :], scalar=sc, in1=self.sg[:, j, HL:TH],
                        op0=ALU.mult, op1=ALU.mult), reads=(psB, sgB, Bv), writes=(yB,))
                    if halo_full:
                        ph, phB = next_halo()
                        self.mm_group(ph, [(blk[:, k, m * 128:(m + 1) * 128], self.PL[:, k, 0:HL])
                                           for k in range(8)], reads=(PLB, bB), writes=(phB,))
                        S.dve.op(lambda h, ph=ph, j=j, sc=sc: h.scalar_tensor_tensor(
                            out=self.y[:, j, 0:HL], in0=ph, scalar=sc, in1=self.sg[:, j, 0:HL],
                            op0=ALU.mult, op1=ALU.mult), reads=(phB, sgB, Bv), writes=(yB,))
            for q in range(4):
                blk, bB = self.w_get(self.wblk(w_out, l * 4096 + g * 1024, 8, q * 512, 512), 8, 512)
                for m in range(4):
                    oc = q * 4 + m
                    ps, psB = next_main()
                    self.mm_group(ps[:, :], [(blk[:, k, m * 128:(m + 1) * 128], self.y[:, k, HL:TH])
                                             for k in range(8)], reads=(yB, bB), writes=(psB,))
                    S.dve.op(lambda h, ps=ps, oc=oc: h.tensor_tensor(
                        out=self.xT[:, oc, HL:TH], in0=self.xT[:, oc, HL:TH], in1=ps[:, :], op=ALU.add),
                        reads=(psB,), writes=(xB[oc],))
                    if halo_full:
                        ph, phB = next_halo()
                        self.mm_group(ph, [(blk[:, k, m * 128:(m + 1) * 128], self.y[:, k, 0:HL])
                                           for k in range(8)], reads=(yB, bB), writes=(phB,))
                        S.dve.op(lambda h, ph=ph, oc=oc: h.tensor_tensor(
                            out=self.xT[:, oc, 0:HL], in0=self.xT[:, oc, 0:HL], in1=ph, op=ALU.add),
                            reads=(phB,), writes=(xB[oc],))

    def plan_KV(self, kv_w_a, kv_w_b):
        self.wplan.append((self.wblk(kv_w_a, 0, 16, 0, 512), 16, 512))
        self.wplan.append((self.wblk(kv_w_a, 0, 16, 512, 128), 16, 128))
        self.wplan.append((self.wblk(kv_w_b, 0, 4, 0, 2048), 4, 2048))
        self.wplan.append((self.wblk(kv_w_b, 0, 4, 2048, 2048), 4, 2048))

    def KV_tile(self, kv_w_a, kv_w_b, s):
        S = self.S
        xB = [self.B(f"xT{c}") for c in range(DC)]
        hB = [self.B(f"h{c}") for c in range(DC)]
        Bv = self.B("vecs")
        self.rms_stats(lambda c: self.xT[:, c, :], DC, TH,
                       [(HL, TH, self.ps[5][:, :], self.psB[5]), (0, HL, self.ps[4][:, 0:HL], self.psB[4])],
                       V_KVNORM, lambda c: self.h[:, c, :], xB, hB, 1.0 / D, "kv")
        cfB = [self.B(f"cf{j}") for j in range(4)]
        cnB = [self.B(f"cn{j}") for j in range(4)]
        kv_alias = tuple(cfB) + tuple(cnB) + tuple(self.B(n) for n in (
            "krw", "sqr", "kst0", "kst1", "krst0", "krst1", "vst0", "vst1"))
        pool_bufs = tuple(self.B(n) for n in ("U", "PL", "sg", "y"))
        S.dve.op(lambda h: h.memset(self.fsc[:, 0:1], 0.0), writes=kv_alias + pool_bufs + (self.B("fsc"),))
        blk, bB = self.w_get(self.wblk(kv_w_a, 0, 16, 0, 512), 16, 512)
        for m in range(4):
            ps, psB = self.ps[m % 4], self.psB[m % 4]
            self.mm_group(ps[:, :], [(blk[:, k, m * 128:(m + 1) * 128], self.h[:, k, HL:TH]) for k in range(DC)],
                          reads=tuple(hB) + (bB,), writes=(psB,))
            S.act.op(lambda h, ps=ps, m=m: h.activation(out=self.cf[:, m, :], in_=ps[:, :], func=AF.Copy),
                     reads=(psB,), writes=(cfB[m],))
        blk, bB = self.w_get(self.wblk(kv_w_a, 0, 16, 512, 128), 16, 128)
        krP, krB = self.ps[0], self.psB[0]
        krsP, krsB = self.ps[1], self.psB[1]
        self.mm_group(krP[0:64, :], [(blk[:, k, 0:64], self.h[:, k, HL:TH]) for k in range(DC)],
                      reads=tuple(hB) + (bB,), writes=(krB,))
        self.mm_group(krsP[0:64, :], [(blk[:, k, 64:128], self.h[:, k, HL:TH]) for k in range(DC)],
                      reads=tuple(hB) + (bB,), writes=(krsB,))
        self.rms_stats(lambda c: self.cf[:, c, :], 4, T, [(0, T, self.ps[5][:, :], self.psB[5])],
                       V_KVLAT, lambda c: self.cn[:, c, :], cfB, cnB, 1.0 / 512, "c")
        tsl = slice(s * T, (s + 1) * T)
        GC, GS, KR, TMP = (self.krw[:, i, :] for i in range(4))
        krwB = self.B("krw")
        BC, BS = self.B("C64"), self.B("S64")
        S.dve.op(lambda h: h.tensor_scalar(out=GC, in0=self.C64[:, tsl], scalar1=self.vecs[0:64, V_KGR:V_KGR + 1],
                                           scalar2=None, op0=ALU.mult), reads=(BC, Bv), writes=(krwB,))
        S.dve.op(lambda h: h.tensor_scalar(out=GS, in0=self.S64[:, tsl],
                                           scalar1=self.vecs[0:64, V_KGRS:V_KGRS + 1], scalar2=None,
                                           op0=ALU.mult), reads=(BS, Bv), writes=(krwB,))
        S.dve.op(lambda h: h.tensor_tensor(out=KR, in0=krP[0:64, :], in1=GC, op=ALU.mult),
                 reads=(krB, krwB), writes=(krwB,))
        S.dve.op(lambda h: h.tensor_tensor(out=TMP, in0=krsP[0:64, :], in1=GS, op=ALU.mult),
                 reads=(krsB, krwB), writes=(krwB,))
        S.dve.op(lambda h: h.tensor_tensor(out=KR, in0=KR, in1=TMP, op=ALU.add), reads=(krwB,), writes=(krwB,))
        sqrB = self.B("sqr")
        S.act.op(lambda h: h.activation(out=self.sqr, in_=krP[0:64, :], func=AF.Square),
                 reads=(krB,), writes=(sqrB,))
        blk, bB = self.w_get(self.wblk(kv_w_b, 0, 4, 0, 2048), 4, 2048, hold=True)
        vblk, vbB = self.w_get(self.wblk(kv_w_b, 0, 4, 2048, 2048), 4, 2048, hold=True)
        v4 = [self.v_loc[(s, hh)].rearrange("(h p) (kt d) -> p h kt d", p=128, d=128) for hh in range(2)]

        def v_group(ts, cb):
            vst, vstB = self.vst[ts % 2], self.B(f"vst{ts % 2}")
            pi = 4 + (cb % 2)
            ps, psB = self.ps[pi], self.psB[pi]
            self.mm_group(ps[:, :], [(self.cn[:, k, ts * 128:(ts + 1) * 128], vblk[:, k, cb * 512:(cb + 1) * 512])
                                     for k in range(4)], reads=tuple(cnB) + (vbB,), writes=(psB,))
            S.dve.op(lambda h, ps=ps, vst=vst, cb=cb: h.tensor_copy(out=vst[:, cb, :], in_=ps[:, :]),
                     reads=(psB,), writes=(vstB,))
            if cb == 3:
                for c2 in range(4):
                    S.sp.dma(f"vst{ts % 2}", v4[c2 // 2][:, (c2 % 2) * 4:(c2 % 2) * 4 + 4, ts, :],
                             vst[:, c2, :].rearrange("p (hh d) -> p hh d", d=128), reads=(vstB,))

        for hd in range(NH):
            pi = 2 + (hd % 2)
            kn, knB = self.ps[pi], self.psB[pi]
            self.mm_group(kn[:, :], [(blk[:, k, hd * 128:(hd + 1) * 128], self.cn[:, k, :]) for k in range(4)],
                          reads=tuple(cnB) + (bB,), writes=(knB,))
            sqt = self.sqb[hd % 4]
            sqB_ = self.B(f"sqb{hd % 4}")
            S.act.op(lambda h, kn=kn, sqt=sqt: h.activation(out=sqt[:, 0:T], in_=kn[:, :], func=AF.Square),
                     reads=(knB,), writes=(sqB_,))
            si = 6 + (hd % 2)
            ss, ssB = self.ps[si], self.psB[si]
            self.mm_group(ss[:, :], [(self.ones[:, :], sqt[:, 0:T]), (self.ones[0:64, :], self.sqr)],
                          reads=(sqB_, sqrB, self.B("ones")), writes=(ssB,))
            sdB, rsB = self.B("sd"), self.B("rstd")
            S.act.op(lambda h, ss=ss: h.activation(out=self.sd[:, 0:T], in_=ss[:, :], func=AF.Ln,
                                                   scale=1.0 / 192, bias=self.epsc[:, 0:1]),
                     reads=(ssB,), writes=(sdB,))
            S.act.op(lambda h: h.activation(out=self.rstd[:, 0:T], in_=self.sd[:, 0:T], func=AF.Exp, scale=-0.5),
                     reads=(sdB,), writes=(rsB,))
            kst, kstB = self.kst[hd % 2], self.B(f"kst{hd % 2}")
            krst, krstB = self.krst[hd % 2], self.B(f"krst{hd % 2}")
            S.dve.op(lambda h, kn=kn, kst=kst: h.scalar_tensor_tensor(
                out=kst, in0=kn[:, :], scalar=self.vecs[:, V_KGN:V_KGN + 1], in1=self.rstd[:, 0:T],
                op0=ALU.mult, op1=ALU.mult), reads=(knB, rsB, Bv), writes=(kstB,))
            S.dve.op(lambda h, krst=krst: h.tensor_tensor(out=krst, in0=KR, in1=self.rstd[0:64, 0:T],
                                                          op=ALU.mult), reads=(krwB, rsB), writes=(krstB,))
            kd = self.kT_loc[(s, hd // 8)]
            r0 = (hd % 8) * 192
            S.sp.dma(f"kst{hd % 2}", kd[r0:r0 + 128, :], kst, reads=(kstB,))
            S.sp.dma(f"krst{hd % 2}", kd[r0 + 128:r0 + 192, :], krst, reads=(krstB,))
            v_group(hd // 4, hd % 4)
        self.wheld.clear()
        S.dve.op(lambda h: h.memset(self.fsc[:, 1:2], 0.0), writes=kv_alias + pool_bufs + (self.B("fsc"),))

    def store_x2(self, x2T, s):
        xB = [self.B(f"xT{c}") for c in range(DC)]
        self.S.sp.dma("x2st", x2T[:, :, s * T:(s + 1) * T], self.xT[:, :, HL:TH], reads=tuple(xB))

    def exchange(self, s):
        S = self.S
        for name, (sem, val) in S.dsems.items():
            if name.startswith(("kst", "krst", "vst")):
                S.pool._wait(Ev(sem, "dma:" + name, val))
        groups = [[0, 1], [2, 3], [4, 5], [6, 7]]
        if not hasattr(self, "ccsem"):
            self.ccsem = self.es_outer.enter_context(self.nc.semaphore("cc_sem"))
            self.ccn = 0
        ccsem = self.ccsem
        for hh in range(2):
            for src, dst, nm in ((self.kT_loc[(s, hh)], self.kT_all[(s, hh)], f"agk{s}{hh}"),
                                 (self.v_loc[(s, hh)], self.v_all[(s, hh)], f"agv{s}{hh}")):
                self.ccn += 1
                S.pool.thunks.append(lambda src=src, dst=dst: self.nc.gpsimd.collective_compute(
                    "AllGather", ALU.bypass, replica_groups=groups, ins=[src.opt()],
                    outs=[dst.opt()]).then_inc(ccsem))
                self.B(nm).writers = [Ev(ccsem, "cc", self.ccn)]

    def alloc_B(self):
        self.w_alloc(2, len(self.wplan))
        self.qf = self.sb("qf", [128, 4, T], F32)
        self.qln = self.sb("qln", [128, 4, T], BF16)
        self.sgB_ = self.sb("sgb", [128, NH, T], BF16)
        self.og = self.sgB_
        self.gq = self.sb("gq", [64, 4, T], F32)
        self.qnT = [self.sb(f"qnT{i}", [128, T], BF16) for i in range(2)]
        self.qrT = [self.sb(f"qrT{i}", [64, T], BF16) for i in range(2)]
        self.sqr2 = self.sb("sqr2", [64, T], BF16)
        self.rsq = self.sb("rsq", [128, T], F32)
        self.ot = self.sb("ot", [128, T], F32)
        self.KTn = [self.sb(f"KTn{i}", [128, 4096], BF16) for i in range(2)]
        self.KTr = [self.sb(f"KTr{i}", [64, 4096], BF16) for i in range(2)]
        self.Vh = [self.sb(f"Vh{i}", [128, 32, 128], BF16) for i in range(2)]
        self.PT = [self.sb(f"PT{i}", [128, T], BF16) for i in range(3)]
        self.xo = [self.qf[:, :, :].rearrange("p a b -> p (a b)")]
        self.kvcount = 0
        self.ptc = 0

    def plan_B_layer(self, w_in, w_qb, w_out, j):
        self.wplan.append((self.wblk(w_in, j * 2048, 16, 0, 512), 16, 512))
        for q in range(4):
            self.wplan.append((self.wblk(w_in, j * 2048, 16, 512 + q * 512, 512), 16, 512))
        self.wplan.append((self.wblk(w_qb, j * 512, 4, 0, 2048), 4, 2048))
        self.wplan.append((self.wblk(w_qb, j * 512, 4, 2048, 2048), 4, 2048))
        for q in range(4):
            self.wplan.append((self.wblk(w_out, j * 2048, 16, q * 512, 512), 16, 512))

    def load_x2(self, x2T, s):
        xB = [self.B(f"xT{c}") for c in range(DC)]
        reads = ()
        if self.mode == "F":
            reads = (self.B("x2dram"),)
        self.S.sp.dma("x2ld", self.xT[:, :, HL:TH], x2T[:, :, s * T:(s + 1) * T], reads=reads, writes=tuple(xB))

    def kv_load(self, s, hd):
        S = self.S
        i = self.kvcount % 2
        self.kvcount += 1
        KBn, KBr, VB = self.B(f"KTn{i}"), self.B(f"KTr{i}"), self.B(f"Vh{i}")
        sem = f"kv{i}"
        hh, h8 = divmod(hd, 8)
        for J in range(NOFF[s] // 4):
            r = 0 if J in SBS[0] else 1
            ls = SBS[r].index(J)
            rdk = (self.B(f"agk{ls}{hh}"),) if self.mode == "F" else ()
            rdv = (self.B(f"agv{ls}{hh}"),) if self.mode == "F" else ()
            kall = self.kT_all[(ls, hh)]
            vall = self.v_all[(ls, hh)].rearrange("(r h p) (kt d) -> r h p kt d", r=2, p=128, d=128)
            base = r * (NH // 2) * 192 + h8 * 192
            S.sp.dma(sem, self.KTn[i][:, J * T:(J + 1) * T], kall[base:base + 128, :], reads=rdk, writes=(KBn,))
            S.sp.dma(sem, self.KTr[i][:, J * T:(J + 1) * T], kall[base + 128:base + 192, :], reads=rdk,
                     writes=(KBr,))
            S.sp.dma(sem, self.Vh[i][:, J * 4:(J + 1) * 4, :], vall[r, h8, :, :, :], reads=rdv, writes=(VB,))
        J = NOFF[s] // 4
        kloc = self.kT_loc[(s, hh)]
        vloc = self.v_loc[(s, hh)].rearrange("(h p) (kt d) -> h p kt d", p=128, d=128)
        r0 = h8 * 192
        S.sp.dma(sem, self.KTn[i][:, J * T:(J + 1) * T], kloc[r0:r0 + 128, :], writes=(KBn,))
        S.sp.dma(sem, self.KTr[i][:, J * T:(J + 1) * T], kloc[r0 + 128:r0 + 192, :], writes=(KBr,))
        S.sp.dma(sem, self.Vh[i][:, J * 4:(J + 1) * 4, :], vloc[h8, :, :, :], writes=(VB,))
        return i

    def B_layer(self, w_in, w_qb, w_out, j, s):
        S = self.S
        xB = [self.B(f"xT{c}") for c in range(DC)]
        hB = [self.B(f"h{c}") for c in range(DC)]
        Bv = self.B("vecs")
        M = slice(HL, TH)
        self.rms_stats(lambda c: self.xT[:, c, M], DC, T, [(0, T, self.ps[5][:, :], self.psB[5])],
                       V_BNORM + j * 16, lambda c: self.h[:, c, M], xB, hB, 1.0 / D, "b")
        kvi = self.kv_load(s, 0)
        qfB = [self.B(f"qf{m}") for m in range(4)]
        qlB = [self.B(f"qln{m}") for m in range(4)]
        blk, bB = self.w_get(self.wblk(w_in, j * 2048, 16, 0, 512), 16, 512)
        for m in range(4):
            ps, psB = self.ps[m], self.psB[m]
            self.mm_group(ps[:, :], [(blk[:, k, m * 128:(m + 1) * 128], self.h[:, k, M]) for k in range(DC)],
                          reads=tuple(hB) + (bB,), writes=(psB,))
            S.act.op(lambda h, ps=ps, m=m: h.activation(out=self.qf[:, m, :], in_=ps[:, :], func=AF.Copy),
                     reads=(psB,), writes=(qfB[m],))
        sgB = self.B("sgb")
        for q in range(4):
            blk, bB = self.w_get(self.wblk(w_in, j * 2048, 16, 512 + q * 512, 512), 16, 512)
            for m in range(4):
                hd = q * 4 + m
                ps, psB = self.ps[m], self.psB[m]
                self.mm_group(ps[:, :], [(blk[:, k, m * 128:(m + 1) * 128], self.h[:, k, M]) for k in range(DC)],
                              reads=tuple(hB) + (bB,), writes=(psB,))
                S.act.op(lambda h, ps=ps, hd=hd: h.activation(out=self.sgB_[:, hd, :], in_=ps[:, :], func=AF.Silu),
                         reads=(psB,), writes=(sgB,))
        self.rms_stats(lambda c: self.qf[:, c, :], 4, T, [(0, T, self.ps[5][:, :], self.psB[5])],
                       V_BQLAT + j * 4, lambda c: self.qln[:, c, :], qfB, qlB, 1.0 / 512, "q")
        tsl = slice(s * T, (s + 1) * T)
        GC, GS, TMP = (self.gq[:, i, :] for i in range(3))
        gqB = self.B("gq")
        gtB = self.B("gqtmp")
        S.dve.op(lambda h: h.tensor_scalar(out=GC, in0=self.C64[:, tsl],
                                           scalar1=self.vecs[0:64, V_QGR + j:V_QGR + j + 1], scalar2=None,
                                           op0=ALU.mult), reads=(self.B("C64"), Bv), writes=(gqB,))
        S.dve.op(lambda h: h.tensor_scalar(out=GS, in0=self.S64[:, tsl],
                                           scalar1=self.vecs[0:64, V_QGRS + j:V_QGRS + j + 1], scalar2=None,
                                           op0=ALU.mult), reads=(self.B("S64"), Bv), writes=(gqB,))
        wn, wnB = self.w_get(self.wblk(w_qb, j * 512, 4, 0, 2048), 4, 2048, hold=True)
        wr, wrB = self.w_get(self.wblk(w_qb, j * 512, 4, 2048, 2048), 4, 2048, hold=True)
        ogB = self.B("sgb")
        TMP2 = self.gq[:, 3, :]
        gt2B = self.B("gqtmp2")

        def q_proj(hd):
            qn, qnB = self.ps[4], self.psB[4]
            qr, qrB = self.ps[5], self.psB[5]
            qs, qsB = self.ps[6], self.psB[6]
            self.mm_group(qn[:, :], [(wn[:, k, hd * 128:(hd + 1) * 128], self.qln[:, k, :]) for k in range(4)],
                          reads=tuple(qlB) + (wnB,), writes=(qnB,))
            self.mm_group(qr[0:64, :], [(wr[:, k, hd * 64:(hd + 1) * 64], self.qln[:, k, :]) for k in range(4)],
                          reads=tuple(qlB) + (wrB,), writes=(qrB,))
            self.mm_group(qs[0:64, :], [(wr[:, k, 1024 + hd * 64:1024 + (hd + 1) * 64], self.qln[:, k, :])
                                        for k in range(4)], reads=tuple(qlB) + (wrB,), writes=(qsB,))
            sqt, sqB_ = self.sqb[hd % 4], self.B(f"sqb{hd % 4}")
            S.act.op(lambda h, sqt=sqt: h.activation(out=sqt[:, 0:T], in_=qn[:, :], func=AF.Square),
                     reads=(qnB,), writes=(sqB_,))
            sqrB = self.B("sqr2")
            S.act.op(lambda h: h.activation(out=self.sqr2[:, :], in_=qr[0:64, :], func=AF.Square),
                     reads=(qrB,), writes=(sqrB,))
            return lambda: q_proj_b(hd, qn, qnB, qr, qrB, qs, qsB, sqt, sqB_, sqrB)

        def q_proj_b(hd, qn, qnB, qr, qrB, qs, qsB, sqt, sqB_, sqrB):
            ss, ssB = self.ps[7], self.psB[7]
            self.mm_group(ss[:, :], [(self.ones[:, :], sqt[:, 0:T]), (self.ones[0:64, :], self.sqr2[:, :])],
                          reads=(sqB_, sqrB, self.B("ones")), writes=(ssB,))
            sdB, rsB = self.B("sd"), self.B("rstd")
            S.act.op(lambda h: h.activation(out=self.sd[:, 0:T], in_=ss[:, :], func=AF.Ln, scale=1.0 / 192,
                                            bias=self.epsc[:, 0:1]), reads=(ssB,), writes=(sdB,))
            S.act.op(lambda h: h.activation(out=self.rstd[:, 0:T], in_=self.sd[:, 0:T], func=AF.Exp, scale=-0.5),
                     reads=(sdB,), writes=(rsB,))
            qnT, qnTB = self.qnT[hd % 2], self.B(f"qnT{hd % 2}")
            qrT, qrTB = self.qrT[hd % 2], self.B(f"qrT{hd % 2}")
            S.dve.op(lambda h, qnT=qnT: h.scalar_tensor_tensor(
                out=qnT[:, :], in0=qn[:, :], scalar=self.vecs[:, V_QGN + j:V_QGN + j + 1], in1=self.rstd[:, 0:T],
                op0=ALU.mult, op1=ALU.mult), reads=(qnB, rsB, Bv), writes=(qnTB,))
            S.dve.op(lambda h: h.tensor_tensor(out=TMP, in0=qr[0:64, :], in1=GC, op=ALU.mult),
                     reads=(qrB, gqB), writes=(gtB,))
            S.dve.op(lambda h: h.tensor_tensor(out=TMP2, in0=qs[0:64, :], in1=GS, op=ALU.mult),
                     reads=(qsB, gqB), writes=(gt2B,))
            S.dve.op(lambda h: h.tensor_tensor(out=TMP, in0=TMP, in1=TMP2, op=ALU.add),
                     reads=(gtB, gt2B), writes=(gtB,))
            S.dve.op(lambda h, qrT=qrT: h.tensor_tensor(out=qrT[:, :], in0=TMP, in1=self.rstd[0:64, 0:T],
                                                        op=ALU.mult), reads=(gtB, rsB), writes=(qrTB,))

        q_proj(0)()
        pending = []
        for hd in range(NH):
            if hd + 1 < NH:
                kv_next = self.kv_load(s, hd + 1)
                pending.append(q_proj(hd + 1))
            qnT, qnTB = self.qnT[hd % 2], self.B(f"qnT{hd % 2}")
            qrT, qrTB = self.qrT[hd % 2], self.B(f"qrT{hd % 2}")
            KTn, KTr, Vh = self.KTn[kvi], self.KTr[kvi], self.Vh[kvi]
            KBn, KBr, VB = self.B(f"KTn{kvi}"), self.B(f"KTr{kvi}"), self.B(f"Vh{kvi}")
            O, OB = self.ps[2], self.psB[2]
            SU, SUB = self.ps[3], self.psB[3]
            ntile = NOFF[s] + 4

            def score(jt):
                d = jt - NOFF[s]
                c0 = 0 if d <= 0 else 128 * d
                st, stB = self.ps[jt % 2], self.psB[jt % 2]
                self.mm_group(st[:, c0:T], [(KTn[:, jt * 128:(jt + 1) * 128], qnT[:, c0:T]),
                                            (KTr[:, jt * 128:(jt + 1) * 128], qrT[:, c0:T])],
                              reads=(KBn, KBr, qnTB, qrTB), writes=(stB,))
                pt, ptB = self.PT[self.ptc % 3], self.B(f"PT{self.ptc % 3}")
                self.ptc += 1
                if d < 0:
                    bcol = V_ABIAS + NOFF_OFS[s] + jt
                    bias = self.vecs[:, bcol:bcol + 1]
                    S.act.op(lambda h, st=st, pt=pt, bias=bias: h.activation(
                        out=pt[:, :], in_=st[:, :], func=AF.Exp, scale=SM_SCALE, bias=bias),
                        reads=(stB, Bv), writes=(ptB,))
                else:
                    S.act.op(lambda h, st=st, pt=pt, c0=c0: h.activation(
                        out=pt[:, c0:T], in_=st[:, c0:T], func=AF.Exp, scale=SM_SCALE),
                        reads=(stB,), writes=(ptB,))
                    S.dve.op(lambda h, pt=pt, c0=c0: h.memset(pt[64:128, c0:c0 + 64], 0.0), reads=(),
                             writes=(ptB,))
                return pt, ptB, c0

            def pv(jt, pt, ptB, c0):
                first = (jt == 0)
                last = (jt == ntile - 1)
                S.pe.group([
                    lambda h, pt=pt, jt=jt, c0=c0, first=first, last=last, Vh=Vh: h.matmul(
                        O[:, c0:T], Vh[:, jt, :], pt[:, c0:T], start=first, stop=last),
                    lambda h, pt=pt, c0=c0, first=first, last=last: h.matmul(
                        SU[:, c0:T], self.ones[:, :], pt[:, c0:T], start=first, stop=last),
                ], reads=(ptB, VB, self.B("ones")), writes=(OB, SUB))

            prev = score(0)
            for jt in range(1, ntile):
                cur = score(jt)
                pv(jt - 1, *prev)
                prev = cur
                if jt == 2:
                    for f in pending:
                        f()
                    pending = []
            pv(ntile - 1, *prev)
            rsqB, otB = self.B("rsq"), self.B("ot")
            S.dve.op(lambda h: h.tensor_copy(out=self.rsq[:, :], in_=SU[:, :]), reads=(SUB,), writes=(rsqB,))
            S.dve.op(lambda h: h.tensor_copy(out=self.ot[:, :], in_=O[:, :]), reads=(OB,), writes=(otB,))
            def head_end(hd=hd):
                S.act.op(lambda h: h.activation(out=self.rsq[:, :], in_=self.rsq[:, :], func=AF.Ln),
                         reads=(rsqB,), writes=(rsqB,))
                S.act.op(lambda h: h.activation(out=self.rsq[:, :], in_=self.rsq[:, :], func=AF.Exp, scale=-1.0),
                         reads=(rsqB,), writes=(rsqB,))
                S.dve.op(lambda h: h.tensor_tensor(out=self.ot[:, :], in0=self.ot[:, :], in1=self.rsq[:, :],
                                                   op=ALU.mult), reads=(otB, rsqB), writes=(otB,))
                S.dve.op(lambda h: h.tensor_tensor(out=self.og[:, hd, :], in0=self.ot[:, :],
                                                   in1=self.sgB_[:, hd, :], op=ALU.mult),
                         reads=(otB, sgB), writes=(ogB,))
            if hd + 1 < NH:
                pending.insert(0, head_end)
            else:
                head_end()
            if DBG and hd == 0 and not getattr(self, "dbg_done", False):
                self.dbg_done = True
                S.sp.dma("dbg", self.dbg_bf[:, 0, :], qnT[:, :], reads=(qnTB,))
                S.sp.dma("dbg", self.dbg_bf[0:64, 1, :], qrT[:, :], reads=(qrTB,))
                S.sp.dma("dbg", self.dbg_bf[:, 3, :], KTn[:, 0:T], reads=(KBn,))
                S.sp.dma("dbg", self.dbg_bf[0:64, 4, :], KTr[:, 0:T], reads=(KBr,))
                S.sp.dma("dbg", self.dbg_bf[:, 5, :], Vh[:, 0:4, :].rearrange("p a b -> p (a b)"), reads=(VB,))
                S.sp.dma("dbg", self.dbg_f[:, 0, :], self.ot[:, :], reads=(otB,))
                S.sp.dma("dbg", self.dbg_f[:, 1, :], self.rsq[:, :], reads=(rsqB,))
            if hd + 1 < NH:
                kvi = kv_next
        self.wheld.clear()
        for q in range(4):
            blk, bB = self.w_get(self.wblk(w_out, j * 2048, 16, q * 512, 512), 16, 512)
            for m in range(4):
                oc = q * 4 + m
                ps, psB = self.ps[m % 2], self.psB[m % 2]
                self.mm_group(ps[:, :], [(blk[:, k, m * 128:(m + 1) * 128], self.og[:, k, :]) for k in range(NH)],
                              reads=(ogB, bB), writes=(psB,))
                S.dve.op(lambda h, ps=ps, oc=oc: h.tensor_tensor(out=self.xT[:, oc, M], in0=self.xT[:, oc, M],
                                                                 in1=ps[:, :], op=ALU.add),
                         reads=(psB,), writes=(xB[oc],))

    def store_out(self, out_d, s):
        S = self.S
        xB = [self.B(f"xT{c}") for c in range(DC)]
        qfB = tuple(self.B(f"qf{m}") for m in range(4))
        S.dve.op(lambda h: h.memset(self.rsq[:, 0:1], 0.0), writes=qfB + (self.B("xo"), self.B("rsq")))
        Bid = self.B("ident")
        for ts in range(4):
            xo, xoB = self.xo[0], self.B("xo")
            for cg in range(4):
                pi = 6 + (cg % 2)
                ps, psB = self.ps[pi], self.psB[pi]
                fns = []
                for jj in range(4):
                    c = cg * 4 + jj
                    fns.append(lambda h, ps=ps, c=c, jj=jj, ts=ts: h.transpose(
                        out=ps[:, jj * 128:(jj + 1) * 128], in_=self.xT[:, c, HL + ts * 128:HL + (ts + 1) * 128],
                        identity=self.ident[:, :]))
                S.pe.group(fns, reads=tuple(xB[cg * 4:cg * 4 + 4]) + (Bid,), writes=(psB,))
                eng = S.act if cg % 2 == 0 else S.dve
                dst = xo[:, cg * 512:(cg + 1) * 512]
                if eng is S.act:
                    eng.op(lambda h, ps=ps, dst=dst: h.activation(out=dst, in_=ps[:, :], func=AF.Copy),
                           reads=(psB,), writes=(xoB,))
                else:
                    eng.op(lambda h, ps=ps, dst=dst: h.tensor_copy(out=dst, in_=ps[:, :]), reads=(psB,),
                           writes=(xoB,))
            r0 = s * T + ts * 128
            S.sp.dma("xo", out_d[r0:r0 + 128, :], xo, reads=(xoB,))


_PROGS = {}


def _prog(mode):
    if mode not in _PROGS:
        _PROGS[mode] = Prog(mode).build()
    return _PROGS[mode]


def _cols(v, n):
    return np.ascontiguousarray(np.asarray(v, np.float32).reshape(n, 128).T)


def _build_vecs(inp, r):
    vecs = np.zeros((128, NV), np.float32)
    for l in range(2):
        vecs[:, V_ANORM + l * 16:V_ANORM + (l + 1) * 16] = _cols(inp["a_norm_g"][l], 16)
        vecs[:, V_ASCALE + l * 32:V_ASCALE + (l + 1) * 32] = _cols(inp["a_scale"][l], 32)
        vecs[:, V_BNORM + l * 16:V_BNORM + (l + 1) * 16] = _cols(inp["b_norm_g"][l], 16)
        vecs[:, V_BQLAT + l * 4:V_BQLAT + (l + 1) * 4] = _cols(inp["b_q_latent_g"][l], 4)
        qg = np.asarray(inp["b_q_norm_g"][l], np.float32)
        vecs[:, V_QGN + l] = qg[:128]
        vecs[:64, V_QGR + l] = qg[128:]
        vecs[:64, V_QGRS + l] = np.concatenate([qg[160:192], qg[128:160]])
    vecs[:, V_KVNORM:V_KVNORM + 16] = _cols(inp["kv_norm_g"], 16)
    vecs[:, V_KVLAT:V_KVLAT + 4] = _cols(inp["kv_latent_g"], 4)
    kg = np.asarray(inp["k_norm_g"], np.float32)
    vecs[:, V_KGN] = kg[:128]
    vecs[:64, V_KGR] = kg[128:]
    vecs[:64, V_KGRS] = np.concatenate([kg[160:192], kg[128:160]])
    invf = (10000.0 ** (-np.arange(0, 64, 2, dtype=np.float32) / 64)).astype(np.float32)
    vecs[:64, V_INVF] = np.concatenate([invf, invf])
    vecs[:32, V_SGN] = -1.0
    vecs[32:64, V_SGN] = 1.0
    for s in range(NSLOT):
        for g, w in enumerate(WIN):
            t = np.arange(16, dtype=np.float32)
            if SBS[r][s] == 0:
                c = 1.0 / np.minimum(t + 1, float(w))
            else:
                c = np.full(16, 1.0 / w, np.float32)
            vecs[:, V_CINV + (s * 4 + g) * 16:V_CINV + (s * 4 + g + 1) * 16] = c[None, :]
        for jt in range(NOFF[s]):
            valid = jt < 4 * SBS[r][s]
            vecs[:, V_ABIAS + NOFF_OFS[s] + jt] = 0.0 if valid else NEG
    return vecs


def _prep_common(inp):
    w = {}
    w["a_w_in"] = np.ascontiguousarray(inp["a_w_in"], np.float32).reshape(2 * 2048, 8192)
    w["a_w_g"] = np.ascontiguousarray(inp["a_w_group"], np.float32).reshape(8 * 1024, 1024)
    w["a_w_out"] = np.ascontiguousarray(inp["a_w_out"], np.float32).reshape(2 * 4096, 2048)
    wa = np.asarray(inp["kv_w_a"], np.float32)
    w["kv_w_a"] = np.ascontiguousarray(np.concatenate([wa, wa[:, 544:576], wa[:, 512:544]], axis=1))
    wb = np.asarray(inp["kv_w_b"], np.float32).reshape(512, NH, 2, 128)
    w["kv_w_b"] = np.ascontiguousarray(np.concatenate([wb[:, :, 0, :].reshape(512, 2048),
                                                       wb[:, :, 1, :].reshape(512, 2048)], axis=1))
    w["b_w_in"] = np.ascontiguousarray(inp["b_w_in"], np.float32).reshape(2 * 2048, 2560)
    wq = np.asarray(inp["b_w_q_b"], np.float32).reshape(2, 512, NH, 192)
    nope = wq[:, :, :, :128].reshape(2, 512, 2048)
    rope = wq[:, :, :, 128:].reshape(2, 512, 1024)
    ropes = np.concatenate([wq[:, :, :, 160:192], wq[:, :, :, 128:160]], axis=3).reshape(2, 512, 1024)
    w["b_w_qb"] = np.ascontiguousarray(np.concatenate([nope, rope, ropes], axis=2)).reshape(2 * 512, 4096)
    w["b_w_out"] = np.ascontiguousarray(inp["b_w_out"], np.float32).reshape(2 * 2048, 2048)
    return w


def _core_inputs(inp, c):
    b, r = divmod(c, 2)
    x = np.asarray(inp["x"], np.float32)[b]
    pos = np.asarray(inp["positions"], np.int32)[b]
    xin = np.zeros((NSLOT, TH, D), np.float32)
    posr = np.zeros((NSLOT * T,), np.int32)
    for s, sb in enumerate(SBS[r]):
        t0 = sb * T
        if sb > 0:
            xin[s] = x[t0 - HL:t0 + T]
        else:
            xin[s, HL:] = x[0:T]
        posr[s * T:(s + 1) * T] = pos[t0:t0 + T]
    return xin, np.ascontiguousarray(np.broadcast_to(posr[None, :], (64, NSLOT * T))), _build_vecs(inp, r)


FUSED = True


def kernel(**inp):
    wts = _prep_common(inp)
    ident = np.eye(128, dtype=np.float32)
    cores = list(range(8))
    per = [_core_inputs(inp, c) for c in cores]
    A_keys = ["a_w_in", "a_w_g", "a_w_out", "kv_w_a", "kv_w_b"]
    B_keys = ["b_w_in", "b_w_qb", "b_w_out"]
    if FUSED:
        nc = _prog("F")
        maps = []
        for c in cores:
            m = {"vecs": per[c][2], "pos": per[c][1], "ident": ident, "xin": per[c][0]}
            for k in A_keys + B_keys:
                m[k] = wts[k]
            maps.append(m)
        res = run_bass_kernel_spmd(nc, maps, core_ids=cores)
        outs = [r["out"] for r in res.results]
    else:
        ncA = _prog("A")
        maps = []
        for c in cores:
            m = {"vecs": per[c][2], "pos": per[c][1], "ident": ident, "xin": per[c][0]}
            for k in A_keys:
                m[k] = wts[k]
            maps.append(m)
        resA = run_bass_kernel_spmd(ncA, maps, core_ids=cores).results
        ncB = _prog("B")
        maps = []
        for c in cores:
            p = c - (c % 2)
            m = {"vecs": per[c][2], "pos": per[c][1], "ident": ident, "x2T": resA[c]["x2T"]}
            for s_ in range(NSLOT):
                for hh in range(2):
                    for nm in ("kT", "v"):
                        key = f"{nm}_loc_{s_}_{hh}"
                        m[key] = resA[c][key]
                        m[f"{nm}_all_{s_}_{hh}"] = np.concatenate([resA[p][key], resA[p + 1][key]], axis=0)
            for k in B_keys:
                m[k] = wts[k]
            maps.append(m)
        resB = run_bass_kernel_spmd(ncB, maps, core_ids=cores).results
        outs = [r["out"] for r in resB]
    out = np.zeros((4, 4096, D), np.float32)
    for c in cores:
        b, r = divmod(c, 2)
        for s, sb in enumerate(SBS[r]):
            out[b, sb * T:(sb + 1) * T] = outs[c][s * T:(s + 1) * T]
    return out
```

```python
import numpy as np
import ml_dtypes
from contextlib import ExitStack
import concourse.bass as bass
import concourse.mybir as mybir
from concourse.bass_utils import run_bass_kernel_spmd

F32 = mybir.dt.float32
BF16 = mybir.dt.bfloat16
I32 = mybir.dt.int32
AF = mybir.ActivationFunctionType
ALU = mybir.AluOpType

D = 2048
DC = 16
T = 512
HL = 32
TH = T + HL
PADW = 16 + TH
NSLOT = 4
SBS = ([0, 3, 4, 7], [1, 2, 5, 6])
NOFF = [4, 12, 20, 28]
NOFF_OFS = [0, 4, 16, 36]
EPS = 1e-6
NH = 16
WIN = (2, 4, 8, 16)
SM_SCALE = 192 ** -0.5
NEG = -30000.0
WSLOT = 8192
NWSLOT = 3
DBG_SLOTS = [0, 1, 2, 3]
DBG_LAYERS = 2
DBG = False

V_ANORM = 0
V_ASCALE = 32
V_KVNORM = 96
V_KVLAT = 112
V_KGN = 116
V_KGR = 117
V_KGRS = 118
V_BNORM = 119
V_BQLAT = 151
V_QGN = 159
V_QGR = 161
V_QGRS = 163
V_INVF = 165
V_SGN = 166
V_CINV = 168
V_ABIAS = 424
NV = 512


class Ev:
    __slots__ = ("sem", "key", "val")

    def __init__(self, sem, key, val):
        self.sem, self.key, self.val = sem, key, val


class Buf:
    __slots__ = ("name", "writers", "readers")

    def __init__(self, name):
        self.name = name
        self.writers = []
        self.readers = {}


class Eng:
    def __init__(self, sched, name, handle, sem):
        self.S = sched
        self.name = name
        self.h = handle
        self.sem = sem
        self.cnt = 0
        self.known = {}
        self.thunks = []

    def _wait(self, ev):
        if ev is None:
            return
        if self.name == "pe" and ev.key == "pe":
            return
        if self.known.get(ev.key, 0) >= ev.val:
            return
        self.known[ev.key] = ev.val
        h, sem, val = self.h, ev.sem, ev.val
        self.thunks.append(lambda: h.wait_ge(sem, val))

    def deps(self, reads, writes):
        for b in reads:
            for e in b.writers:
                self._wait(e)
        for b in writes:
            for e in b.writers:
                self._wait(e)
            for e in b.readers.values():
                self._wait(e)

    def mark(self, ev, reads, writes):
        for b in reads:
            old = b.readers.get(ev.key)
            if old is None or old.val < ev.val:
                b.readers[ev.key] = ev
        for b in writes:
            b.writers = [ev]
            b.readers = {}

    def op(self, fn, reads=(), writes=()):
        self.deps(reads, writes)
        self.cnt += 1
        sem, val = self.sem, self.cnt
        self.thunks.append(lambda: fn(self.h).then_inc(sem, 1))
        ev = Ev(sem, self.name, val)
        self.mark(ev, reads, writes)
        return ev

    def group(self, fns, reads=(), writes=()):
        self.deps(reads, writes)
        self.cnt += 1
        sem, val = self.sem, self.cnt
        h = self.h
        for f in fns[:-1]:
            self.thunks.append(lambda f=f: f(h))
        last = fns[-1]
        self.thunks.append(lambda: last(h).then_inc(sem, 1))
        ev = Ev(sem, self.name, val)
        self.mark(ev, reads, writes)
        return ev

    def dma(self, semname, out, in_, reads=(), writes=()):
        self.deps(reads, writes)
        sem, val = self.S.dma_sem(semname)
        h = self.h
        self.thunks.append(lambda: h.dma_start(out=out, in_=in_).then_inc(sem, 16))
        ev = Ev(sem, "dma:" + semname, val)
        self.mark(ev, reads, writes)
        return ev


class Sched:
    def __init__(self, nc, es):
        self.nc = nc
        self.es = es
        self.dsems = {}
        mk = lambda n, h: Eng(self, n, h, es.enter_context(nc.semaphore("s_" + n)))
        self.pe = mk("pe", nc.tensor)
        self.act = mk("act", nc.scalar)
        self.dve = mk("dve", nc.vector)
        self.pool = mk("pool", nc.gpsimd)
        self.sp = mk("sp", nc.sync)
        self.all_ev = []

    def dma_sem(self, name):
        if name not in self.dsems:
            self.dsems[name] = [self.es.enter_context(self.nc.semaphore("d_" + name)), 0]
        ent = self.dsems[name]
        ent[1] += 16
        return ent[0], ent[1]

    def barrier(self):
        engs = (self.pe, self.act, self.dve, self.pool, self.sp)
        for x in engs:
            for y in engs:
                if y is not x and y.cnt:
                    x._wait(Ev(y.sem, y.name, y.cnt))
            for name, (sem, val) in self.dsems.items():
                if val:
                    x._wait(Ev(sem, "dma:" + name, val))

    def emit(self):
        nc = self.nc
        lists = {}
        for e in (self.pe, self.act, self.dve, self.pool, self.sp):
            lists[e.name] = e.thunks
            e.thunks = []
        with nc.Block() as block:
            @block.tensor
            def _(e):
                for t in lists["pe"]:
                    t()

            @block.scalar
            def _(e):
                for t in lists["act"]:
                    t()

            @block.vector
            def _(e):
                for t in lists["dve"]:
                    t()

            @block.gpsimd
            def _(e):
                for t in lists["pool"]:
                    t()

            @block.sync
            def _(e):
                for t in lists["sp"]:
                    t()


class Prog:
    def __init__(self, mode):
        self.mode = mode
        self.nc = bass.Bass("TRN2", target_bir_lowering=False)
        self.es = ExitStack()
        self.bufs = {}

    def dram_in(self, name, shape, dt):
        return self.nc.dram_tensor(name, list(shape), dt, kind="ExternalInput").ap()

    def dram_out(self, name, shape, dt):
        return self.nc.dram_tensor(name, list(shape), dt, kind="ExternalOutput").ap()

    def dram_int(self, name, shape, dt):
        return self.nc.dram_tensor(name, list(shape), dt, kind="Internal").ap()

    def sb(self, name, shape, dt):
        self._sbn = getattr(self, "_sbn", 0) + 1
        return self.es.enter_context(self.nc.sbuf_tensor(f"sb{self._sbn}_{name}", list(shape), dt))

    def B(self, name):
        if name not in self.bufs:
            self.bufs[name] = Buf(name)
        return self.bufs[name]

    def w_init(self):
        self.wplan = []
        self.wissued = 0
        self.wused = 0
        self.wbase = 0
        self.wend = 0
        self.nw = 0
        self.wheld = set()

    def w_alloc(self, nw, end):
        assert self.wissued == self.wused
        self.nw = nw
        self.wbase = self.wused
        self.wend = end
        self.wslots = [self.sb(f"wslot{i}", [128, WSLOT], BF16) for i in range(nw)]

    def w_issue_upto(self, n):
        S = self.S
        while self.wissued < min(n, self.wend):
            i = self.wissued
            if (i - self.nw) in self.wheld:
                break
            k = (i - self.wbase) % self.nw
            src, K, ncols = self.wplan[i]
            dst = self.wslots[k][:, 0:K * ncols].rearrange("p (k n) -> p k n", n=ncols)
            S.pool.dma(f"w{k}", dst, src, reads=(), writes=(self.B(f"wslot{k}"),))
            self.wissued += 1

    def w_get(self, src, K, ncols, hold=False):
        i = self.wused
        if hold:
            self.wheld.add(i)
        assert i < self.wend, "weight plan exhausted"
        psrc, pK, pn = self.wplan[i]
        assert (pK, pn) == (K, ncols) and str(psrc) == str(src), f"weight plan mismatch at {i}"
        self.w_issue_upto(i + self.nw)
        self.wused += 1
        k = (i - self.wbase) % self.nw
        return self.wslots[k][:, 0:K * ncols].rearrange("p (k n) -> p k n", n=ncols), self.B(f"wslot{k}")

    @staticmethod
    def wblk(W2d, r0, K, c0, ncols):
        return W2d[r0:r0 + K * 128, c0:c0 + ncols].rearrange("(k p) n -> p k n", p=128)

    def mm_group(self, out, pairs, reads, writes):
        n = len(pairs)
        fns = []
        for i, (l, r) in enumerate(pairs):
            fns.append(lambda h, l=l, r=r, i=i: h.matmul(out, l, r, start=(i == 0), stop=(i == n - 1)))
        return self.S.pe.group(fns, reads=reads, writes=writes)

    def rms_stats(self, src_fn, nchunk, ncols_total, col_ranges, g_cols, dst_fn, src_bufs, dst_bufs, inv_n,
                  tag):
        S = self.S
        sq = self.sqb
        sq_views = []
        for c in range(nchunk):
            sqt = sq[c % len(sq)]
            sqB = self.B(f"sqb{c % len(sq)}")
            src = src_fn(c)
            dst = sqt[:, 0:ncols_total]
            S.act.op(lambda h, src=src, dst=dst: h.activation(out=dst, in_=src, func=AF.Square),
                     reads=(src_bufs[c],), writes=(sqB,))
            for (c0, c1, ps, psB) in col_ranges:
                l = self.ones[:, :]
                r = sqt[:, c0:c1]
                S.pe.group([lambda h, ps=ps, l=l, r=r, c=c: h.matmul(ps, l, r, start=(c == 0),
                                                                  stop=(c == nchunk - 1))],
                           reads=(sqB,), writes=(psB,))
        rstdB = self.B("rstd")
        for (c0, c1, ps, psB) in col_ranges:
            sd = self.sd[:, c0:c1]
            S.act.op(lambda h, ps=ps, sd=sd: h.activation(out=sd, in_=ps, func=AF.Ln, scale=inv_n,
                                                        bias=self.epsc[:, 0:1]),
                     reads=(psB,), writes=(self.B("sd"),))
            rs = self.rstd[:, c0:c1]
            S.act.op(lambda h, sd=sd, rs=rs: h.activation(out=rs, in_=sd, func=AF.Exp, scale=-0.5),
                     reads=(self.B("sd"),), writes=(rstdB,))
        for c in range(nchunk):
            src = src_fn(c)
            dst = dst_fn(c)
            g = self.vecs[:, g_cols + c:g_cols + c + 1]
            rs = self.rstd[:, 0:ncols_total]
            S.dve.op(lambda h, src=src, dst=dst, g=g, rs=rs: h.scalar_tensor_tensor(
                out=dst, in0=src, scalar=g, in1=rs, op0=ALU.mult, op1=ALU.mult),
                reads=(src_bufs[c], rstdB), writes=(dst_bufs[c],))

    def build(self):
        nc, es = self.nc, self.es
        mode = self.mode
        doA = mode in ("A", "F")
        doB = mode in ("B", "F")
        with es:
            self.S = S = Sched(nc, es)
            self.es_outer = es
            vecs_d = self.dram_in("vecs", [128, NV], F32)
            pos_d = self.dram_in("pos", [64, NSLOT * T], I32)
            ident_d = self.dram_in("ident", [128, 128], F32)
            if doA:
                xin = self.dram_in("xin", [NSLOT, TH, D], F32)
                a_w_in = self.dram_in("a_w_in", [2 * 2048, 8192], F32)
                a_w_g = self.dram_in("a_w_g", [8 * 1024, 1024], F32)
                a_w_out = self.dram_in("a_w_out", [2 * 4096, 2048], F32)
                kv_w_a = self.dram_in("kv_w_a", [2048, 640], F32)
                kv_w_b = self.dram_in("kv_w_b", [512, 4096], F32)
            if doB:
                b_w_in = self.dram_in("b_w_in", [2 * 2048, 2560], F32)
                b_w_qb = self.dram_in("b_w_qb", [2 * 512, 4096], F32)
                b_w_out = self.dram_in("b_w_out", [2 * 2048, 2048], F32)
                out_d = self.dram_out("out", [NSLOT * T, D], F32)
                if DBG:
                    self.dbg_bf = self.dram_out("dbg_bf", [128, 6, T], BF16)
                    self.dbg_f = self.dram_out("dbg_f", [128, 4, T], F32)
            KS, VS = [NH // 2 * 192, T], [NH // 2 * 128, T]
            KS2, VS2 = [NH * 192, T], [NH * 128, T]
            sh = [(s_, hh) for s_ in range(NSLOT) for hh in range(2)]
            if mode == "A":
                x2T = self.dram_out("x2T", [128, DC, NSLOT * T], F32)
                kT_loc = {k: self.dram_out(f"kT_loc_{k[0]}_{k[1]}", KS, BF16) for k in sh}
                v_loc = {k: self.dram_out(f"v_loc_{k[0]}_{k[1]}", VS, BF16) for k in sh}
            elif mode == "B":
                x2T = self.dram_in("x2T", [128, DC, NSLOT * T], F32)
                kT_loc = {k: self.dram_in(f"kT_loc_{k[0]}_{k[1]}", KS, BF16) for k in sh}
                v_loc = {k: self.dram_in(f"v_loc_{k[0]}_{k[1]}", VS, BF16) for k in sh}
                kT_all = {k: self.dram_in(f"kT_all_{k[0]}_{k[1]}", KS2, BF16) for k in sh}
                v_all = {k: self.dram_in(f"v_all_{k[0]}_{k[1]}", VS2, BF16) for k in sh}
            else:
                x2T = self.dram_int("x2T", [128, DC, NSLOT * T], F32)
                kT_loc = {k: self.dram_int(f"kT_loc_{k[0]}_{k[1]}", KS, BF16) for k in sh}
                v_loc = {k: self.dram_int(f"v_loc_{k[0]}_{k[1]}", VS, BF16) for k in sh}
                kT_all = {k: self.dram_int(f"kT_all_{k[0]}_{k[1]}", KS2, BF16) for k in sh}
                v_all = {k: self.dram_int(f"v_all_{k[0]}_{k[1]}", VS2, BF16) for k in sh}
            self.kT_loc, self.v_loc = kT_loc, v_loc
            if doB:
                self.kT_all, self.v_all = kT_all, v_all

            self.vecs = self.sb("vecs", [128, NV], F32)
            self.ident = self.sb("ident", [128, 128], F32)
            self.ones = self.sb("ones", [128, 128], BF16)
            self.epsc = self.sb("epsc", [128, 1], F32)
            self.C64 = self.sb("C64", [64, NSLOT * T], F32)
            self.S64 = self.sb("S64", [64, NSLOT * T], F32)
            self.sqb = [self.sb(f"sqb{i}", [128, TH], BF16) for i in range(4)]
            self.sd = self.sb("sd", [128, TH], F32)
            self.rstd = self.sb("rstd", [128, TH], F32)
            self.xT = self.sb("xT", [128, DC, TH], F32)
            self.h = self.sb("h", [128, DC, TH], BF16)
            self.ps = [es.enter_context(nc.psum_tensor(f"ps{i}", [128, 512], F32)) for i in range(8)]
            self.psB = [self.B(f"ps{i}") for i in range(8)]
            self.w_init()

            S.sp.dma("c0", self.vecs[:, :], vecs_d, writes=(self.B("vecs"),))
            S.sp.dma("cid", self.ident[:, :], ident_d, writes=(self.B("ident"),))
            S.dve.op(lambda h: h.memset(self.ones[:, :], 1.0), writes=(self.B("ones"),))
            S.dve.op(lambda h: h.memset(self.epsc[:, :], EPS), writes=(self.B("epsc"),))
            self.rope_tables(pos_d)

            if doA:
                for s in range(NSLOT):
                    for l in range(2):
                        self.plan_A_layer(a_w_in, a_w_g, a_w_out, l)
                    self.plan_KV(kv_w_a, kv_w_b)
            self.wplanA_end = len(self.wplan)
            if doB:
                for s in DBG_SLOTS:
                    for j in range(DBG_LAYERS):
                        self.plan_B_layer(b_w_in, b_w_qb, b_w_out, j)

            if doA:
                with ExitStack() as esA:
                    es_saved, self.es = self.es, esA
                    self.alloc_A()
                    for s in range(NSLOT):
                        self.load_x_tile(xin, s)
                        for l in range(2):
                            self.A_layer(a_w_in, a_w_g, a_w_out, l, s)
                        self.KV_tile(kv_w_a, kv_w_b, s)
                        self.store_x2(x2T, s)
                        if mode == "F":
                            self.exchange(s)
                    if mode == "F":
                        S.barrier()
                        S.emit()
                    self.es = es_saved
                    if mode == "A":
                        self.finish()
            if doB:
                with ExitStack() as esB:
                    es_saved, self.es = self.es, esB
                    self.alloc_B()
                    for s in DBG_SLOTS:
                        self.load_x2(x2T, s)
                        for j in range(DBG_LAYERS):
                            self.B_layer(b_w_in, b_w_qb, b_w_out, j, s)
                        self.store_out(out_d, s)
                    self.es = es_saved
                    self.finish()
        return nc

    def finish(self):
        S = self.S
        assert self.wused == len(self.wplan), (self.wused, len(self.wplan))
        S.barrier()
        S.emit()

    def rope_tables(self, pos_d):
        S = self.S
        NT = NSLOT * T
        with ExitStack() as es2:
            posi = es2.enter_context(self.nc.sbuf_tensor("posi", [64, NT], I32))
            ang = es2.enter_context(self.nc.sbuf_tensor("ang", [64, NT], F32))
            t1 = es2.enter_context(self.nc.sbuf_tensor("rt1", [64, NT], F32))
            ti = es2.enter_context(self.nc.sbuf_tensor("rti", [64, NT], I32))
            Bp, Ba, Bt, Bi = self.B("posi"), self.B("ang"), self.B("rt1"), self.B("rti")
            Bv = self.B("vecs")
            S.sp.dma("c1", posi[:, :], pos_d, writes=(Bp,))
            S.dve.op(lambda h: h.tensor_copy(out=ang[:, :], in_=posi[:, :]), reads=(Bp,), writes=(Ba,))
            invf = self.vecs[0:64, V_INVF:V_INVF + 1]
            S.dve.op(lambda h: h.tensor_scalar(out=ang[:, :], in0=ang[:, :], scalar1=invf, scalar2=None,
                                               op0=ALU.mult), reads=(Ba, Bv), writes=(Ba,))
            C1 = 6.28125
            C2 = 2.0 * np.pi - C1
            for which, dst, Bd in (("sin", self.S64, self.B("S64")), ("cos", self.C64, self.B("C64"))):
                src = ang
                if which == "cos":
                    S.dve.op(lambda h: h.tensor_scalar(out=t1[:, :], in0=ang[:, :], scalar1=float(np.pi / 2),
                                                       scalar2=None, op0=ALU.add), reads=(Ba,), writes=(Bt,))
                    src = t1
                    Bs = Bt
                else:
                    Bs = Ba
                S.dve.op(lambda h, src=src, dst=dst: h.tensor_scalar(out=dst[:, :], in0=src[:, :],
                                                            scalar1=float(1.0 / (2 * np.pi)), scalar2=None,
                                                            op0=ALU.mult), reads=(Bs,), writes=(Bd,))
                S.dve.op(lambda h, dst=dst: h.tensor_copy(out=ti[:, :], in_=dst[:, :]), reads=(Bd,), writes=(Bi,))
                S.dve.op(lambda h, dst=dst: h.tensor_copy(out=dst[:, :], in_=ti[:, :]), reads=(Bi,), writes=(Bd,))
                S.dve.op(lambda h, src=src, dst=dst: h.scalar_tensor_tensor(out=t1[:, :], in0=dst[:, :], scalar=-C1,
                                                                   in1=src[:, :], op0=ALU.mult, op1=ALU.add),
                         reads=(Bd, Bs), writes=(Bt,))
                S.dve.op(lambda h, dst=dst: h.scalar_tensor_tensor(out=t1[:, :], in0=dst[:, :], scalar=-C2,
                                                          in1=t1[:, :], op0=ALU.mult, op1=ALU.add),
                         reads=(Bd, Bt), writes=(Bt,))
                S.dve.op(lambda h: h.tensor_scalar(out=t1[:, :], in0=t1[:, :], scalar1=3.14159, scalar2=-3.14159,
                                                   op0=ALU.min, op1=ALU.max), reads=(Bt,), writes=(Bt,))
                if which == "sin":
                    sgn = self.vecs[0:64, V_SGN:V_SGN + 1]
                    S.act.op(lambda h, dst=dst: h.activation(out=dst[:, :], in_=t1[:, :], func=AF.Sin, scale=sgn),
                             reads=(Bt, Bv), writes=(Bd,))
                else:
                    S.act.op(lambda h, dst=dst: h.activation(out=dst[:, :], in_=t1[:, :], func=AF.Sin),
                             reads=(Bt,), writes=(Bd,))
            S.barrier()
            S.emit()

    def alloc_A(self):
        self.w_alloc(3, self.wplanA_end)
        self.xtok = [self.sb(f"xtok{i}", [128, D], F32) for i in range(2)]
        self.U = self.sb("U", [128, 8, PADW], F32)
        self.pa = self.sb("pa", [128, 2, PADW], F32)
        self.pb = self.sb("pb", [128, 2, PADW], F32)
        self.sg = self.sb("sg", [128, 8, TH], BF16)
        self.PL = self.sb("PL", [128, 8, TH], BF16)
        self.y = self.sb("y", [128, 8, TH], BF16)
        self.cf = self.U[:, 0:4, 0:T]
        self.krw = self.U[0:64, 4:8, 0:T]
        self.cn = self.PL[:, 0:4, 0:T]
        self.sqr = self.PL[0:64, 4, 0:T]
        self.kst = [self.sg[:, i, 0:T] for i in range(2)]
        self.krst = [self.sg[0:64, 2 + i, 0:T] for i in range(2)]
        self.vst = [self.y[:, 4 * i:4 * i + 4, 0:T] for i in range(2)]
        self.fsc = self.sb("fsc", [128, 4], F32)
        S = self.S
        S.dve.op(lambda h: h.memset(self.U[:, :, 0:16], 0.0), reads=(), writes=(self.B("U"),))

    def plan_A_layer(self, w_in, w_g, w_out, l):
        for g in range(4):
            for half in range(2):
                self.wplan.append((self.wblk(w_in, l * 2048, 16, g * 1024 + half * 512, 512), 16, 512))
            for half in range(2):
                self.wplan.append((self.wblk(w_in, l * 2048, 16, 4096 + g * 1024 + half * 512, 512), 16, 512))
            for half in range(2):
                self.wplan.append((self.wblk(w_g, (l * 4 + g) * 1024, 8, half * 512, 512), 8, 512))
            for q in range(4):
                self.wplan.append((self.wblk(w_out, l * 4096 + g * 1024, 8, q * 512, 512), 8, 512))

    def load_x_tile(self, xin, s):
        S = self.S
        Bid = self.B("ident")
        subs = [(0, HL, 0)] + [(HL + i * 128, 128, HL + i * 128) for i in range(4)]
        pre = getattr(self, "x_prefetched", -1)
        for si, (r0, nr, c0) in enumerate(subs):
            xt = self.xtok[si % 2]
            Bx = self.B(f"xtok{si % 2}")
            if not (pre == s and si < 2):
                S.sp.dma(f"xtok{si % 2}", xt[0:nr, :], xin[s, r0:r0 + nr, :], writes=(Bx,))
            for cg in range(4):
                pi = 6 + (cg % 2)
                ps, psB = self.ps[pi], self.psB[pi]
                fns = []
                for j in range(4):
                    c = cg * 4 + j
                    fns.append(lambda h, ps=ps, xt=xt, c=c, j=j, nr=nr: h.transpose(
                        out=ps[:, j * 128:j * 128 + nr], in_=xt[0:nr, c * 128:(c + 1) * 128],
                        identity=self.ident[0:nr, 0:nr]))
                S.pe.group(fns, reads=(Bx, Bid), writes=(psB,))
                src = ps[:, :].rearrange("p (j n) -> p j n", n=128)[:, :, 0:nr]
                dst = self.xT[:, cg * 4:cg * 4 + 4, c0:c0 + nr]
                wb = tuple(self.B(f"xT{c}") for c in range(cg * 4, cg * 4 + 4))
                eng = S.act if cg % 2 == 0 else S.dve
                if eng is S.act:
                    eng.op(lambda h, src=src, dst=dst: h.activation(out=dst, in_=src, func=AF.Copy),
                           reads=(psB,), writes=wb)
                else:
                    eng.op(lambda h, src=src, dst=dst: h.tensor_copy(out=dst, in_=src), reads=(psB,), writes=wb)
        if s + 1 < NSLOT:
            for si, (r0, nr, c0) in enumerate(subs[:2]):
                S.sp.dma(f"xtok{si % 2}", self.xtok[si % 2][0:nr, :], xin[s + 1, r0:r0 + nr, :],
                         writes=(self.B(f"xtok{si % 2}"),))
            self.x_prefetched = s + 1

    def A_layer(self, w_in, w_g, w_out, l, s):
        S = self.S
        halo_full = (l == 0)
        xB = [self.B(f"xT{c}") for c in range(DC)]
        hB = [self.B(f"h{c}") for c in range(DC)]
        Bv = self.B("vecs")
        self.rms_stats(lambda c: self.xT[:, c, :], DC, TH,
                       [(HL, TH, self.ps[5][:, :], self.psB[5]), (0, HL, self.ps[4][:, 0:HL], self.psB[4])],
                       V_ANORM + l * 16, lambda c: self.h[:, c, :], xB, hB, 1.0 / D, "a")
        mi = [0]

        def next_main():
            i = mi[0] % 4
            mi[0] += 1
            return self.ps[i], self.psB[i]

        hi = [0]

        def next_halo():
            i = hi[0] % 16
            hi[0] += 1
            return self.ps[4][:, i * HL:(i + 1) * HL], self.psB[4]

        for g in range(4):
            w = WIN[g]
            UB = self.B("U")
            sgB = self.B("sg")
            PLB = self.B("PL")
            yB = self.B("y")
            for half in range(2):
                blk, bB = self.w_get(self.wblk(w_in, l * 2048, 16, g * 1024 + half * 512, 512), 16, 512)
                for m in range(4):
                    j = half * 4 + m
                    ps, psB = next_main()
                    self.mm_group(ps[:, :], [(blk[:, k, m * 128:(m + 1) * 128], self.h[:, k, HL:TH])
                                             for k in range(DC)], reads=tuple(hB) + (bB,), writes=(psB,))
                    S.act.op(lambda h, ps=ps, j=j: h.activation(out=self.U[:, j, 16 + HL:PADW], in_=ps[:, :],
                                                               func=AF.Copy), reads=(psB,), writes=(UB,))
                    ph, phB = next_halo()
                    self.mm_group(ph, [(blk[:, k, m * 128:(m + 1) * 128], self.h[:, k, 0:HL])
                                       for k in range(DC)], reads=tuple(hB) + (bB,), writes=(phB,))
                    S.act.op(lambda h, ph=ph, j=j: h.activation(out=self.U[:, j, 16:16 + HL], in_=ph,
                                                               func=AF.Copy), reads=(phB,), writes=(UB,))
            paB, pbB = self.B("pa"), self.B("pb")
            for jj in range(4):
                Uv = self.U[:, 2 * jj:2 * jj + 2, :]
                A_, B_ = self.pa, self.pb
                S.dve.op(lambda h, Uv=Uv: h.tensor_tensor(out=A_[:, :, 1:PADW], in0=Uv[:, :, 1:PADW],
                                                          in1=Uv[:, :, 0:PADW - 1], op=ALU.add),
                         reads=(UB,), writes=(paB,))
                cur, curB, oth, othB = A_, paB, B_, pbB
                sh = 2
                lo = 1
                while sh < w:
                    lo2 = lo + sh
                    S.dve.op(lambda h, cur=cur, oth=oth, lo2=lo2, sh=sh: h.tensor_tensor(
                        out=oth[:, :, lo2:PADW], in0=cur[:, :, lo2:PADW], in1=cur[:, :, lo2 - sh:PADW - sh],
                        op=ALU.add), reads=(curB,), writes=(othB,))
                    cur, curB, oth, othB = oth, othB, cur, curB
                    lo = lo2
                    sh *= 2
                S.dve.op(lambda h, cur=cur, Uv=Uv, jj=jj, w=w: h.scalar_tensor_tensor(
                    out=self.PL[:, 2 * jj:2 * jj + 2, :], in0=cur[:, :, 16:PADW], scalar=1.0 / w,
                    in1=Uv[:, :, 16:PADW], op0=ALU.mult, op1=ALU.subtract), reads=(curB, UB), writes=(PLB,))
                for q in range(2):
                    ci = V_CINV + (s * 4 + g) * 16
                    cinv = self.vecs[:, ci:ci + 16]
                    S.dve.op(lambda h, cur=cur, q=q, cinv=cinv, oth=oth: h.tensor_tensor(
                        out=oth[:, q, 0:16], in0=cur[:, q, 16 + HL:16 + HL + 16], in1=cinv, op=ALU.mult),
                        reads=(curB, Bv), writes=(othB,))
                    S.dve.op(lambda h, q=q, jj=jj, Uv=Uv, oth=oth: h.tensor_tensor(
                        out=self.PL[:, 2 * jj + q, HL:HL + 16], in0=oth[:, q, 0:16],
                        in1=Uv[:, q, 16 + HL:16 + HL + 16], op=ALU.subtract), reads=(othB, UB), writes=(PLB,))
            for half in range(2):
                blk, bB = self.w_get(self.wblk(w_in, l * 2048, 16, 4096 + g * 1024 + half * 512, 512), 16, 512)
                for m in range(4):
                    j = half * 4 + m
                    ps, psB = next_main()
                    self.mm_group(ps[:, :], [(blk[:, k, m * 128:(m + 1) * 128], self.h[:, k, HL:TH])
                                             for k in range(DC)], reads=tuple(hB) + (bB,), writes=(psB,))
                    S.act.op(lambda h, ps=ps, j=j: h.activation(out=self.sg[:, j, HL:TH], in_=ps[:, :],
                                                               func=AF.Silu), reads=(psB,), writes=(sgB,))
                    if halo_full:
                        ph, phB = next_halo()
                        self.mm_group(ph, [(blk[:, k, m * 128:(m + 1) * 128], self.h[:, k, 0:HL])
                                           for k in range(DC)], reads=tuple(hB) + (bB,), writes=(phB,))
                        S.act.op(lambda h, ph=ph, j=j: h.activation(out=self.sg[:, j, 0:HL], in_=ph,
                                                                   func=AF.Silu), reads=(phB,), writes=(sgB,))
            for half in range(2):
                blk, bB = self.w_get(self.wblk(w_g, (l * 4 + g) * 1024, 8, half * 512, 512), 8, 512)
                for m in range(4):
                    j = half * 4 + m
                    sc = self.vecs[:, V_ASCALE + l * 32 + g * 8 + j:V_ASCALE + l * 32 + g * 8 + j + 1]
                    ps, psB = next_main()
                    self.mm_group(ps[:, :], [(blk[:, k, m * 128:(m + 1) * 128], self.PL[:, k, HL:TH])
                                             for k in range(8)], reads=(PLB, bB), writes=(psB,))
                    S.dve.op(lambda h, ps=ps, j=j, sc=sc: h.scalar_tensor_tensor(
                        out=self.y[:, j, HL:TH], in0=ps[:, :], scalar=sc, in1=self.sg[:, j, HL:TH],
                        op0=ALU.mult, op1=ALU.mult), reads=(psB, sgB, Bv), writes=(yB,))
                    if halo_full:
                        ph, phB = next_halo()
                        self.mm_group(ph, [(blk[:, k, m * 128:(m + 1) * 128], self.PL[:, k, 0:HL])
                                           for k in range(8)], reads=(PLB, bB), writes=(phB,))
                        S.dve.op(lambda h, ph=ph, j=j, sc=sc: h.scalar_tensor_tensor(
                            out=self.y[:, j, 0:HL], in0=ph, scalar=sc, in1=self.sg[:, j, 0:HL],
                            op0=ALU.mult, op1=ALU.mult), reads=(phB, sgB, Bv), writes=(yB,))
            for q in range(4):
                blk, bB = self.w_get(self.wblk(w_out, l * 4096 + g * 1024, 8, q * 512, 512), 8, 512)
                for m in range(4):
                    oc = q * 4 + m
                    ps, psB = next_main()
                    self.mm_group(ps[:, :], [(blk[:, k, m * 128:(m + 1) * 128], self.y[:, k, HL:TH])
                                             for k in range(8)], reads=(yB, bB), writes=(psB,))
                    S.dve.op(lambda h, ps=ps, oc=oc: h.tensor_tensor(
                        out=self.xT[:, oc, HL:TH], in0=self.xT[:, oc, HL:TH], in1=ps[:, :], op=ALU.add),
                        reads=(psB,), writes=(xB[oc],))
                    if halo_full:
                        ph, phB = next_halo()
                        self.mm_group(ph, [(blk[:, k, m * 128:(m + 1) * 128], self.y[:, k, 0:HL])
                                           for k in range(8)], reads=(yB, bB), writes=(phB,))
                        S.dve.op(lambda h, ph=ph, oc=oc: h.tensor_tensor(
                            out=self.xT[:, oc, 0:HL], in0=self.xT[:, oc, 0:HL], in1=ph, op=ALU.add),
                            reads=(phB,), writes=(xB[oc],))

    def plan_KV(self, kv_w_a, kv_w_b):
        self.wplan.append((self.wblk(kv_w_a, 0, 16, 0, 512), 16, 512))
        self.wplan.append((self.wblk(kv_w_a, 0, 16, 512, 128), 16, 128))
        self.wplan.append((self.wblk(kv_w_b, 0, 4, 0, 2048), 4, 2048))
        self.wplan.append((self.wblk(kv_w_b, 0, 4, 2048, 2048), 4, 2048))

    def KV_tile(self, kv_w_a, kv_w_b, s):
        S = self.S
        xB = [self.B(f"xT{c}") for c in range(DC)]
        hB = [self.B(f"h{c}") for c in range(DC)]
        Bv = self.B("vecs")
        self.rms_stats(lambda c: self.xT[:, c, :], DC, TH,
                       [(HL, TH, self.ps[5][:, :], self.psB[5]), (0, HL, self.ps[4][:, 0:HL], self.psB[4])],
                       V_KVNORM, lambda c: self.h[:, c, :], xB, hB, 1.0 / D, "kv")
        cfB = [self.B(f"cf{j}") for j in range(4)]
        cnB = [self.B(f"cn{j}") for j in range(4)]
        kv_alias = tuple(cfB) + tuple(cnB) + tuple(self.B(n) for n in (
            "krw", "sqr", "kst0", "kst1", "krst0", "krst1", "vst0", "vst1"))
        pool_bufs = tuple(self.B(n) for n in ("U", "PL", "sg", "y"))
        S.dve.op(lambda h: h.memset(self.fsc[:, 0:1], 0.0), writes=kv_alias + pool_bufs + (self.B("fsc"),))
        blk, bB = self.w_get(self.wblk(kv_w_a, 0, 16, 0, 512), 16, 512)
        for m in range(4):
            ps, psB = self.ps[m % 4], self.psB[m % 4]
            self.mm_group(ps[:, :], [(blk[:, k, m * 128:(m + 1) * 128], self.h[:, k, HL:TH]) for k in range(DC)],
                          reads=tuple(hB) + (bB,), writes=(psB,))
            S.act.op(lambda h, ps=ps, m=m: h.activation(out=self.cf[:, m, :], in_=ps[:, :], func=AF.Copy),
                     reads=(psB,), writes=(cfB[m],))
        blk, bB = self.w_get(self.wblk(kv_w_a, 0, 16, 512, 128), 16, 128)
        krP, krB = self.ps[0], self.psB[0]
        krsP, krsB = self.ps[1], self.psB[1]
        self.mm_group(krP[0:64, :], [(blk[:, k, 0:64], self.h[:, k, HL:TH]) for k in range(DC)],
                      reads=tuple(hB) + (bB,), writes=(krB,))
        self.mm_group(krsP[0:64, :], [(blk[:, k, 64:128], self.h[:, k, HL:TH]) for k in range(DC)],
                      reads=tuple(hB) + (bB,), writes=(krsB,))
        self.rms_stats(lambda c: self.cf[:, c, :], 4, T, [(0, T, self.ps[5][:, :], self.psB[5])],
                       V_KVLAT, lambda c: self.cn[:, c, :], cfB, cnB, 1.0 / 512, "c")
        tsl = slice(s * T, (s + 1) * T)
        GC, GS, KR, TMP = (self.krw[:, i, :] for i in range(4))
        krwB = self.B("krw")
        BC, BS = self.B("C64"), self.B("S64")
        S.dve.op(lambda h: h.tensor_scalar(out=GC, in0=self.C64[:, tsl], scalar1=self.vecs[0:64, V_KGR:V_KGR + 1],
                                           scalar2=None, op0=ALU.mult), reads=(BC, Bv), writes=(krwB,))
        S.dve.op(lambda h: h.tensor_scalar(out=GS, in0=self.S64[:, tsl],
                                           scalar1=self.vecs[0:64, V_KGRS:V_KGRS + 1], scalar2=None,
                                           op0=ALU.mult), reads=(BS, Bv), writes=(krwB,))
        S.dve.op(lambda h: h.tensor_tensor(out=KR, in0=krP[0:64, :], in1=GC, op=ALU.mult),
                 reads=(krB, krwB), writes=(krwB,))
        S.dve.op(lambda h: h.tensor_tensor(out=TMP, in0=krsP[0:64, :], in1=GS, op=ALU.mult),
                 reads=(krsB, krwB), writes=(krwB,))
        S.dve.op(lambda h: h.tensor_tensor(out=KR, in0=KR, in1=TMP, op=ALU.add), reads=(krwB,), writes=(krwB,))
        sqrB = self.B("sqr")
        S.act.op(lambda h: h.activation(out=self.sqr, in_=krP[0:64, :], func=AF.Square),
                 reads=(krB,), writes=(sqrB,))
        blk, bB = self.w_get(self.wblk(kv_w_b, 0, 4, 0, 2048), 4, 2048, hold=True)
        vblk, vbB = self.w_get(self.wblk(kv_w_b, 0, 4, 2048, 2048), 4, 2048, hold=True)
        v4 = [self.v_loc[(s, hh)].rearrange("(h p) (kt d) -> p h kt d", p=128, d=128) for hh in range(2)]

        def v_group(ts, cb):
            vst, vstB = self.vst[ts % 2], self.B(f"vst{ts % 2}")
            pi = 4 + (cb % 2)
            ps, psB = self.ps[pi], self.psB[pi]
            self.mm_group(ps[:, :], [(self.cn[:, k, ts * 128:(ts + 1) * 128], vblk[:, k, cb * 512:(cb + 1) * 512])
                                     for k in range(4)], reads=tuple(cnB) + (vbB,), writes=(psB,))
            S.dve.op(lambda h, ps=ps, vst=vst, cb=cb: h.tensor_copy(out=vst[:, cb, :], in_=ps[:, :]),
                     reads=(psB,), writes=(vstB,))
            if cb == 3:
                for c2 in range(4):
                    S.sp.dma(f"vst{ts % 2}", v4[c2 // 2][:, (c2 % 2) * 4:(c2 % 2) * 4 + 4, ts, :],
                             vst[:, c2, :].rearrange("p (hh d) -> p hh d", d=128), reads=(vstB,))

        for hd in range(NH):
            pi = 2 + (hd % 2)
            kn, knB = self.ps[pi], self.psB[pi]
            self.mm_group(kn[:, :], [(blk[:, k, hd * 128:(hd + 1) * 128], self.cn[:, k, :]) for k in range(4)],
                          reads=tuple(cnB) + (bB,), writes=(knB,))
            sqt = self.sqb[hd % 4]
            sqB_ = self.B(f"sqb{hd % 4}")
            S.act.op(lambda h, kn=kn, sqt=sqt: h.activation(out=sqt[:, 0:T], in_=kn[:, :], func=AF.Square),
                     reads=(knB,), writes=(sqB_,))
            si = 6 + (hd % 2)
            ss, ssB = self.ps[si], self.psB[si]
            self.mm_group(ss[:, :], [(self.ones[:, :], sqt[:, 0:T]), (self.ones[0:64, :], self.sqr)],
                          reads=(sqB_, sqrB, self.B("ones")), writes=(ssB,))
            sdB, rsB = self.B("sd"), self.B("rstd")
            S.act.op(lambda h, ss=ss: h.activation(out=self.sd[:, 0:T], in_=ss[:, :], func=AF.Ln,
                                                   scale=1.0 / 192, bias=self.epsc[:, 0:1]),
                     reads=(ssB,), writes=(sdB,))
            S.act.op(lambda h: h.activation(out=self.rstd[:, 0:T], in_=self.sd[:, 0:T], func=AF.Exp, scale=-0.5),
                     reads=(sdB,), writes=(rsB,))
            kst, kstB = self.kst[hd % 2], self.B(f"kst{hd % 2}")
            krst, krstB = self.krst[hd % 2], self.B(f"krst{hd % 2}")
            S.dve.op(lambda h, kn=kn, kst=kst: h.scalar_tensor_tensor(
                out=kst, in0=kn[:, :], scalar=self.vecs[:, V_KGN:V_KGN + 1], in1=self.rstd[:, 0:T],
                op0=ALU.mult, op1=ALU.mult), reads=(knB, rsB, Bv), writes=(kstB,))
            S.dve.op(lambda h, krst=krst: h.tensor_tensor(out=krst, in0=KR, in1=self.rstd[0:64, 0:T],
                                                          op=ALU.mult), reads=(krwB, rsB), writes=(krstB,))
            kd = self.kT_loc[(s, hd // 8)]
            r0 = (hd % 8) * 192
            S.sp.dma(f"kst{hd % 2}", kd[r0:r0 + 128, :], kst, reads=(kstB,))
            S.sp.dma(f"krst{hd % 2}", kd[r0 + 128:r0 + 192, :], krst, reads=(krstB,))
            v_group(hd // 4, hd % 4)
        self.wheld.clear()
        S.dve.op(lambda h: h.memset(self.fsc[:, 1:2], 0.0), writes=kv_alias + pool_bufs + (self.B("fsc"),))

    def store_x2(self, x2T, s):
        xB = [self.B(f"xT{c}") for c in range(DC)]
        self.S.sp.dma("x2st", x2T[:, :, s * T:(s + 1) * T], self.xT[:, :, HL:TH], reads=tuple(xB))

    def exchange(self, s):
        S = self.S
        for name, (sem, val) in S.dsems.items():
            if name.startswith(("kst", "krst", "vst")):
                S.pool._wait(Ev(sem, "dma:" + name, val))
        groups = [[0, 1], [2, 3], [4, 5], [6, 7]]
        if not hasattr(self, "ccsem"):
            self.ccsem = self.es_outer.enter_context(self.nc.semaphore("cc_sem"))
            self.ccn = 0
        ccsem = self.ccsem
        for hh in range(2):
            for src, dst, nm in ((self.kT_loc[(s, hh)], self.kT_all[(s, hh)], f"agk{s}{hh}"),
                                 (self.v_loc[(s, hh)], self.v_all[(s, hh)], f"agv{s}{hh}")):
                self.ccn += 1
                S.pool.thunks.append(lambda src=src, dst=dst: self.nc.gpsimd.collective_compute(
                    "AllGather", ALU.bypass, replica_groups=groups, ins=[src.opt()],
                    outs=[dst.opt()]).then_inc(ccsem))
                self.B(nm).writers = [Ev(ccsem, "cc", self.ccn)]

    def alloc_B(self):
        self.w_alloc(2, len(self.wplan))
        self.qf = self.sb("qf", [128, 4, T], F32)
        self.qln = self.sb("qln", [128, 4, T], BF16)
        self.sgB_ = self.sb("sgb", [128, NH, T], BF16)
        self.og = self.sgB_
        self.gq = self.sb("gq", [64, 4, T], F32)
        self.qnT = [self.sb(f"qnT{i}", [128, T], BF16) for i in range(2)]
        self.qrT = [self.sb(f"qrT{i}", [64, T], BF16) for i in range(2)]
        self.sqr2 = self.sb("sqr2", [64, T], BF16)
        self.rsq = self.sb("rsq", [128, T], F32)
        self.ot = self.sb("ot", [128, T], F32)
        self.KTn = [self.sb(f"KTn{i}", [128, 4096], BF16) for i in range(2)]
        self.KTr = [self.sb(f"KTr{i}", [64, 4096], BF16) for i in range(2)]
        self.Vh = [self.sb(f"Vh{i}", [128, 32, 128], BF16) for i in range(2)]
        self.PT = [self.sb(f"PT{i}", [128, T], BF16) for i in range(3)]
        self.xo = [self.qf[:, :, :].rearrange("p a b -> p (a b)")]
        self.kvcount = 0
        self.ptc = 0

    def plan_B_layer(self, w_in, w_qb, w_out, j):
        self.wplan.append((self.wblk(w_in, j * 2048, 16, 0, 512), 16, 512))
        for q in range(4):
            self.wplan.append((self.wblk(w_in, j * 2048, 16, 512 + q * 512, 512), 16, 512))
        self.wplan.append((self.wblk(w_qb, j * 512, 4, 0, 2048), 4, 2048))
        self.wplan.append((self.wblk(w_qb, j * 512, 4, 2048, 2048), 4, 2048))
        for q in range(4):
            self.wplan.append((self.wblk(w_out, j * 2048, 16, q * 512, 512), 16, 512))

    def load_x2(self, x2T, s):
        xB = [self.B(f"xT{c}") for c in range(DC)]
        reads = ()
        if self.mode == "F":
            reads = (self.B("x2dram"),)
        self.S.sp.dma("x2ld", self.xT[:, :, HL:TH], x2T[:, :, s * T:(s + 1) * T], reads=reads, writes=tuple(xB))

    def kv_load(self, s, hd):
        S = self.S
        i = self.kvcount % 2
        self.kvcount += 1
        KBn, KBr, VB = self.B(f"KTn{i}"), self.B(f"KTr{i}"), self.B(f"Vh{i}")
        sem = f"kv{i}"
        hh, h8 = divmod(hd, 8)
        for J in range(NOFF[s] // 4):
            r = 0 if J in SBS[0] else 1
            ls = SBS[r].index(J)
            rdk = (self.B(f"agk{ls}{hh}"),) if self.mode == "F" else ()
            rdv = (self.B(f"agv{ls}{hh}"),) if self.mode == "F" else ()
            kall = self.kT_all[(ls, hh)]
            vall = self.v_all[(ls, hh)].rearrange("(r h p) (kt d) -> r h p kt d", r=2, p=128, d=128)
            base = r * (NH // 2) * 192 + h8 * 192
            S.sp.dma(sem, self.KTn[i][:, J * T:(J + 1) * T], kall[base:base + 128, :], reads=rdk, writes=(KBn,))
            S.sp.dma(sem, self.KTr[i][:, J * T:(J + 1) * T], kall[base + 128:base + 192, :], reads=rdk,
                     writes=(KBr,))
            S.sp.dma(sem, self.Vh[i][:, J * 4:(J + 1) * 4, :], vall[r, h8, :, :, :], reads=rdv, writes=(VB,))
        J = NOFF[s] // 4
        kloc = self.kT_loc[(s, hh)]
        vloc = self.v_loc[(s, hh)].rearrange("(h p) (kt d) -> h p kt d", p=128, d=128)
        r0 = h8 * 192
        S.sp.dma(sem, self.KTn[i][:, J * T:(J + 1) * T], kloc[r0:r0 + 128, :], writes=(KBn,))
        S.sp.dma(sem, self.KTr[i][:, J * T:(J + 1) * T], kloc[r0 + 128:r0 + 192, :], writes=(KBr,))
        ev = S.sp.dma(sem, self.Vh[i][:, J * 4:(J + 1) * 4, :], vloc[h8, :, :, :], writes=(VB,))
        KBn.writers = [ev]
        KBr.writers = [ev]
        VB.writers = [ev]
        return i

    def B_layer(self, w_in, w_qb, w_out, j, s):
        S = self.S
        xB = [self.B(f"xT{c}") for c in range(DC)]
        hB = [self.B(f"h{c}") for c in range(DC)]
        Bv = self.B("vecs")
        M = slice(HL, TH)
        self.rms_stats(lambda c: self.xT[:, c, M], DC, T, [(0, T, self.ps[5][:, :], self.psB[5])],
                       V_BNORM + j * 16, lambda c: self.h[:, c, M], xB, hB, 1.0 / D, "b")
        kvi = self.kv_load(s, 0)
        qfB = [self.B(f"qf{m}") for m in range(4)]
        qlB = [self.B(f"qln{m}") for m in range(4)]
        blk, bB = self.w_get(self.wblk(w_in, j * 2048, 16, 0, 512), 16, 512)
        for m in range(4):
            ps, psB = self.ps[m], self.psB[m]
            self.mm_group(ps[:, :], [(blk[:, k, m * 128:(m + 1) * 128], self.h[:, k, M]) for k in range(DC)],
                          reads=tuple(hB) + (bB,), writes=(psB,))
            S.act.op(lambda h, ps=ps, m=m: h.activation(out=self.qf[:, m, :], in_=ps[:, :], func=AF.Copy),
                     reads=(psB,), writes=(qfB[m],))
        sgB = self.B("sgb")
        for q in range(4):
            blk, bB = self.w_get(self.wblk(w_in, j * 2048, 16, 512 + q * 512, 512), 16, 512)
            for m in range(4):
                hd = q * 4 + m
                ps, psB = self.ps[m], self.psB[m]
                self.mm_group(ps[:, :], [(blk[:, k, m * 128:(m + 1) * 128], self.h[:, k, M]) for k in range(DC)],
                              reads=tuple(hB) + (bB,), writes=(psB,))
                S.act.op(lambda h, ps=ps, hd=hd: h.activation(out=self.sgB_[:, hd, :], in_=ps[:, :], func=AF.Silu),
                         reads=(psB,), writes=(sgB,))
        self.rms_stats(lambda c: self.qf[:, c, :], 4, T, [(0, T, self.ps[5][:, :], self.psB[5])],
                       V_BQLAT + j * 4, lambda c: self.qln[:, c, :], qfB, qlB, 1.0 / 512, "q")
        tsl = slice(s * T, (s + 1) * T)
        GC, GS, TMP = (self.gq[:, i, :] for i in range(3))
        gqB = self.B("gq")
        gtB = self.B("gqtmp")
        S.dve.op(lambda h: h.tensor_scalar(out=GC, in0=self.C64[:, tsl],
                                           scalar1=self.vecs[0:64, V_QGR + j:V_QGR + j + 1], scalar2=None,
                                           op0=ALU.mult), reads=(self.B("C64"), Bv), writes=(gqB,))
        S.dve.op(lambda h: h.tensor_scalar(out=GS, in0=self.S64[:, tsl],
                                           scalar1=self.vecs[0:64, V_QGRS + j:V_QGRS + j + 1], scalar2=None,
                                           op0=ALU.mult), reads=(self.B("S64"), Bv), writes=(gqB,))
        wn, wnB = self.w_get(self.wblk(w_qb, j * 512, 4, 0, 2048), 4, 2048, hold=True)
        wr, wrB = self.w_get(self.wblk(w_qb, j * 512, 4, 2048, 2048), 4, 2048, hold=True)
        ogB = self.B("sgb")
        TMP2 = self.gq[:, 3, :]
        gt2B = self.B("gqtmp2")

        def q_proj(hd):
            qn, qnB = self.ps[4], self.psB[4]
            qr, qrB = self.ps[5], self.psB[5]
            qs, qsB = self.ps[6], self.psB[6]
            self.mm_group(qn[:, :], [(wn[:, k, hd * 128:(hd + 1) * 128], self.qln[:, k, :]) for k in range(4)],
                          reads=tuple(qlB) + (wnB,), writes=(qnB,))
            self.mm_group(qr[0:64, :], [(wr[:, k, hd * 64:(hd + 1) * 64], self.qln[:, k, :]) for k in range(4)],
                          reads=tuple(qlB) + (wrB,), writes=(qrB,))
            self.mm_group(qs[0:64, :], [(wr[:, k, 1024 + hd * 64:1024 + (hd + 1) * 64], self.qln[:, k, :])
                                        for k in range(4)], reads=tuple(qlB) + (wrB,), writes=(qsB,))
            sqt, sqB_ = self.sqb[hd % 4], self.B(f"sqb{hd % 4}")
            S.act.op(lambda h, sqt=sqt: h.activation(out=sqt[:, 0:T], in_=qn[:, :], func=AF.Square),
                     reads=(qnB,), writes=(sqB_,))
            sqrB = self.B("sqr2")
            S.act.op(lambda h: h.activation(out=self.sqr2[:, :], in_=qr[0:64, :], func=AF.Square),
                     reads=(qrB,), writes=(sqrB,))
            return lambda: q_proj_b(hd, qn, qnB, qr, qrB, qs, qsB, sqt, sqB_, sqrB)

        def q_proj_b(hd, qn, qnB, qr, qrB, qs, qsB, sqt, sqB_, sqrB):
            ss, ssB = self.ps[7], self.psB[7]
            self.mm_group(ss[:, :], [(self.ones[:, :], sqt[:, 0:T]), (self.ones[0:64, :], self.sqr2[:, :])],
                          reads=(sqB_, sqrB, self.B("ones")), writes=(ssB,))
            sdB, rsB = self.B("sd"), self.B("rstd")
            S.act.op(lambda h: h.activation(out=self.sd[:, 0:T], in_=ss[:, :], func=AF.Ln, scale=1.0 / 192,
                                            bias=self.epsc[:, 0:1]), reads=(ssB,), writes=(sdB,))
            S.act.op(lambda h: h.activation(out=self.rstd[:, 0:T], in_=self.sd[:, 0:T], func=AF.Exp, scale=-0.5),
                     reads=(sdB,), writes=(rsB,))
            qnT, qnTB = self.qnT[hd % 2], self.B(f"qnT{hd % 2}")
            qrT, qrTB = self.qrT[hd % 2], self.B(f"qrT{hd % 2}")
            S.dve.op(lambda h, qnT=qnT: h.scalar_tensor_tensor(
                out=qnT[:, :], in0=qn[:, :], scalar=self.vecs[:, V_QGN + j:V_QGN + j + 1], in1=self.rstd[:, 0:T],
                op0=ALU.mult, op1=ALU.mult), reads=(qnB, rsB, Bv), writes=(qnTB,))
            S.dve.op(lambda h: h.tensor_tensor(out=TMP, in0=qr[0:64, :], in1=GC, op=ALU.mult),
                     reads=(qrB, gqB), writes=(gtB,))
            S.dve.op(lambda h: h.tensor_tensor(out=TMP2, in0=qs[0:64, :], in1=GS, op=ALU.mult),
                     reads=(qsB, gqB), writes=(gt2B,))
            S.dve.op(lambda h: h.tensor_tensor(out=TMP, in0=TMP, in1=TMP2, op=ALU.add),
                     reads=(gtB, gt2B), writes=(gtB,))
            S.dve.op(lambda h, qrT=qrT: h.tensor_tensor(out=qrT[:, :], in0=TMP, in1=self.rstd[0:64, 0:T],
                                                        op=ALU.mult), reads=(gtB, rsB), writes=(qrTB,))

        q_proj(0)()
        pending = []
        for hd in range(NH):
            if hd + 1 < NH:
                kv_next = self.kv_load(s, hd + 1)
                pending.append(q_proj(hd + 1))
            qnT, qnTB = self.qnT[hd % 2], self.B(f"qnT{hd % 2}")
            qrT, qrTB = self.qrT[hd % 2], self.B(f"qrT{hd % 2}")
            KTn, KTr, Vh = self.KTn[kvi], self.KTr[kvi], self.Vh[kvi]
            KBn, KBr, VB = self.B(f"KTn{kvi}"), self.B(f"KTr{kvi}"), self.B(f"Vh{kvi}")
            O, OB = self.ps[2], self.psB[2]
            SU, SUB = self.ps[3], self.psB[3]
            ntile = NOFF[s] + 4

            def score(jt):
                d = jt - NOFF[s]
                c0 = 0 if d <= 0 else 128 * d
                st, stB = self.ps[jt % 2], self.psB[jt % 2]
                self.mm_group(st[:, c0:T], [(KTn[:, jt * 128:(jt + 1) * 128], qnT[:, c0:T]),
                                            (KTr[:, jt * 128:(jt + 1) * 128], qrT[:, c0:T])],
                              reads=(KBn, KBr, qnTB, qrTB), writes=(stB,))
                pt, ptB = self.PT[self.ptc % 3], self.B(f"PT{self.ptc % 3}")
                self.ptc += 1
                if d < 0:
                    bcol = V_ABIAS + NOFF_OFS[s] + jt
                    bias = self.vecs[:, bcol:bcol + 1]
                    S.act.op(lambda h, st=st, pt=pt, bias=bias: h.activation(
                        out=pt[:, :], in_=st[:, :], func=AF.Exp, scale=SM_SCALE, bias=bias),
                        reads=(stB, Bv), writes=(ptB,))
                else:
                    S.act.op(lambda h, st=st, pt=pt, c0=c0: h.activation(
                        out=pt[:, c0:T], in_=st[:, c0:T], func=AF.Exp, scale=SM_SCALE),
                        reads=(stB,), writes=(ptB,))
                    S.dve.op(lambda h, pt=pt, c0=c0: h.memset(pt[64:128, c0:c0 + 64], 0.0), reads=(),
                             writes=(ptB,))
                return pt, ptB, c0

            def pv(jt, pt, ptB, c0):
                first = (jt == 0)
                last = (jt == ntile - 1)
                S.pe.group([
                    lambda h, pt=pt, jt=jt, c0=c0, first=first, last=last, Vh=Vh: h.matmul(
                        O[:, c0:T], Vh[:, jt, :], pt[:, c0:T], start=first, stop=last),
                    lambda h, pt=pt, c0=c0, first=first, last=last: h.matmul(
                        SU[:, c0:T], self.ones[:, :], pt[:, c0:T], start=first, stop=last),
                ], reads=(ptB, VB, self.B("ones")), writes=(OB, SUB))

            prev = score(0)
            for jt in range(1, ntile):
                cur = score(jt)
                pv(jt - 1, *prev)
                prev = cur
                if jt == 2:
                    for f in pending:
                        f()
                    pending = []
            pv(ntile - 1, *prev)
            rsqB, otB = self.B("rsq"), self.B("ot")
            S.dve.op(lambda h: h.tensor_copy(out=self.rsq[:, :], in_=SU[:, :]), reads=(SUB,), writes=(rsqB,))
            S.dve.op(lambda h: h.tensor_copy(out=self.ot[:, :], in_=O[:, :]), reads=(OB,), writes=(otB,))
            def head_end(hd=hd):
                S.act.op(lambda h: h.activation(out=self.rsq[:, :], in_=self.rsq[:, :], func=AF.Ln),
                         reads=(rsqB,), writes=(rsqB,))
                S.act.op(lambda h: h.activation(out=self.rsq[:, :], in_=self.rsq[:, :], func=AF.Exp, scale=-1.0),
                         reads=(rsqB,), writes=(rsqB,))
                S.dve.op(lambda h: h.tensor_tensor(out=self.ot[:, :], in0=self.ot[:, :], in1=self.rsq[:, :],
                                                   op=ALU.mult), reads=(otB, rsqB), writes=(otB,))
                S.dve.op(lambda h: h.tensor_tensor(out=self.og[:, hd, :], in0=self.ot[:, :],
                                                   in1=self.sgB_[:, hd, :], op=ALU.mult),
                         reads=(otB, sgB), writes=(ogB,))
            if hd + 1 < NH:
                pending.insert(0, head_end)
            else:
                head_end()
            if DBG and hd == 0 and not getattr(self, "dbg_done", False):
                self.dbg_done = True
                S.sp.dma("dbg", self.dbg_bf[:, 0, :], qnT[:, :], reads=(qnTB,))
                S.sp.dma("dbg", self.dbg_bf[0:64, 1, :], qrT[:, :], reads=(qrTB,))
                S.sp.dma("dbg", self.dbg_bf[:, 3, :], KTn[:, 0:T], reads=(KBn,))
                S.sp.dma("dbg", self.dbg_bf[0:64, 4, :], KTr[:, 0:T], reads=(KBr,))
                S.sp.dma("dbg", self.dbg_bf[:, 5, :], Vh[:, 0:4, :].rearrange("p a b -> p (a b)"), reads=(VB,))
                S.sp.dma("dbg", self.dbg_f[:, 0, :], self.ot[:, :], reads=(otB,))
                S.sp.dma("dbg", self.dbg_f[:, 1, :], self.rsq[:, :], reads=(rsqB,))
            if hd + 1 < NH:
                kvi = kv_next
        self.wheld.clear()
        for q in range(4):
            blk, bB = self.w_get(self.wblk(w_out, j * 2048, 16, q * 512, 512), 16, 512)
            for m in range(4):
                oc = q * 4 + m
                ps, psB = self.ps[m % 2], self.psB[m % 2]
                self.mm_group(ps[:, :], [(blk[:, k, m * 128:(m + 1) * 128], self.og[:, k, :]) for k in range(NH)],
                              reads=(ogB, bB), writes=(psB,))
                S.dve.op(lambda h, ps=ps, oc=oc: h.tensor_tensor(out=self.xT[:, oc, M], in0=self.xT[:, oc, M],
                                                                 in1=ps[:, :], op=ALU.add),
                         reads=(psB,), writes=(xB[oc],))

    def store_out(self, out_d, s):
        S = self.S
        xB = [self.B(f"xT{c}") for c in range(DC)]
        qfB = tuple(self.B(f"qf{m}") for m in range(4))
        S.dve.op(lambda h: h.memset(self.rsq[:, 0:1], 0.0), writes=qfB + (self.B("xo"), self.B("rsq")))
        Bid = self.B("ident")
        for ts in range(4):
            xo, xoB = self.xo[0], self.B("xo")
            for cg in range(4):
                pi = 6 + (cg % 2)
                ps, psB = self.ps[pi], self.psB[pi]
                fns = []
                for jj in range(4):
                    c = cg * 4 + jj
                    fns.append(lambda h, ps=ps, c=c, jj=jj, ts=ts: h.transpose(
                        out=ps[:, jj * 128:(jj + 1) * 128], in_=self.xT[:, c, HL + ts * 128:HL + (ts + 1) * 128],
                        identity=self.ident[:, :]))
                S.pe.group(fns, reads=tuple(xB[cg * 4:cg * 4 + 4]) + (Bid,), writes=(psB,))
                eng = S.act if cg % 2 == 0 else S.dve
                dst = xo[:, cg * 512:(cg + 1) * 512]
                if eng is S.act:
                    eng.op(lambda h, ps=ps, dst=dst: h.activation(out=dst, in_=ps[:, :], func=AF.Copy),
                           reads=(psB,), writes=(xoB,))
                else:
                    eng.op(lambda h, ps=ps, dst=dst: h.tensor_copy(out=dst, in_=ps[:, :]), reads=(psB,),
                           writes=(xoB,))
            r0 = s * T + ts * 128
            S.sp.dma("xo", out_d[r0:r0 + 128, :], xo, reads=(xoB,))


_PROGS = {}


def _prog(mode):
    if mode not in _PROGS:
        _PROGS[mode] = Prog(mode).build()
    return _PROGS[mode]


def _cols(v, n):
    return np.ascontiguousarray(np.asarray(v, np.float32).reshape(n, 128).T)


def _build_vecs(inp, r):
    vecs = np.zeros((128, NV), np.float32)
    for l in range(2):
        vecs[:, V_ANORM + l * 16:V_ANORM + (l + 1) * 16] = _cols(inp["a_norm_g"][l], 16)
        vecs[:, V_ASCALE + l * 32:V_ASCALE + (l + 1) * 32] = _cols(inp["a_scale"][l], 32)
        vecs[:, V_BNORM + l * 16:V_BNORM + (l + 1) * 16] = _cols(inp["b_norm_g"][l], 16)
        vecs[:, V_BQLAT + l * 4:V_BQLAT + (l + 1) * 4] = _cols(inp["b_q_latent_g"][l], 4)
        qg = np.asarray(inp["b_q_norm_g"][l], np.float32)
        vecs[:, V_QGN + l] = qg[:128]
        vecs[:64, V_QGR + l] = qg[128:]
        vecs[:64, V_QGRS + l] = np.concatenate([qg[160:192], qg[128:160]])
    vecs[:, V_KVNORM:V_KVNORM + 16] = _cols(inp["kv_norm_g"], 16)
    vecs[:, V_KVLAT:V_KVLAT + 4] = _cols(inp["kv_latent_g"], 4)
    kg = np.asarray(inp["k_norm_g"], np.float32)
    vecs[:, V_KGN] = kg[:128]
    vecs[:64, V_KGR] = kg[128:]
    vecs[:64, V_KGRS] = np.concatenate([kg[160:192], kg[128:160]])
    invf = (10000.0 ** (-np.arange(0, 64, 2, dtype=np.float32) / 64)).astype(np.float32)
    vecs[:64, V_INVF] = np.concatenate([invf, invf])
    vecs[:32, V_SGN] = -1.0
    vecs[32:64, V_SGN] = 1.0
    for s in range(NSLOT):
        for g, w in enumerate(WIN):
            t = np.arange(16, dtype=np.float32)
            if SBS[r][s] == 0:
                c = 1.0 / np.minimum(t + 1, float(w))
            else:
                c = np.full(16, 1.0 / w, np.float32)
            vecs[:, V_CINV + (s * 4 + g) * 16:V_CINV + (s * 4 + g + 1) * 16] = c[None, :]
        for jt in range(NOFF[s]):
            valid = jt < 4 * SBS[r][s]
            vecs[:, V_ABIAS + NOFF_OFS[s] + jt] = 0.0 if valid else NEG
    return vecs


def _prep_common(inp):
    w = {}
    w["a_w_in"] = np.ascontiguousarray(inp["a_w_in"], np.float32).reshape(2 * 2048, 8192)
    w["a_w_g"] = np.ascontiguousarray(inp["a_w_group"], np.float32).reshape(8 * 1024, 1024)
    w["a_w_out"] = np.ascontiguousarray(inp["a_w_out"], np.float32).reshape(2 * 4096, 2048)
    wa = np.asarray(inp["kv_w_a"], np.float32)
    w["kv_w_a"] = np.ascontiguousarray(np.concatenate([wa, wa[:, 544:576], wa[:, 512:544]], axis=1))
    wb = np.asarray(inp["kv_w_b"], np.float32).reshape(512, NH, 2, 128)
    w["kv_w_b"] = np.ascontiguousarray(np.concatenate([wb[:, :, 0, :].reshape(512, 2048),
                                                       wb[:, :, 1, :].reshape(512, 2048)], axis=1))
    w["b_w_in"] = np.ascontiguousarray(inp["b_w_in"], np.float32).reshape(2 * 2048, 2560)
    wq = np.asarray(inp["b_w_q_b"], np.float32).reshape(2, 512, NH, 192)
    nope = wq[:, :, :, :128].reshape(2, 512, 2048)
    rope = wq[:, :, :, 128:].reshape(2, 512, 1024)
    ropes = np.concatenate([wq[:, :, :, 160:192], wq[:, :, :, 128:160]], axis=3).reshape(2, 512, 1024)
    w["b_w_qb"] = np.ascontiguousarray(np.concatenate([nope, rope, ropes], axis=2)).reshape(2 * 512, 4096)
    w["b_w_out"] = np.ascontiguousarray(inp["b_w_out"], np.float32).reshape(2 * 2048, 2048)
    return w


def _core_inputs(inp, c):
    b, r = divmod(c, 2)
    x = np.asarray(inp["x"], np.float32)[b]
    pos = np.asarray(inp["positions"], np.int32)[b]
    xin = np.zeros((NSLOT, TH, D), np.float32)
    posr = np.zeros((NSLOT * T,), np.int32)
    for s, sb in enumerate(SBS[r]):
        t0 = sb * T
        if sb > 0:
            xin[s] = x[t0 - HL:t0 + T]
        else:
            xin[s, HL:] = x[0:T]
        posr[s * T:(s + 1) * T] = pos[t0:t0 + T]
    return xin, np.ascontiguousarray(np.broadcast_to(posr[None, :], (64, NSLOT * T))), _build_vecs(inp, r)


FUSED = True


def kernel(**inp):
    wts = _prep_common(inp)
    ident = np.eye(128, dtype=np.float32)
    cores = list(range(8))
    per = [_core_inputs(inp, c) for c in cores]
    A_keys = ["a_w_in", "a_w_g", "a_w_out", "kv_w_a", "kv_w_b"]
    B_keys = ["b_w_in", "b_w_qb", "b_w_out"]
    if FUSED:
        nc = _prog("F")
        maps = []
        for c in cores:
            m = {"vecs": per[c][2], "pos": per[c][1], "ident": ident, "xin": per[c][0]}
            for k in A_keys + B_keys:
                m[k] = wts[k]
            maps.append(m)
        res = run_bass_kernel_spmd(nc, maps, core_ids=cores)
        outs = [r["out"] for r in res.results]
    else:
        ncA = _prog("A")
        maps = []
        for c in cores:
            m = {"vecs": per[c][2], "pos": per[c][1], "ident": ident, "xin": per[c][0]}
            for k in A_keys:
                m[k] = wts[k]
            maps.append(m)
        resA = run_bass_kernel_spmd(ncA, maps, core_ids=cores).results
        ncB = _prog("B")
        maps = []
        for c in cores:
            p = c - (c % 2)
            m = {"vecs": per[c][2], "pos": per[c][1], "ident": ident, "x2T": resA[c]["x2T"]}
            for s_ in range(NSLOT):
                for hh in range(2):
                    for nm in ("kT", "v"):
                        key = f"{nm}_loc_{s_}_{hh}"
                        m[key] = resA[c][key]
                        m[f"{nm}_all_{s_}_{hh}"] = np.concatenate([resA[p][key], resA[p + 1][key]], axis=0)
            for k in B_keys:
                m[k] = wts[k]
            maps.append(m)
        resB = run_bass_kernel_spmd(ncB, maps, core_ids=cores).results
        outs = [r["out"] for r in resB]
    out = np.zeros((4, 4096, D), np.float32)
    for c in cores:
        b, r = divmod(c, 2)
        for s, sb in enumerate(SBS[r]):
            out[b, sb * T:(sb + 1) * T] = outs[c][s * T:(s + 1) * T]
    return out
```

```python
import numpy as np
import ml_dtypes
from contextlib import ExitStack
import concourse.bass as bass
import concourse.mybir as mybir
from concourse.bass_utils import run_bass_kernel_spmd

F32 = mybir.dt.float32
BF16 = mybir.dt.bfloat16
I32 = mybir.dt.int32
AF = mybir.ActivationFunctionType
ALU = mybir.AluOpType

D = 2048
DC = 16
T = 512
HL = 32
TH = T + HL
PADW = 16 + TH
NSLOT = 4
SBS = ([0, 3, 4, 7], [1, 2, 5, 6])
NOFF = [4, 12, 20, 28]
NOFF_OFS = [0, 4, 16, 36]
EPS = 1e-6
NH = 16
WIN = (2, 4, 8, 16)
SM_SCALE = 192 ** -0.5
NEG = -30000.0
WSLOT = 8192
NWSLOT = 3
DBG_SLOTS = [0, 1, 2, 3]
DBG_LAYERS = 2
DBG = False

V_ANORM = 0
V_ASCALE = 32
V_KVNORM = 96
V_KVLAT = 112
V_KGN = 116
V_KGR = 117
V_KGRS = 118
V_BNORM = 119
V_BQLAT = 151
V_QGN = 159
V_QGR = 161
V_QGRS = 163
V_INVF = 165
V_SGN = 166
V_CINV = 168
V_ABIAS = 424
NV = 512


class Ev:
    __slots__ = ("sem", "key", "val")

    def __init__(self, sem, key, val):
        self.sem, self.key, self.val = sem, key, val


class Buf:
    __slots__ = ("name", "writers", "readers")

    def __init__(self, name):
        self.name = name
        self.writers = []
        self.readers = {}


class Eng:
    def __init__(self, sched, name, handle, sem):
        self.S = sched
        self.name = name
        self.h = handle
        self.sem = sem
        self.cnt = 0
        self.known = {}
        self.thunks = []

    def _wait(self, ev):
        if ev is None:
            return
        if self.name == "pe" and ev.key == "pe":
            return
        if self.known.get(ev.key, 0) >= ev.val:
            return
        self.known[ev.key] = ev.val
        h, sem, val = self.h, ev.sem, ev.val
        self.thunks.append(lambda: h.wait_ge(sem, val))

    def deps(self, reads, writes):
        for b in reads:
            for e in b.writers:
                self._wait(e)
        for b in writes:
            for e in b.writers:
                self._wait(e)
            for e in b.readers.values():
                self._wait(e)

    def mark(self, ev, reads, writes):
        for b in reads:
            old = b.readers.get(ev.key)
            if old is None or old.val < ev.val:
                b.readers[ev.key] = ev
        for b in writes:
            b.writers = [ev]
            b.readers = {}

    def op(self, fn, reads=(), writes=()):
        self.deps(reads, writes)
        self.cnt += 1
        sem, val = self.sem, self.cnt
        self.thunks.append(lambda: fn(self.h).then_inc(sem, 1))
        ev = Ev(sem, self.name, val)
        self.mark(ev, reads, writes)
        return ev

    def group(self, fns, reads=(), writes=()):
        self.deps(reads, writes)
        self.cnt += 1
        sem, val = self.sem, self.cnt
        h = self.h
        for f in fns[:-1]:
            self.thunks.append(lambda f=f: f(h))
        last = fns[-1]
        self.thunks.append(lambda: last(h).then_inc(sem, 1))
        ev = Ev(sem, self.name, val)
        self.mark(ev, reads, writes)
        return ev

    def dma(self, semname, out, in_, reads=(), writes=()):
        self.deps(reads, writes)
        sem, val = self.S.dma_sem(semname)
        h = self.h
        self.thunks.append(lambda: h.dma_start(out=out, in_=in_).then_inc(sem, 16))
        ev = Ev(sem, "dma:" + semname, val)
        self.mark(ev, reads, writes)
        return ev


class Sched:
    def __init__(self, nc, es):
        self.nc = nc
        self.es = es
        self.dsems = {}
        mk = lambda n, h: Eng(self, n, h, es.enter_context(nc.semaphore("s_" + n)))
        self.pe = mk("pe", nc.tensor)
        self.act = mk("act", nc.scalar)
        self.dve = mk("dve", nc.vector)
        self.pool = mk("pool", nc.gpsimd)
        self.sp = mk("sp", nc.sync)
        self.all_ev = []

    def dma_sem(self, name):
        if name not in self.dsems:
            self.dsems[name] = [self.es.enter_context(self.nc.semaphore("d_" + name)), 0]
        ent = self.dsems[name]
        ent[1] += 16
        return ent[0], ent[1]

    def barrier(self):
        engs = (self.pe, self.act, self.dve, self.pool, self.sp)
        for x in engs:
            for y in engs:
                if y is not x and y.cnt:
                    x._wait(Ev(y.sem, y.name, y.cnt))
            for name, (sem, val) in self.dsems.items():
                if val:
                    x._wait(Ev(sem, "dma:" + name, val))

    def emit(self):
        nc = self.nc
        lists = {}
        for e in (self.pe, self.act, self.dve, self.pool, self.sp):
            lists[e.name] = e.thunks
            e.thunks = []
        with nc.Block() as block:
            @block.tensor
            def _(e):
                for t in lists["pe"]:
                    t()

            @block.scalar
            def _(e):
                for t in lists["act"]:
                    t()

            @block.vector
            def _(e):
                for t in lists["dve"]:
                    t()

            @block.gpsimd
            def _(e):
                for t in lists["pool"]:
                    t()

            @block.sync
            def _(e):
                for t in lists["sp"]:
                    t()


class Prog:
    def __init__(self, mode):
        self.mode = mode
        self.nc = bass.Bass("TRN2", target_bir_lowering=False)
        self.es = ExitStack()
        self.bufs = {}

    def dram_in(self, name, shape, dt):
        return self.nc.dram_tensor(name, list(shape), dt, kind="ExternalInput").ap()

    def dram_out(self, name, shape, dt):
        return self.nc.dram_tensor(name, list(shape), dt, kind="ExternalOutput").ap()

    def dram_int(self, name, shape, dt):
        return self.nc.dram_tensor(name, list(shape), dt, kind="Internal").ap()

    def sb(self, name, shape, dt):
        self._sbn = getattr(self, "_sbn", 0) + 1
        return self.es.enter_context(self.nc.sbuf_tensor(f"sb{self._sbn}_{name}", list(shape), dt))

    def B(self, name):
        if name not in self.bufs:
            self.bufs[name] = Buf(name)
        return self.bufs[name]

    def w_init(self):
        self.wplan = []
        self.wissued = 0
        self.wused = 0
        self.wbase = 0
        self.wend = 0
        self.nw = 0
        self.wheld = set()

    def w_alloc(self, nw, end):
        assert self.wissued == self.wused
        self.nw = nw
        self.wbase = self.wused
        self.wend = end
        self.wslots = [self.sb(f"wslot{i}", [128, WSLOT], BF16) for i in range(nw)]

    def w_issue_upto(self, n):
        S = self.S
        while self.wissued < min(n, self.wend):
            i = self.wissued
            if (i - self.nw) in self.wheld:
                break
            k = (i - self.wbase) % self.nw
            src, K, ncols = self.wplan[i]
            dst = self.wslots[k][:, 0:K * ncols].rearrange("p (k n) -> p k n", n=ncols)
            S.pool.dma(f"w{k}", dst, src, reads=(), writes=(self.B(f"wslot{k}"),))
            self.wissued += 1

    def w_get(self, src, K, ncols, hold=False):
        i = self.wused
        if hold:
            self.wheld.add(i)
        assert i < self.wend, "weight plan exhausted"
        psrc, pK, pn = self.wplan[i]
        assert (pK, pn) == (K, ncols) and str(psrc) == str(src), f"weight plan mismatch at {i}"
        self.w_issue_upto(i + self.nw)
        self.wused += 1
        k = (i - self.wbase) % self.nw
        return self.wslots[k][:, 0:K * ncols].rearrange("p (k n) -> p k n", n=ncols), self.B(f"wslot{k}")

    @staticmethod
    def wblk(W2d, r0, K, c0, ncols):
        return W2d[r0:r0 + K * 128, c0:c0 + ncols].rearrange("(k p) n -> p k n", p=128)

    def mm_group(self, out, pairs, reads, writes):
        n = len(pairs)
        fns = []
        for i, (l, r) in enumerate(pairs):
            fns.append(lambda h, l=l, r=r, i=i: h.matmul(out, l, r, start=(i == 0), stop=(i == n - 1)))
        return self.S.pe.group(fns, reads=reads, writes=writes)

    def rms_stats(self, src_fn, nchunk, ncols_total, col_ranges, g_cols, dst_fn, src_bufs, dst_bufs, inv_n,
                  tag):
        S = self.S
        sq = self.sqb
        sq_views = []
        for c in range(nchunk):
            sqt = sq[c % len(sq)]
            sqB = self.B(f"sqb{c % len(sq)}")
            src = src_fn(c)
            dst = sqt[:, 0:ncols_total]
            if c % 2 == 0:
                S.act.op(lambda h, src=src, dst=dst: h.activation(out=dst, in_=src, func=AF.Square),
                         reads=(src_bufs[c],), writes=(sqB,))
            else:
                S.dve.op(lambda h, src=src, dst=dst: h.tensor_tensor(out=dst, in0=src, in1=src, op=ALU.mult),
                         reads=(src_bufs[c],), writes=(sqB,))
            for (c0, c1, ps, psB) in col_ranges:
                l = self.ones[:, :]
                r = sqt[:, c0:c1]
                S.pe.group([lambda h, ps=ps, l=l, r=r, c=c: h.matmul(ps, l, r, start=(c == 0),
                                                                  stop=(c == nchunk - 1))],
                           reads=(sqB,), writes=(psB,))
        rstdB = self.B("rstd")
        for (c0, c1, ps, psB) in col_ranges:
            sd = self.sd[:, c0:c1]
            S.act.op(lambda h, ps=ps, sd=sd: h.activation(out=sd, in_=ps, func=AF.Ln, scale=inv_n,
                                                        bias=self.epsc[:, 0:1]),
                     reads=(psB,), writes=(self.B("sd"),))
            rs = self.rstd[:, c0:c1]
            S.act.op(lambda h, sd=sd, rs=rs: h.activation(out=rs, in_=sd, func=AF.Exp, scale=-0.5),
                     reads=(self.B("sd"),), writes=(rstdB,))
        for c in range(nchunk):
            src = src_fn(c)
            dst = dst_fn(c)
            g = self.vecs[:, g_cols + c:g_cols + c + 1]
            rs = self.rstd[:, 0:ncols_total]
            S.dve.op(lambda h, src=src, dst=dst, g=g, rs=rs: h.scalar_tensor_tensor(
                out=dst, in0=src, scalar=g, in1=rs, op0=ALU.mult, op1=ALU.mult),
                reads=(src_bufs[c], rstdB), writes=(dst_bufs[c],))

    def build(self):
        nc, es = self.nc, self.es
        mode = self.mode
        doA = mode in ("A", "F")
        doB = mode in ("B", "F")
        with es:
            self.S = S = Sched(nc, es)
            self.es_outer = es
            vecs_d = self.dram_in("vecs", [128, NV], F32)
            pos_d = self.dram_in("pos", [64, NSLOT * T], I32)
            ident_d = self.dram_in("ident", [128, 128], F32)
            if doA:
                xin = self.dram_in("xin", [NSLOT, TH, D], F32)
                a_w_in = self.dram_in("a_w_in", [2 * 2048, 8192], F32)
                a_w_g = self.dram_in("a_w_g", [8 * 1024, 1024], F32)
                a_w_out = self.dram_in("a_w_out", [2 * 4096, 2048], F32)
                kv_w_a = self.dram_in("kv_w_a", [2048, 640], F32)
                kv_w_b = self.dram_in("kv_w_b", [512, 4096], F32)
            if doB:
                b_w_in = self.dram_in("b_w_in", [2 * 2048, 2560], F32)
                b_w_qb = self.dram_in("b_w_qb", [2 * 512, 4096], F32)
                b_w_out = self.dram_in("b_w_out", [2 * 2048, 2048], F32)
                out_d = self.dram_out("out", [NSLOT * T, D], F32)
                if DBG:
                    self.dbg_bf = self.dram_out("dbg_bf", [128, 6, T], BF16)
                    self.dbg_f = self.dram_out("dbg_f", [128, 4, T], F32)
            KS, VS = [NH // 2 * 192, T], [NH // 2 * 128, T]
            KS2, VS2 = [NH * 192, T], [NH * 128, T]
            sh = [(s_, hh) for s_ in range(NSLOT) for hh in range(2)]
            if mode == "A":
                x2T = self.dram_out("x2T", [128, DC, NSLOT * T], F32)
                kT_loc = {k: self.dram_out(f"kT_loc_{k[0]}_{k[1]}", KS, BF16) for k in sh}
                v_loc = {k: self.dram_out(f"v_loc_{k[0]}_{k[1]}", VS, BF16) for k in sh}
            elif mode == "B":
                x2T = self.dram_in("x2T", [128, DC, NSLOT * T], F32)
                kT_loc = {k: self.dram_in(f"kT_loc_{k[0]}_{k[1]}", KS, BF16) for k in sh}
                v_loc = {k: self.dram_in(f"v_loc_{k[0]}_{k[1]}", VS, BF16) for k in sh}
                kT_all = {k: self.dram_in(f"kT_all_{k[0]}_{k[1]}", KS2, BF16) for k in sh}
                v_all = {k: self.dram_in(f"v_all_{k[0]}_{k[1]}", VS2, BF16) for k in sh}
            else:
                x2T = self.dram_int("x2T", [128, DC, NSLOT * T], F32)
                kT_loc = {k: self.dram_int(f"kT_loc_{k[0]}_{k[1]}", KS, BF16) for k in sh}
                v_loc = {k: self.dram_int(f"v_loc_{k[0]}_{k[1]}", VS, BF16) for k in sh}
                kT_all = {k: self.dram_int(f"kT_all_{k[0]}_{k[1]}", KS2, BF16) for k in sh}
                v_all = {k: self.dram_int(f"v_all_{k[0]}_{k[1]}", VS2, BF16) for k in sh}
            self.kT_loc, self.v_loc = kT_loc, v_loc
            if doB:
                self.kT_all, self.v_all = kT_all, v_all

            self.vecs = self.sb("vecs", [128, NV], F32)
            self.ident = self.sb("ident", [128, 128], F32)
            self.ones = self.sb("ones", [128, 128], BF16)
            self.epsc = self.sb("epsc", [128, 1], F32)
            self.C64 = self.sb("C64", [64, NSLOT * T], F32)
            self.S64 = self.sb("S64", [64, NSLOT * T], F32)
            self.sqb = [self.sb(f"sqb{i}", [128, TH], BF16) for i in range(4)]
            self.sd = self.sb("sd", [128, TH], F32)
            self.rstd = self.sb("rstd", [128, TH], F32)
            self.xT = self.sb("xT", [128, DC, TH], F32)
            self.h = self.sb("h", [128, DC, TH], BF16)
            self.ps = [es.enter_context(nc.psum_tensor(f"ps{i}", [128, 512], F32)) for i in range(8)]
            self.psB = [self.B(f"ps{i}") for i in range(8)]
            self.w_init()

            S.sp.dma("c0", self.vecs[:, :], vecs_d, writes=(self.B("vecs"),))
            S.sp.dma("cid", self.ident[:, :], ident_d, writes=(self.B("ident"),))
            S.dve.op(lambda h: h.memset(self.ones[:, :], 1.0), writes=(self.B("ones"),))
            S.dve.op(lambda h: h.memset(self.epsc[:, :], EPS), writes=(self.B("epsc"),))
            self.rope_tables(pos_d)

            if doA:
                for s in range(NSLOT):
                    for l in range(2):
                        self.plan_A_layer(a_w_in, a_w_g, a_w_out, l)
                    self.plan_KV(kv_w_a, kv_w_b)
            self.wplanA_end = len(self.wplan)
            if doB:
                for s in DBG_SLOTS:
                    for j in range(DBG_LAYERS):
                        self.plan_B_layer(b_w_in, b_w_qb, b_w_out, j)

            if doA:
                with ExitStack() as esA:
                    es_saved, self.es = self.es, esA
                    self.alloc_A()
                    for s in range(NSLOT):
                        self.load_x_tile(xin, s)
                        for l in range(2):
                            self.A_layer(a_w_in, a_w_g, a_w_out, l, s)
                        self.KV_tile(kv_w_a, kv_w_b, s)
                        self.store_x2(x2T, s)
                        if mode == "F":
                            self.exchange(s)
                    if mode == "F":
                        S.barrier()
                        S.emit()
                    self.es = es_saved
                    if mode == "A":
                        self.finish()
            if doB:
                with ExitStack() as esB:
                    es_saved, self.es = self.es, esB
                    self.alloc_B()
                    for s in DBG_SLOTS:
                        self.load_x2(x2T, s)
                        for j in range(DBG_LAYERS):
                            self.B_layer(b_w_in, b_w_qb, b_w_out, j, s)
                        self.store_out(out_d, s)
                    self.es = es_saved
                    self.finish()
        return nc

    def finish(self):
        S = self.S
        assert self.wused == len(self.wplan), (self.wused, len(self.wplan))
        S.barrier()
        S.emit()

    def rope_tables(self, pos_d):
        S = self.S
        NT = NSLOT * T
        with ExitStack() as es2:
            posi = es2.enter_context(self.nc.sbuf_tensor("posi", [64, NT], I32))
            ang = es2.enter_context(self.nc.sbuf_tensor("ang", [64, NT], F32))
            t1 = es2.enter_context(self.nc.sbuf_tensor("rt1", [64, NT], F32))
            ti = es2.enter_context(self.nc.sbuf_tensor("rti", [64, NT], I32))
            Bp, Ba, Bt, Bi = self.B("posi"), self.B("ang"), self.B("rt1"), self.B("rti")
            Bv = self.B("vecs")
            S.sp.dma("c1", posi[:, :], pos_d, writes=(Bp,))
            S.dve.op(lambda h: h.tensor_copy(out=ang[:, :], in_=posi[:, :]), reads=(Bp,), writes=(Ba,))
            invf = self.vecs[0:64, V_INVF:V_INVF + 1]
            S.dve.op(lambda h: h.tensor_scalar(out=ang[:, :], in0=ang[:, :], scalar1=invf, scalar2=None,
                                               op0=ALU.mult), reads=(Ba, Bv), writes=(Ba,))
            C1 = 6.28125
            C2 = 2.0 * np.pi - C1
            for which, dst, Bd in (("sin", self.S64, self.B("S64")), ("cos", self.C64, self.B("C64"))):
                src = ang
                if which == "cos":
                    S.dve.op(lambda h: h.tensor_scalar(out=t1[:, :], in0=ang[:, :], scalar1=float(np.pi / 2),
                                                       scalar2=None, op0=ALU.add), reads=(Ba,), writes=(Bt,))
                    src = t1
                    Bs = Bt
                else:
                    Bs = Ba
                S.dve.op(lambda h, src=src, dst=dst: h.tensor_scalar(out=dst[:, :], in0=src[:, :],
                                                            scalar1=float(1.0 / (2 * np.pi)), scalar2=None,
                                                            op0=ALU.mult), reads=(Bs,), writes=(Bd,))
                S.dve.op(lambda h, dst=dst: h.tensor_copy(out=ti[:, :], in_=dst[:, :]), reads=(Bd,), writes=(Bi,))
                S.dve.op(lambda h, dst=dst: h.tensor_copy(out=dst[:, :], in_=ti[:, :]), reads=(Bi,), writes=(Bd,))
                S.dve.op(lambda h, src=src, dst=dst: h.scalar_tensor_tensor(out=t1[:, :], in0=dst[:, :], scalar=-C1,
                                                                   in1=src[:, :], op0=ALU.mult, op1=ALU.add),
                         reads=(Bd, Bs), writes=(Bt,))
                S.dve.op(lambda h, dst=dst: h.scalar_tensor_tensor(out=t1[:, :], in0=dst[:, :], scalar=-C2,
                                                          in1=t1[:, :], op0=ALU.mult, op1=ALU.add),
                         reads=(Bd, Bt), writes=(Bt,))
                S.dve.op(lambda h: h.tensor_scalar(out=t1[:, :], in0=t1[:, :], scalar1=3.14159, scalar2=-3.14159,
                                                   op0=ALU.min, op1=ALU.max), reads=(Bt,), writes=(Bt,))
                if which == "sin":
                    sgn = self.vecs[0:64, V_SGN:V_SGN + 1]
                    S.act.op(lambda h, dst=dst: h.activation(out=dst[:, :], in_=t1[:, :], func=AF.Sin, scale=sgn),
                             reads=(Bt, Bv), writes=(Bd,))
                else:
                    S.act.op(lambda h, dst=dst: h.activation(out=dst[:, :], in_=t1[:, :], func=AF.Sin),
                             reads=(Bt,), writes=(Bd,))
            S.barrier()
            S.emit()

    def alloc_A(self):
        self.w_alloc(3, self.wplanA_end)
        self.xtok = [self.sb(f"xtok{i}", [128, D], F32) for i in range(2)]
        self.U = self.sb("U", [128, 8, PADW], F32)
        self.pa = self.sb("pa", [128, 2, PADW], F32)
        self.pb = self.sb("pb", [128, 2, PADW], F32)
        self.sg = self.sb("sg", [128, 8, TH], BF16)
        self.PL = self.sb("PL", [128, 8, TH], BF16)
        self.y = self.sb("y", [128, 8, TH], BF16)
        self.cf = self.U[:, 0:4, 0:T]
        self.krw = self.U[0:64, 4:8, 0:T]
        self.cn = self.PL[:, 0:4, 0:T]
        self.sqr = self.PL[0:64, 4, 0:T]
        self.kst = [self.sg[:, i, 0:T] for i in range(2)]
        self.krst = [self.sg[0:64, 2 + i, 0:T] for i in range(2)]
        self.vst = [self.y[:, 4 * i:4 * i + 4, 0:T] for i in range(2)]
        self.fsc = self.sb("fsc", [128, 4], F32)
        S = self.S
        S.dve.op(lambda h: h.memset(self.U[:, :, 0:16], 0.0), reads=(), writes=(self.B("U"),))

    def plan_A_layer(self, w_in, w_g, w_out, l):
        for g in range(4):
            for half in range(2):
                self.wplan.append((self.wblk(w_in, l * 2048, 16, g * 1024 + half * 512, 512), 16, 512))
            for half in range(2):
                self.wplan.append((self.wblk(w_in, l * 2048, 16, 4096 + g * 1024 + half * 512, 512), 16, 512))
            self.wplan.append((self.wblk(w_g, (l * 4 + g) * 1024, 8, 0, 1024), 8, 1024))
            for q in range(2):
                self.wplan.append((self.wblk(w_out, l * 4096 + g * 1024, 8, q * 1024, 1024), 8, 1024))

    def load_x_tile(self, xin, s):
        S = self.S
        Bid = self.B("ident")
        subs = [(0, HL, 0)] + [(HL + i * 128, 128, HL + i * 128) for i in range(4)]
        pre = getattr(self, "x_prefetched", -1)
        for si, (r0, nr, c0) in enumerate(subs):
            xt = self.xtok[si % 2]
            Bx = self.B(f"xtok{si % 2}")
            if not (pre == s and si < 2):
                S.sp.dma(f"xtok{si % 2}", xt[0:nr, :], xin[s, r0:r0 + nr, :], writes=(Bx,))
            for cg in range(4):
                pi = 6 + (cg % 2)
                ps, psB = self.ps[pi], self.psB[pi]
                fns = []
                for j in range(4):
                    c = cg * 4 + j
                    fns.append(lambda h, ps=ps, xt=xt, c=c, j=j, nr=nr: h.transpose(
                        out=ps[:, j * 128:j * 128 + nr], in_=xt[0:nr, c * 128:(c + 1) * 128],
                        identity=self.ident[0:nr, 0:nr]))
                S.pe.group(fns, reads=(Bx, Bid), writes=(psB,))
                src = ps[:, :].rearrange("p (j n) -> p j n", n=128)[:, :, 0:nr]
                dst = self.xT[:, cg * 4:cg * 4 + 4, c0:c0 + nr]
                wb = tuple(self.B(f"xT{c}") for c in range(cg * 4, cg * 4 + 4))
                eng = S.act if cg % 2 == 0 else S.dve
                if eng is S.act:
                    eng.op(lambda h, src=src, dst=dst: h.activation(out=dst, in_=src, func=AF.Copy),
                           reads=(psB,), writes=wb)
                else:
                    eng.op(lambda h, src=src, dst=dst: h.tensor_copy(out=dst, in_=src), reads=(psB,), writes=wb)
        if s + 1 < NSLOT:
            for si, (r0, nr, c0) in enumerate(subs[:2]):
                S.sp.dma(f"xtok{si % 2}", self.xtok[si % 2][0:nr, :], xin[s + 1, r0:r0 + nr, :],
                         writes=(self.B(f"xtok{si % 2}"),))
            self.x_prefetched = s + 1

    def A_layer(self, w_in, w_g, w_out, l, s):
        S = self.S
        halo_full = (l == 0)
        xB = [self.B(f"xT{c}") for c in range(DC)]
        hB = [self.B(f"h{c}") for c in range(DC)]
        Bv = self.B("vecs")
        self.rms_stats(lambda c: self.xT[:, c, :], DC, TH,
                       [(HL, TH, self.ps[5][:, :], self.psB[5]), (0, HL, self.ps[4][:, 0:HL], self.psB[4])],
                       V_ANORM + l * 16, lambda c: self.h[:, c, :], xB, hB, 1.0 / D, "a")
        mi = [0]

        def next_main():
            i = mi[0] % 4
            mi[0] += 1
            return self.ps[i], self.psB[i]

        hi = [0]

        def next_halo():
            i = hi[0] % 16
            hi[0] += 1
            return self.ps[4][:, i * HL:(i + 1) * HL], self.psB[4]

        for g in range(4):
            w = WIN[g]
            UB = self.B("U")
            sgB = self.B("sg")
            PLB = self.B("PL")
            yB = self.B("y")
            for half in range(2):
                blk, bB = self.w_get(self.wblk(w_in, l * 2048, 16, g * 1024 + half * 512, 512), 16, 512)
                for m in range(4):
                    j = half * 4 + m
                    ps, psB = next_main()
                    self.mm_group(ps[:, :], [(blk[:, k, m * 128:(m + 1) * 128], self.h[:, k, HL:TH])
                                             for k in range(DC)], reads=tuple(hB) + (bB,), writes=(psB,))
                    S.act.op(lambda h, ps=ps, j=j: h.activation(out=self.U[:, j, 16 + HL:PADW], in_=ps[:, :],
                                                               func=AF.Copy), reads=(psB,), writes=(UB,))
                    ph, phB = next_halo()
                    self.mm_group(ph, [(blk[:, k, m * 128:(m + 1) * 128], self.h[:, k, 0:HL])
                                       for k in range(DC)], reads=tuple(hB) + (bB,), writes=(phB,))
                    S.act.op(lambda h, ph=ph, j=j: h.activation(out=self.U[:, j, 16:16 + HL], in_=ph,
                                                               func=AF.Copy), reads=(phB,), writes=(UB,))
            paB, pbB = self.B("pa"), self.B("pb")
            for jj in range(4):
                Uv = self.U[:, 2 * jj:2 * jj + 2, :]
                A_, B_ = self.pa, self.pb
                S.dve.op(lambda h, Uv=Uv: h.tensor_tensor(out=A_[:, :, 1:PADW], in0=Uv[:, :, 1:PADW],
                                                          in1=Uv[:, :, 0:PADW - 1], op=ALU.add),
                         reads=(UB,), writes=(paB,))
                cur, curB, oth, othB = A_, paB, B_, pbB
                sh = 2
                lo = 1
                while sh < w:
                    lo2 = lo + sh
                    S.dve.op(lambda h, cur=cur, oth=oth, lo2=lo2, sh=sh: h.tensor_tensor(
                        out=oth[:, :, lo2:PADW], in0=cur[:, :, lo2:PADW], in1=cur[:, :, lo2 - sh:PADW - sh],
                        op=ALU.add), reads=(curB,), writes=(othB,))
                    cur, curB, oth, othB = oth, othB, cur, curB
                    lo = lo2
                    sh *= 2
                S.dve.op(lambda h, cur=cur, Uv=Uv, jj=jj, w=w: h.scalar_tensor_tensor(
                    out=self.PL[:, 2 * jj:2 * jj + 2, :], in0=cur[:, :, 16:PADW], scalar=1.0 / w,
                    in1=Uv[:, :, 16:PADW], op0=ALU.mult, op1=ALU.subtract), reads=(curB, UB), writes=(PLB,))
                for q in range(2):
                    ci = V_CINV + (s * 4 + g) * 16
                    cinv = self.vecs[:, ci:ci + 16]
                    S.dve.op(lambda h, cur=cur, q=q, cinv=cinv, oth=oth: h.tensor_tensor(
                        out=oth[:, q, 0:16], in0=cur[:, q, 16 + HL:16 + HL + 16], in1=cinv, op=ALU.mult),
                        reads=(curB, Bv), writes=(othB,))
                    S.dve.op(lambda h, q=q, jj=jj, Uv=Uv, oth=oth: h.tensor_tensor(
                        out=self.PL[:, 2 * jj + q, HL:HL + 16], in0=oth[:, q, 0:16],
                        in1=Uv[:, q, 16 + HL:16 + HL + 16], op=ALU.subtract), reads=(othB, UB), writes=(PLB,))
            for half in range(2):
                blk, bB = self.w_get(self.wblk(w_in, l * 2048, 16, 4096 + g * 1024 + half * 512, 512), 16, 512)
                for m in range(4):
                    j = half * 4 + m
                    ps, psB = next_main()
                    self.mm_group(ps[:, :], [(blk[:, k, m * 128:(m + 1) * 128], self.h[:, k, HL:TH])
                                             for k in range(DC)], reads=tuple(hB) + (bB,), writes=(psB,))
                    S.act.op(lambda h, ps=ps, j=j: h.activation(out=self.sg[:, j, HL:TH], in_=ps[:, :],
                                                               func=AF.Silu), reads=(psB,), writes=(sgB,))
                    if halo_full:
                        ph, phB = next_halo()
                        self.mm_group(ph, [(blk[:, k, m * 128:(m + 1) * 128], self.h[:, k, 0:HL])
                                           for k in range(DC)], reads=tuple(hB) + (bB,), writes=(phB,))
                        S.act.op(lambda h, ph=ph, j=j: h.activation(out=self.sg[:, j, 0:HL], in_=ph,
                                                                   func=AF.Silu), reads=(phB,), writes=(sgB,))
            for half in range(1):
                blk, bB = self.w_get(self.wblk(w_g, (l * 4 + g) * 1024, 8, 0, 1024), 8, 1024)
                for m in range(8):
                    j = m
                    sc = self.vecs[:, V_ASCALE + l * 32 + g * 8 + j:V_ASCALE + l * 32 + g * 8 + j + 1]
                    ps, psB = next_main()
                    self.mm_group(ps[:, :], [(blk[:, k, m * 128:(m + 1) * 128], self.PL[:, k, HL:TH])
                                             for k in range(8)], reads=(PLB, bB), writes=(psB,))
                    S.dve.op(lambda h, ps=ps, j=j, sc=sc: h.scalar_tensor_tensor(
                        out=self.y[:, j, HL:TH], in0=ps[:, :], scalar=sc, in1=self.sg[:, j, HL:TH],
                        op0=ALU.mult, op1=ALU.mult), reads=(psB, sgB, Bv), writes=(yB,))
                    if halo_full:
                        ph, phB = next_halo()
                        self.mm_group(ph, [(blk[:, k, m * 128:(m + 1) * 128], self.PL[:, k, 0:HL])
                                           for k in range(8)], reads=(PLB, bB), writes=(phB,))
                        S.dve.op(lambda h, ph=ph, j=j, sc=sc: h.scalar_tensor_tensor(
                            out=self.y[:, j, 0:HL], in0=ph, scalar=sc, in1=self.sg[:, j, 0:HL],
                            op0=ALU.mult, op1=ALU.mult), reads=(phB, sgB, Bv), writes=(yB,))
            for q in range(2):
                blk, bB = self.w_get(self.wblk(w_out, l * 4096 + g * 1024, 8, q * 1024, 1024), 8, 1024)
                for m in range(8):
                    oc = q * 8 + m
                    ps, psB = next_main()
                    self.mm_group(ps[:, :], [(blk[:, k, m * 128:(m + 1) * 128], self.y[:, k, HL:TH])
                                             for k in range(8)], reads=(yB, bB), writes=(psB,))
                    S.dve.op(lambda h, ps=ps, oc=oc: h.tensor_tensor(
                        out=self.xT[:, oc, HL:TH], in0=self.xT[:, oc, HL:TH], in1=ps[:, :], op=ALU.add),
                        reads=(psB,), writes=(xB[oc],))
                    if halo_full:
                        ph, phB = next_halo()
                        self.mm_group(ph, [(blk[:, k, m * 128:(m + 1) * 128], self.y[:, k, 0:HL])
                                           for k in range(8)], reads=(yB, bB), writes=(phB,))
                        S.dve.op(lambda h, ph=ph, oc=oc: h.tensor_tensor(
                            out=self.xT[:, oc, 0:HL], in0=self.xT[:, oc, 0:HL], in1=ph, op=ALU.add),
                            reads=(phB,), writes=(xB[oc],))

    def plan_KV(self, kv_w_a, kv_w_b):
        self.wplan.append((self.wblk(kv_w_a, 0, 16, 0, 512), 16, 512))
        self.wplan.append((self.wblk(kv_w_a, 0, 16, 512, 128), 16, 128))
        self.wplan.append((self.wblk(kv_w_b, 0, 4, 0, 2048), 4, 2048))
        self.wplan.append((self.wblk(kv_w_b, 0, 4, 2048, 2048), 4, 2048))

    def KV_tile(self, kv_w_a, kv_w_b, s):
        S = self.S
        xB = [self.B(f"xT{c}") for c in range(DC)]
        hB = [self.B(f"h{c}") for c in range(DC)]
        Bv = self.B("vecs")
        self.rms_stats(lambda c: self.xT[:, c, :], DC, TH,
                       [(HL, TH, self.ps[5][:, :], self.psB[5]), (0, HL, self.ps[4][:, 0:HL], self.psB[4])],
                       V_KVNORM, lambda c: self.h[:, c, :], xB, hB, 1.0 / D, "kv")
        cfB = [self.B(f"cf{j}") for j in range(4)]
        cnB = [self.B(f"cn{j}") for j in range(4)]
        kv_alias = tuple(cfB) + tuple(cnB) + tuple(self.B(n) for n in (
            "krw", "sqr", "kst0", "kst1", "krst0", "krst1", "vst0", "vst1"))
        pool_bufs = tuple(self.B(n) for n in ("U", "PL", "sg", "y"))
        S.dve.op(lambda h: h.memset(self.fsc[:, 0:1], 0.0), writes=kv_alias + pool_bufs + (self.B("fsc"),))
        blk, bB = self.w_get(self.wblk(kv_w_a, 0, 16, 0, 512), 16, 512)
        for m in range(4):
            ps, psB = self.ps[m % 4], self.psB[m % 4]
            self.mm_group(ps[:, :], [(blk[:, k, m * 128:(m + 1) * 128], self.h[:, k, HL:TH]) for k in range(DC)],
                          reads=tuple(hB) + (bB,), writes=(psB,))
            S.act.op(lambda h, ps=ps, m=m: h.activation(out=self.cf[:, m, :], in_=ps[:, :], func=AF.Copy),
                     reads=(psB,), writes=(cfB[m],))
        blk, bB = self.w_get(self.wblk(kv_w_a, 0, 16, 512, 128), 16, 128)
        krP, krB = self.ps[0], self.psB[0]
        krsP, krsB = self.ps[1], self.psB[1]
        self.mm_group(krP[0:64, :], [(blk[:, k, 0:64], self.h[:, k, HL:TH]) for k in range(DC)],
                      reads=tuple(hB) + (bB,), writes=(krB,))
        self.mm_group(krsP[0:64, :], [(blk[:, k, 64:128], self.h[:, k, HL:TH]) for k in range(DC)],
                      reads=tuple(hB) + (bB,), writes=(krsB,))
        self.rms_stats(lambda c: self.cf[:, c, :], 4, T, [(0, T, self.ps[5][:, :], self.psB[5])],
                       V_KVLAT, lambda c: self.cn[:, c, :], cfB, cnB, 1.0 / 512, "c")
        tsl = slice(s * T, (s + 1) * T)
        GC, GS, KR, TMP = (self.krw[:, i, :] for i in range(4))
        krwB = self.B("krw")
        BC, BS = self.B("C64"), self.B("S64")
        S.dve.op(lambda h: h.tensor_scalar(out=GC, in0=self.C64[:, tsl], scalar1=self.vecs[0:64, V_KGR:V_KGR + 1],
                                           scalar2=None, op0=ALU.mult), reads=(BC, Bv), writes=(krwB,))
        S.dve.op(lambda h: h.tensor_scalar(out=GS, in0=self.S64[:, tsl],
                                           scalar1=self.vecs[0:64, V_KGRS:V_KGRS + 1], scalar2=None,
                                           op0=ALU.mult), reads=(BS, Bv), writes=(krwB,))
        S.dve.op(lambda h: h.tensor_tensor(out=KR, in0=krP[0:64, :], in1=GC, op=ALU.mult),
                 reads=(krB, krwB), writes=(krwB,))
        S.dve.op(lambda h: h.tensor_tensor(out=TMP, in0=krsP[0:64, :], in1=GS, op=ALU.mult),
                 reads=(krsB, krwB), writes=(krwB,))
        S.dve.op(lambda h: h.tensor_tensor(out=KR, in0=KR, in1=TMP, op=ALU.add), reads=(krwB,), writes=(krwB,))
        sqrB = self.B("sqr")
        S.act.op(lambda h: h.activation(out=self.sqr, in_=krP[0:64, :], func=AF.Square),
                 reads=(krB,), writes=(sqrB,))
        blk, bB = self.w_get(self.wblk(kv_w_b, 0, 4, 0, 2048), 4, 2048, hold=True)
        vblk, vbB = self.w_get(self.wblk(kv_w_b, 0, 4, 2048, 2048), 4, 2048, hold=True)
        v4 = [self.v_loc[(s, hh)].rearrange("(h p) (kt d) -> p h kt d", p=128, d=128) for hh in range(2)]

        def v_group(ts, cb):
            vst, vstB = self.vst[ts % 2], self.B(f"vst{ts % 2}")
            ps, psB = self.ps[4], self.psB[4]
            self.mm_group(ps[:, :], [(self.cn[:, k, ts * 128:(ts + 1) * 128], vblk[:, k, cb * 512:(cb + 1) * 512])
                                     for k in range(4)], reads=tuple(cnB) + (vbB,), writes=(psB,))
            S.dve.op(lambda h, ps=ps, vst=vst, cb=cb: h.tensor_copy(out=vst[:, cb, :], in_=ps[:, :]),
                     reads=(psB,), writes=(vstB,))
            if cb == 3:
                for c2 in range(4):
                    S.sp.dma(f"vst{ts % 2}", v4[c2 // 2][:, (c2 % 2) * 4:(c2 % 2) * 4 + 4, ts, :],
                             vst[:, c2, :].rearrange("p (hh d) -> p hh d", d=128), reads=(vstB,))

        KNB = (2, 3, 5)

        def kn_mm(hd):
            pi = KNB[hd % 3]
            kn, knB = self.ps[pi], self.psB[pi]
            self.mm_group(kn[:, :], [(blk[:, k, hd * 128:(hd + 1) * 128], self.cn[:, k, :]) for k in range(4)],
                          reads=tuple(cnB) + (bB,), writes=(knB,))
            sqt = self.sqb[hd % 4]
            sqB_ = self.B(f"sqb{hd % 4}")
            S.act.op(lambda h, kn=kn, sqt=sqt: h.activation(out=sqt[:, 0:T], in_=kn[:, :], func=AF.Square),
                     reads=(knB,), writes=(sqB_,))

        kn_mm(0)
        for hd in range(NH):
            if hd + 1 < NH:
                kn_mm(hd + 1)
            v_group(hd // 4, hd % 4)
            pi = KNB[hd % 3]
            kn, knB = self.ps[pi], self.psB[pi]
            sqt = self.sqb[hd % 4]
            sqB_ = self.B(f"sqb{hd % 4}")
            si = 6 + (hd % 2)
            ss, ssB = self.ps[si], self.psB[si]
            self.mm_group(ss[:, :], [(self.ones[:, :], sqt[:, 0:T]), (self.ones[0:64, :], self.sqr)],
                          reads=(sqB_, sqrB, self.B("ones")), writes=(ssB,))
            sdB, rsB = self.B("sd"), self.B("rstd")
            S.act.op(lambda h, ss=ss: h.activation(out=self.sd[:, 0:T], in_=ss[:, :], func=AF.Ln,
                                                   scale=1.0 / 192, bias=self.epsc[:, 0:1]),
                     reads=(ssB,), writes=(sdB,))
            S.act.op(lambda h: h.activation(out=self.rstd[:, 0:T], in_=self.sd[:, 0:T], func=AF.Exp, scale=-0.5),
                     reads=(sdB,), writes=(rsB,))
            kst, kstB = self.kst[hd % 2], self.B(f"kst{hd % 2}")
            krst, krstB = self.krst[hd % 2], self.B(f"krst{hd % 2}")
            S.dve.op(lambda h, kn=kn, kst=kst: h.scalar_tensor_tensor(
                out=kst, in0=kn[:, :], scalar=self.vecs[:, V_KGN:V_KGN + 1], in1=self.rstd[:, 0:T],
                op0=ALU.mult, op1=ALU.mult), reads=(knB, rsB, Bv), writes=(kstB,))
            S.dve.op(lambda h, krst=krst: h.tensor_tensor(out=krst, in0=KR, in1=self.rstd[0:64, 0:T],
                                                          op=ALU.mult), reads=(krwB, rsB), writes=(krstB,))
            kd = self.kT_loc[(s, hd // 8)]
            r0 = (hd % 8) * 192
            S.sp.dma(f"kst{hd % 2}", kd[r0:r0 + 128, :], kst, reads=(kstB,))
            S.sp.dma(f"krst{hd % 2}", kd[r0 + 128:r0 + 192, :], krst, reads=(krstB,))
        self.wheld.clear()
        S.dve.op(lambda h: h.memset(self.fsc[:, 1:2], 0.0), writes=kv_alias + pool_bufs + (self.B("fsc"),))

    def store_x2(self, x2T, s):
        xB = [self.B(f"xT{c}") for c in range(DC)]
        self.S.sp.dma("x2st", x2T[:, :, s * T:(s + 1) * T], self.xT[:, :, HL:TH], reads=tuple(xB))

    def exchange(self, s):
        S = self.S
        for name, (sem, val) in S.dsems.items():
            if name.startswith(("kst", "krst", "vst")):
                S.pool._wait(Ev(sem, "dma:" + name, val))
        groups = [[0, 1], [2, 3], [4, 5], [6, 7]]
        if not hasattr(self, "ccsem"):
            self.ccsem = self.es_outer.enter_context(self.nc.semaphore("cc_sem"))
            self.ccn = 0
        ccsem = self.ccsem
        for hh in range(2):
            for src, dst, nm in ((self.kT_loc[(s, hh)], self.kT_all[(s, hh)], f"agk{s}{hh}"),
                                 (self.v_loc[(s, hh)], self.v_all[(s, hh)], f"agv{s}{hh}")):
                self.ccn += 1
                S.pool.thunks.append(lambda src=src, dst=dst: self.nc.gpsimd.collective_compute(
                    "AllGather", ALU.bypass, replica_groups=groups, ins=[src.opt()],
                    outs=[dst.opt()]).then_inc(ccsem))
                self.B(nm).writers = [Ev(ccsem, "cc", self.ccn)]

    def alloc_B(self):
        self.w_alloc(2, len(self.wplan))
        self.qf = self.sb("qf", [128, 4, T], F32)
        self.qln = self.sb("qln", [128, 4, T], BF16)
        self.sgB_ = self.sb("sgb", [128, NH, T], BF16)
        self.og = self.sgB_
        self.gq = self.sb("gq", [64, 4, T], F32)
        self.qnT = [self.sb(f"qnT{i}", [128, T], BF16) for i in range(2)]
        self.qrT = [self.sb(f"qrT{i}", [64, T], BF16) for i in range(2)]
        self.sqr2 = self.sb("sqr2", [64, T], BF16)
        self.rsq = self.sb("rsq", [128, T], F32)
        self.ot = self.sb("ot", [128, T], F32)
        self.KTn = [self.sb(f"KTn{i}", [128, 4096], BF16) for i in range(2)]
        self.KTr = [self.sb(f"KTr{i}", [64, 4096], BF16) for i in range(2)]
        self.Vh = [self.sb(f"Vh{i}", [128, 32, 128], BF16) for i in range(2)]
        self.PT = [self.sb(f"PT{i}", [128, T], BF16) for i in range(3)]
        self.xo = [self.qf[:, :, :].rearrange("p a b -> p (a b)")]
        self.kvcount = 0
        self.ptc = 0

    def plan_B_layer(self, w_in, w_qb, w_out, j):
        self.wplan.append((self.wblk(w_in, j * 2048, 16, 0, 512), 16, 512))
        for q in range(4):
            self.wplan.append((self.wblk(w_in, j * 2048, 16, 512 + q * 512, 512), 16, 512))
        self.wplan.append((self.wblk(w_qb, j * 512, 4, 0, 2048), 4, 2048))
        self.wplan.append((self.wblk(w_qb, j * 512, 4, 2048, 2048), 4, 2048))
        for q in range(4):
            self.wplan.append((self.wblk(w_out, j * 2048, 16, q * 512, 512), 16, 512))

    def load_x2(self, x2T, s):
        xB = [self.B(f"xT{c}") for c in range(DC)]
        reads = ()
        if self.mode == "F":
            reads = (self.B("x2dram"),)
        self.S.sp.dma("x2ld", self.xT[:, :, HL:TH], x2T[:, :, s * T:(s + 1) * T], reads=reads, writes=tuple(xB))

    def kv_load(self, s, hd):
        S = self.S
        i = self.kvcount % 2
        self.kvcount += 1
        KBn, KBr, VB = self.B(f"KTn{i}"), self.B(f"KTr{i}"), self.B(f"Vh{i}")
        sem = f"kv{i}"
        hh, h8 = divmod(hd, 8)
        for J in range(NOFF[s] // 4):
            r = 0 if J in SBS[0] else 1
            ls = SBS[r].index(J)
            rdk = (self.B(f"agk{ls}{hh}"),) if self.mode == "F" else ()
            rdv = (self.B(f"agv{ls}{hh}"),) if self.mode == "F" else ()
            kall = self.kT_all[(ls, hh)]
            vall = self.v_all[(ls, hh)].rearrange("(r h p) (kt d) -> r h p kt d", r=2, p=128, d=128)
            base = r * (NH // 2) * 192 + h8 * 192
            S.sp.dma(sem, self.KTn[i][:, J * T:(J + 1) * T], kall[base:base + 128, :], reads=rdk, writes=(KBn,))
            S.sp.dma(sem, self.KTr[i][:, J * T:(J + 1) * T], kall[base + 128:base + 192, :], reads=rdk,
                     writes=(KBr,))
            S.sp.dma(sem, self.Vh[i][:, J * 4:(J + 1) * 4, :], vall[r, h8, :, :, :], reads=rdv, writes=(VB,))
        J = NOFF[s] // 4
        kloc = self.kT_loc[(s, hh)]
        vloc = self.v_loc[(s, hh)].rearrange("(h p) (kt d) -> h p kt d", p=128, d=128)
        r0 = h8 * 192
        S.sp.dma(sem, self.KTn[i][:, J * T:(J + 1) * T], kloc[r0:r0 + 128, :], writes=(KBn,))
        S.sp.dma(sem, self.KTr[i][:, J * T:(J + 1) * T], kloc[r0 + 128:r0 + 192, :], writes=(KBr,))
        ev = S.sp.dma(sem, self.Vh[i][:, J * 4:(J + 1) * 4, :], vloc[h8, :, :, :], writes=(VB,))
        KBn.writers = [ev]
        KBr.writers = [ev]
        VB.writers = [ev]
        return i

    def B_layer(self, w_in, w_qb, w_out, j, s):
        S = self.S
        xB = [self.B(f"xT{c}") for c in range(DC)]
        hB = [self.B(f"h{c}") for c in range(DC)]
        Bv = self.B("vecs")
        M = slice(HL, TH)
        self.rms_stats(lambda c: self.xT[:, c, M], DC, T, [(0, T, self.ps[5][:, :], self.psB[5])],
                       V_BNORM + j * 16, lambda c: self.h[:, c, M], xB, hB, 1.0 / D, "b")
        kvi = self.kv_load(s, 0)
        qfB = [self.B(f"qf{m}") for m in range(4)]
        qlB = [self.B(f"qln{m}") for m in range(4)]
        blk, bB = self.w_get(self.wblk(w_in, j * 2048, 16, 0, 512), 16, 512)
        for m in range(4):
            ps, psB = self.ps[m], self.psB[m]
            self.mm_group(ps[:, :], [(blk[:, k, m * 128:(m + 1) * 128], self.h[:, k, M]) for k in range(DC)],
                          reads=tuple(hB) + (bB,), writes=(psB,))
            S.act.op(lambda h, ps=ps, m=m: h.activation(out=self.qf[:, m, :], in_=ps[:, :], func=AF.Copy),
                     reads=(psB,), writes=(qfB[m],))
        sgB = self.B("sgb")
        for q in range(4):
            blk, bB = self.w_get(self.wblk(w_in, j * 2048, 16, 512 + q * 512, 512), 16, 512)
            for m in range(4):
                hd = q * 4 + m
                ps, psB = self.ps[m], self.psB[m]
                self.mm_group(ps[:, :], [(blk[:, k, m * 128:(m + 1) * 128], self.h[:, k, M]) for k in range(DC)],
                              reads=tuple(hB) + (bB,), writes=(psB,))
                S.act.op(lambda h, ps=ps, hd=hd: h.activation(out=self.sgB_[:, hd, :], in_=ps[:, :], func=AF.Silu),
                         reads=(psB,), writes=(sgB,))
        self.rms_stats(lambda c: self.qf[:, c, :], 4, T, [(0, T, self.ps[5][:, :], self.psB[5])],
                       V_BQLAT + j * 4, lambda c: self.qln[:, c, :], qfB, qlB, 1.0 / 512, "q")
        tsl = slice(s * T, (s + 1) * T)
        GC, GS, TMP = (self.gq[:, i, :] for i in range(3))
        gqB = self.B("gq")
        gtB = self.B("gqtmp")
        S.dve.op(lambda h: h.tensor_scalar(out=GC, in0=self.C64[:, tsl],
                                           scalar1=self.vecs[0:64, V_QGR + j:V_QGR + j + 1], scalar2=None,
                                           op0=ALU.mult), reads=(self.B("C64"), Bv), writes=(gqB,))
        S.dve.op(lambda h: h.tensor_scalar(out=GS, in0=self.S64[:, tsl],
                                           scalar1=self.vecs[0:64, V_QGRS + j:V_QGRS + j + 1], scalar2=None,
                                           op0=ALU.mult), reads=(self.B("S64"), Bv), writes=(gqB,))
        wn, wnB = self.w_get(self.wblk(w_qb, j * 512, 4, 0, 2048), 4, 2048, hold=True)
        wr, wrB = self.w_get(self.wblk(w_qb, j * 512, 4, 2048, 2048), 4, 2048, hold=True)
        ogB = self.B("sgb")
        TMP2 = self.gq[:, 3, :]
        gt2B = self.B("gqtmp2")

        def q_proj(hd):
            qn, qnB = self.ps[4], self.psB[4]
            qr, qrB = self.ps[5], self.psB[5]
            qs, qsB = self.ps[6], self.psB[6]
            self.mm_group(qn[:, :], [(wn[:, k, hd * 128:(hd + 1) * 128], self.qln[:, k, :]) for k in range(4)],
                          reads=tuple(qlB) + (wnB,), writes=(qnB,))
            self.mm_group(qr[0:64, :], [(wr[:, k, hd * 64:(hd + 1) * 64], self.qln[:, k, :]) for k in range(4)],
                          reads=tuple(qlB) + (wrB,), writes=(qrB,))
            self.mm_group(qs[0:64, :], [(wr[:, k, 1024 + hd * 64:1024 + (hd + 1) * 64], self.qln[:, k, :])
                                        for k in range(4)], reads=tuple(qlB) + (wrB,), writes=(qsB,))
            sqt, sqB_ = self.sqb[hd % 4], self.B(f"sqb{hd % 4}")
            S.act.op(lambda h, sqt=sqt: h.activation(out=sqt[:, 0:T], in_=qn[:, :], func=AF.Square),
                     reads=(qnB,), writes=(sqB_,))
            sqrB = self.B("sqr2")
            S.act.op(lambda h: h.activation(out=self.sqr2[:, :], in_=qr[0:64, :], func=AF.Square),
                     reads=(qrB,), writes=(sqrB,))
            args = (hd, qn, qnB, qr, qrB, qs, qsB, sqt, sqB_, sqrB)
            return [lambda: q_proj_b1(*args), lambda: q_proj_b2(*args)]

        def q_proj_b1(hd, qn, qnB, qr, qrB, qs, qsB, sqt, sqB_, sqrB):
            ss, ssB = self.ps[7], self.psB[7]
            self.mm_group(ss[:, :], [(self.ones[:, :], sqt[:, 0:T]), (self.ones[0:64, :], self.sqr2[:, :])],
                          reads=(sqB_, sqrB, self.B("ones")), writes=(ssB,))
            sdB, rsB = self.B("sd"), self.B("rstd")
            S.act.op(lambda h: h.activation(out=self.sd[:, 0:T], in_=ss[:, :], func=AF.Ln, scale=1.0 / 192,
                                            bias=self.epsc[:, 0:1]), reads=(ssB,), writes=(sdB,))

        def q_proj_b2(hd, qn, qnB, qr, qrB, qs, qsB, sqt, sqB_, sqrB):
            sdB, rsB = self.B("sd"), self.B("rstd")
            S.act.op(lambda h: h.activation(out=self.rstd[:, 0:T], in_=self.sd[:, 0:T], func=AF.Exp, scale=-0.5),
                     reads=(sdB,), writes=(rsB,))
            qnT, qnTB = self.qnT[hd % 2], self.B(f"qnT{hd % 2}")
            qrT, qrTB = self.qrT[hd % 2], self.B(f"qrT{hd % 2}")
            S.dve.op(lambda h, qnT=qnT: h.scalar_tensor_tensor(
                out=qnT[:, :], in0=qn[:, :], scalar=self.vecs[:, V_QGN + j:V_QGN + j + 1], in1=self.rstd[:, 0:T],
                op0=ALU.mult, op1=ALU.mult), reads=(qnB, rsB, Bv), writes=(qnTB,))
            S.dve.op(lambda h: h.tensor_tensor(out=TMP, in0=qr[0:64, :], in1=GC, op=ALU.mult),
                     reads=(qrB, gqB), writes=(gtB,))
            S.dve.op(lambda h: h.tensor_tensor(out=TMP2, in0=qs[0:64, :], in1=GS, op=ALU.mult),
                     reads=(qsB, gqB), writes=(gt2B,))
            S.dve.op(lambda h: h.tensor_tensor(out=TMP, in0=TMP, in1=TMP2, op=ALU.add),
                     reads=(gtB, gt2B), writes=(gtB,))
            S.dve.op(lambda h, qrT=qrT: h.tensor_tensor(out=qrT[:, :], in0=TMP, in1=self.rstd[0:64, 0:T],
                                                        op=ALU.mult), reads=(gtB, rsB), writes=(qrTB,))

        for f in q_proj(0):
            f()
        pending = []
        for hd in range(NH):
            if hd + 1 < NH:
                kv_next = self.kv_load(s, hd + 1)
                pending.extend(q_proj(hd + 1))
            qnT, qnTB = self.qnT[hd % 2], self.B(f"qnT{hd % 2}")
            qrT, qrTB = self.qrT[hd % 2], self.B(f"qrT{hd % 2}")
            KTn, KTr, Vh = self.KTn[kvi], self.KTr[kvi], self.Vh[kvi]
            KBn, KBr, VB = self.B(f"KTn{kvi}"), self.B(f"KTr{kvi}"), self.B(f"Vh{kvi}")
            O, OB = self.ps[2], self.psB[2]
            SU, SUB = self.ps[3], self.psB[3]
            ntile = NOFF[s] + 4

            def score(jt):
                d = jt - NOFF[s]
                c0 = 0 if d <= 0 else 128 * d
                st, stB = self.ps[jt % 2], self.psB[jt % 2]
                self.mm_group(st[:, c0:T], [(KTn[:, jt * 128:(jt + 1) * 128], qnT[:, c0:T]),
                                            (KTr[:, jt * 128:(jt + 1) * 128], qrT[:, c0:T])],
                              reads=(KBn, KBr, qnTB, qrTB), writes=(stB,))
                pt, ptB = self.PT[self.ptc % 3], self.B(f"PT{self.ptc % 3}")
                self.ptc += 1
                if d < 0:
                    bcol = V_ABIAS + NOFF_OFS[s] + jt
                    bias = self.vecs[:, bcol:bcol + 1]
                    S.act.op(lambda h, st=st, pt=pt, bias=bias: h.activation(
                        out=pt[:, :], in_=st[:, :], func=AF.Exp, scale=SM_SCALE, bias=bias),
                        reads=(stB, Bv), writes=(ptB,))
                else:
                    S.act.op(lambda h, st=st, pt=pt, c0=c0: h.activation(
                        out=pt[:, c0:T], in_=st[:, c0:T], func=AF.Exp, scale=SM_SCALE),
                        reads=(stB,), writes=(ptB,))
                    S.dve.op(lambda h, pt=pt, c0=c0: h.memset(pt[64:128, c0:c0 + 64], 0.0), reads=(),
                             writes=(ptB,))
                return pt, ptB, c0

            def pv(jt, pt, ptB, c0):
                first = (jt == 0)
                last = (jt == ntile - 1)
                S.pe.group([
                    lambda h, pt=pt, jt=jt, c0=c0, first=first, last=last, Vh=Vh: h.matmul(
                        O[:, c0:T], Vh[:, jt, :], pt[:, c0:T], start=first, stop=last),
                    lambda h, pt=pt, c0=c0, first=first, last=last: h.matmul(
                        SU[:, c0:T], self.ones[:, :], pt[:, c0:T], start=first, stop=last),
                ], reads=(ptB, VB, self.B("ones")), writes=(OB, SUB))

            prev = score(0)
            for jt in range(1, ntile):
                cur = score(jt)
                pv(jt - 1, *prev)
                prev = cur
                if jt >= 2 and pending:
                    pending.pop(0)()
            assert not pending
            pv(ntile - 1, *prev)
            rsqB, otB = self.B("rsq"), self.B("ot")
            S.dve.op(lambda h: h.tensor_copy(out=self.rsq[:, :], in_=SU[:, :]), reads=(SUB,), writes=(rsqB,))
            S.dve.op(lambda h: h.tensor_copy(out=self.ot[:, :], in_=O[:, :]), reads=(OB,), writes=(otB,))
            def head_end1(hd=hd):
                S.act.op(lambda h: h.activation(out=self.rsq[:, :], in_=self.rsq[:, :], func=AF.Ln),
                         reads=(rsqB,), writes=(rsqB,))

            def head_end2(hd=hd):
                S.act.op(lambda h: h.activation(out=self.rsq[:, :], in_=self.rsq[:, :], func=AF.Exp, scale=-1.0),
                         reads=(rsqB,), writes=(rsqB,))
                S.dve.op(lambda h: h.tensor_tensor(out=self.ot[:, :], in0=self.ot[:, :], in1=self.rsq[:, :],
                                                   op=ALU.mult), reads=(otB, rsqB), writes=(otB,))
                S.dve.op(lambda h: h.tensor_tensor(out=self.og[:, hd, :], in0=self.ot[:, :],
                                                   in1=self.sgB_[:, hd, :], op=ALU.mult),
                         reads=(otB, sgB), writes=(ogB,))
            if hd + 1 < NH:
                pending = [head_end1, head_end2] + pending
            else:
                head_end1()
                head_end2()
            if DBG and hd == 0 and not getattr(self, "dbg_done", False):
                self.dbg_done = True
                S.sp.dma("dbg", self.dbg_bf[:, 0, :], qnT[:, :], reads=(qnTB,))
                S.sp.dma("dbg", self.dbg_bf[0:64, 1, :], qrT[:, :], reads=(qrTB,))
                S.sp.dma("dbg", self.dbg_bf[:, 3, :], KTn[:, 0:T], reads=(KBn,))
                S.sp.dma("dbg", self.dbg_bf[0:64, 4, :], KTr[:, 0:T], reads=(KBr,))
                S.sp.dma("dbg", self.dbg_bf[:, 5, :], Vh[:, 0:4, :].rearrange("p a b -> p (a b)"), reads=(VB,))
                S.sp.dma("dbg", self.dbg_f[:, 0, :], self.ot[:, :], reads=(otB,))
                S.sp.dma("dbg", self.dbg_f[:, 1, :], self.rsq[:, :], reads=(rsqB,))
            if hd + 1 < NH:
                kvi = kv_next
        self.wheld.clear()
        for q in range(4):
            blk, bB = self.w_get(self.wblk(w_out, j * 2048, 16, q * 512, 512), 16, 512)
            for m in range(4):
                oc = q * 4 + m
                ps, psB = self.ps[m % 2], self.psB[m % 2]
                self.mm_group(ps[:, :], [(blk[:, k, m * 128:(m + 1) * 128], self.og[:, k, :]) for k in range(NH)],
                              reads=(ogB, bB), writes=(psB,))
                S.dve.op(lambda h, ps=ps, oc=oc: h.tensor_tensor(out=self.xT[:, oc, M], in0=self.xT[:, oc, M],
                                                                 in1=ps[:, :], op=ALU.add),
                         reads=(psB,), writes=(xB[oc],))

    def store_out(self, out_d, s):
        S = self.S
        xB = [self.B(f"xT{c}") for c in range(DC)]
        qfB = tuple(self.B(f"qf{m}") for m in range(4))
        S.dve.op(lambda h: h.memset(self.rsq[:, 0:1], 0.0), writes=qfB + (self.B("xo"), self.B("rsq")))
        Bid = self.B("ident")
        for ts in range(4):
            xo, xoB = self.xo[0], self.B("xo")
            for cg in range(4):
                pi = 6 + (cg % 2)
                ps, psB = self.ps[pi], self.psB[pi]
                fns = []
                for jj in range(4):
                    c = cg * 4 + jj
                    fns.append(lambda h, ps=ps, c=c, jj=jj, ts=ts: h.transpose(
                        out=ps[:, jj * 128:(jj + 1) * 128], in_=self.xT[:, c, HL + ts * 128:HL + (ts + 1) * 128],
                        identity=self.ident[:, :]))
                S.pe.group(fns, reads=tuple(xB[cg * 4:cg * 4 + 4]) + (Bid,), writes=(psB,))
                eng = S.act if cg % 2 == 0 else S.dve
                dst = xo[:, cg * 512:(cg + 1) * 512]
                if eng is S.act:
                    eng.op(lambda h, ps=ps, dst=dst: h.activation(out=dst, in_=ps[:, :], func=AF.Copy),
                           reads=(psB,), writes=(xoB,))
                else:
                    eng.op(lambda h, ps=ps, dst=dst: h.tensor_copy(out=dst, in_=ps[:, :]), reads=(psB,),
                           writes=(xoB,))
            r0 = s * T + ts * 128
            S.sp.dma("xo", out_d[r0:r0 + 128, :], xo, reads=(xoB,))


_PROGS = {}


def _prog(mode):
    if mode not in _PROGS:
        _PROGS[mode] = Prog(mode).build()
    return _PROGS[mode]


def _cols(v, n):
    return np.ascontiguousarray(np.asarray(v, np.float32).reshape(n, 128).T)


def _build_vecs(inp, r):
    vecs = np.zeros((128, NV), np.float32)
    for l in range(2):
        vecs[:, V_ANORM + l * 16:V_ANORM + (l + 1) * 16] = _cols(inp["a_norm_g"][l], 16)
        vecs[:, V_ASCALE + l * 32:V_ASCALE + (l + 1) * 32] = _cols(inp["a_scale"][l], 32)
        vecs[:, V_BNORM + l * 16:V_BNORM + (l + 1) * 16] = _cols(inp["b_norm_g"][l], 16)
        vecs[:, V_BQLAT + l * 4:V_BQLAT + (l + 1) * 4] = _cols(inp["b_q_latent_g"][l], 4)
        qg = np.asarray(inp["b_q_norm_g"][l], np.float32)
        vecs[:, V_QGN + l] = qg[:128]
        vecs[:64, V_QGR + l] = qg[128:]
        vecs[:64, V_QGRS + l] = np.concatenate([qg[160:192], qg[128:160]])
    vecs[:, V_KVNORM:V_KVNORM + 16] = _cols(inp["kv_norm_g"], 16)
    vecs[:, V_KVLAT:V_KVLAT + 4] = _cols(inp["kv_latent_g"], 4)
    kg = np.asarray(inp["k_norm_g"], np.float32)
    vecs[:, V_KGN] = kg[:128]
    vecs[:64, V_KGR] = kg[128:]
    vecs[:64, V_KGRS] = np.concatenate([kg[160:192], kg[128:160]])
    invf = (10000.0 ** (-np.arange(0, 64, 2, dtype=np.float32) / 64)).astype(np.float32)
    vecs[:64, V_INVF] = np.concatenate([invf, invf])
    vecs[:32, V_SGN] = -1.0
    vecs[32:64, V_SGN] = 1.0
    for s in range(NSLOT):
        for g, w in enumerate(WIN):
            t = np.arange(16, dtype=np.float32)
            if SBS[r][s] == 0:
                c = 1.0 / np.minimum(t + 1, float(w))
            else:
                c = np.full(16, 1.0 / w, np.float32)
            vecs[:, V_CINV + (s * 4 + g) * 16:V_CINV + (s * 4 + g + 1) * 16] = c[None, :]
        for jt in range(NOFF[s]):
            valid = jt < 4 * SBS[r][s]
            vecs[:, V_ABIAS + NOFF_OFS[s] + jt] = 0.0 if valid else NEG
    return vecs


def _prep_common(inp):
    w = {}
    w["a_w_in"] = np.ascontiguousarray(inp["a_w_in"], np.float32).reshape(2 * 2048, 8192)
    w["a_w_g"] = np.ascontiguousarray(inp["a_w_group"], np.float32).reshape(8 * 1024, 1024)
    w["a_w_out"] = np.ascontiguousarray(inp["a_w_out"], np.float32).reshape(2 * 4096, 2048)
    wa = np.asarray(inp["kv_w_a"], np.float32)
    w["kv_w_a"] = np.ascontiguousarray(np.concatenate([wa, wa[:, 544:576], wa[:, 512:544]], axis=1))
    wb = np.asarray(inp["kv_w_b"], np.float32).reshape(512, NH, 2, 128)
    w["kv_w_b"] = np.ascontiguousarray(np.concatenate([wb[:, :, 0, :].reshape(512, 2048),
                                                       wb[:, :, 1, :].reshape(512, 2048)], axis=1))
    w["b_w_in"] = np.ascontiguousarray(inp["b_w_in"], np.float32).reshape(2 * 2048, 2560)
    wq = np.asarray(inp["b_w_q_b"], np.float32).reshape(2, 512, NH, 192)
    nope = wq[:, :, :, :128].reshape(2, 512, 2048)
    rope = wq[:, :, :, 128:].reshape(2, 512, 1024)
    ropes = np.concatenate([wq[:, :, :, 160:192], wq[:, :, :, 128:160]], axis=3).reshape(2, 512, 1024)
    w["b_w_qb"] = np.ascontiguousarray(np.concatenate([nope, rope, ropes], axis=2)).reshape(2 * 512, 4096)
    w["b_w_out"] = np.ascontiguousarray(inp["b_w_out"], np.float32).reshape(2 * 2048, 2048)
    return w


def _core_inputs(inp, c):
    b, r = divmod(c, 2)
    x = np.asarray(inp["x"], np.float32)[b]
    pos = np.asarray(inp["positions"], np.int32)[b]
    xin = np.zeros((NSLOT, TH, D), np.float32)
    posr = np.zeros((NSLOT * T,), np.int32)
    for s, sb in enumerate(SBS[r]):
        t0 = sb * T
        if sb > 0:
            xin[s] = x[t0 - HL:t0 + T]
        else:
            xin[s, HL:] = x[0:T]
        posr[s * T:(s + 1) * T] = pos[t0:t0 + T]
    return xin, np.ascontiguousarray(np.broadcast_to(posr[None, :], (64, NSLOT * T))), _build_vecs(inp, r)


FUSED = True


def kernel(**inp):
    wts = _prep_common(inp)
    ident = np.eye(128, dtype=np.float32)
    cores = list(range(8))
    per = [_core_inputs(inp, c) for c in cores]
    A_keys = ["a_w_in", "a_w_g", "a_w_out", "kv_w_a", "kv_w_b"]
    B_keys = ["b_w_in", "b_w_qb", "b_w_out"]
    if FUSED:
        nc = _prog("F")
        maps = []
        for c in cores:
            m = {"vecs": per[c][2], "pos": per[c][1], "ident": ident, "xin": per[c][0]}
            for k in A_keys + B_keys:
                m[k] = wts[k]
            maps.append(m)
        res = run_bass_kernel_spmd(nc, maps, core_ids=cores)
        outs = [r["out"] for r in res.results]
    else:
        ncA = _prog("A")
        maps = []
        for c in cores:
            m = {"vecs": per[c][2], "pos": per[c][1], "ident": ident, "xin": per[c][0]}
            for k in A_keys:
                m[k] = wts[k]
            maps.append(m)
        resA = run_bass_kernel_spmd(ncA, maps, core_ids=cores).results
        ncB = _prog("B")
        maps = []
        for c in cores:
            p = c - (c % 2)
            m = {"vecs": per[c][2], "pos": per[c][1], "ident": ident, "x2T": resA[c]["x2T"]}
            for s_ in range(NSLOT):
                for hh in range(2):
                    for nm in ("kT", "v"):
                        key = f"{nm}_loc_{s_}_{hh}"
                        m[key] = resA[c][key]
                        m[f"{nm}_all_{s_}_{hh}"] = np.concatenate([resA[p][key], resA[p + 1][key]], axis=0)
            for k in B_keys:
                m[k] = wts[k]
            maps.append(m)
        resB = run_bass_kernel_spmd(ncB, maps, core_ids=cores).results
        outs = [r["out"] for r in resB]
    out = np.zeros((4, 4096, D), np.float32)
    for c in cores:
        b, r = divmod(c, 2)
        for s, sb in enumerate(SBS[r]):
            out[b, sb * T:(sb + 1) * T] = outs[c][s * T:(s + 1) * T]
    return out
```
